# Optimizing a Trainium2 kernel written in Bass

```python
import math
import jax, jax.numpy as jnp
from jax import lax
import numpy as np

D_MODEL = 2048
BATCH = 8
SEQ = 4096
DEPTH = 4

N_BRANCHES = 4
BRANCH_WIDTH = D_MODEL // N_BRANCHES
DSA_HEADS = 4
DSA_HEAD_DIM = BRANCH_WIDTH // DSA_HEADS
IDX_HEADS = 8
IDX_DIM = 64
INDEX_TOPK = 256
GLA_HEADS = 4
GLA_KEY_DIM = BRANCH_WIDTH // GLA_HEADS // 2
GLA_VAL_DIM = BRANCH_WIDTH // GLA_HEADS
GLA_GATE_RANK = 16
GLA_TAU = 16.0
GLA_CHUNK = 64
S5_GROUP = 16
S5_GROUPS = BRANCH_WIDTH // S5_GROUP
S5_STATE = 64
S5_DT_MIN = 1e-3
S5_DT_MAX = 1e-1
MLA_HEADS = 4
MLA_NOPE_DIM = 128
MLA_ROPE_DIM = 64
MLA_V_DIM = 128
MLA_Q_RANK = 384
MLA_KV_RANK = 128
D_FF = ((8 * D_MODEL // 3 + 127) // 128) * 128
ROPE_THETA = 10000.0
Q_BLOCK = 128
NORM_EPS = 1e-6
N_MOD = 9

IN_SIZES = (
    DSA_HEADS * DSA_HEAD_DIM, DSA_HEADS * DSA_HEAD_DIM, DSA_HEADS * DSA_HEAD_DIM,
    IDX_HEADS * IDX_DIM, IDX_DIM, IDX_HEADS,
    GLA_HEADS * GLA_KEY_DIM, GLA_HEADS * GLA_KEY_DIM, GLA_HEADS * GLA_VAL_DIM,
    GLA_GATE_RANK, GLA_HEADS * GLA_VAL_DIM,
    BRANCH_WIDTH,
    MLA_Q_RANK, MLA_KV_RANK, MLA_ROPE_DIM,
)
IN_COLS = int(sum(IN_SIZES))
IN_OFFSETS = tuple(int(o) for o in np.cumsum(IN_SIZES)[:-1])

kernel_name = 'hybrid_gated_dsa_gla_s5_mla_block'


def rms_norm(x, g):
    xf = x.astype(jnp.float32)
    y = xf * lax.rsqrt(jnp.mean(xf * xf, axis=-1, keepdims=True) + NORM_EPS)
    return (y * g.astype(jnp.float32)).astype(x.dtype)


def ada_norm(x, gain, shift, scale):
    return rms_norm(x, gain) * (1.0 + scale) + shift


def rope(x, positions):
    half = x.shape[-1] // 2
    inv_freq = jnp.power(ROPE_THETA, -jnp.arange(half, dtype=jnp.float32) / half)
    ang = positions.astype(jnp.float32)[:, :, None] * inv_freq
    cos = jnp.cos(ang)[:, :, None, :]
    sin = jnp.sin(ang)[:, :, None, :]
    xf = x.astype(jnp.float32)
    x1, x2 = xf[..., :half], xf[..., half:]
    return jnp.concatenate([x1 * cos - x2 * sin, x2 * cos + x1 * sin], axis=-1).astype(x.dtype)


def swiglu(h, w_in, w_out):
    a, b = jnp.split(h @ w_in, 2, axis=-1)
    return (jax.nn.silu(a) * b) @ w_out


def dsa_sparse_attention(q, k, v, iq, ik, iw, k_sel):
    bsz, seq, heads, hd = q.shape
    nb = seq // Q_BLOCK
    key_pos = jnp.arange(seq)

    def to_blocks(a):
        return a.reshape(bsz, nb, Q_BLOCK, *a.shape[2:]).swapaxes(0, 1)

    def block(args):
        qb, iqb, iwb, t = args
        s = jnp.einsum('bqhd,bsd->bqhs', iqb, ik).astype(jnp.float32) * IDX_DIM ** -0.5
        score = jnp.einsum('bqh,bqhs->bqs', iwb.astype(jnp.float32), jax.nn.relu(s))
        score = jnp.where(key_pos[None, None, :] <= t[None, :, None], score, -jnp.inf)
        _, idx = lax.top_k(score, k_sel)
        kg = jax.vmap(lambda kb, ib: kb[ib])(k, idx)
        vg = jax.vmap(lambda vb, ib: vb[ib])(v, idx)
        logits = jnp.einsum('bqhd,bqkhd->bhqk', qb, kg).astype(jnp.float32) * hd ** -0.5
        valid = (idx <= t[None, :, None])[:, None]
        logits = jnp.where(valid, logits, -jnp.inf)
        p = jax.nn.softmax(logits, axis=-1).astype(vg.dtype)
        return jnp.einsum('bhqk,bqkhd->bqhd', p, vg)

    out = lax.map(block, (to_blocks(q), to_blocks(iq), to_blocks(iw), key_pos.reshape(nb, Q_BLOCK)))
    return out.swapaxes(0, 1).reshape(bsz, seq, heads, hd)


def dsa_branch(a_q, a_k, a_v, i_q, i_k, i_w, positions, qk_norm, k_sel):
    bsz, seq, _ = a_q.shape
    shp = (bsz, seq, DSA_HEADS, DSA_HEAD_DIM)
    q = rope(rms_norm(a_q.reshape(shp), qk_norm[0]), positions)
    k = rope(rms_norm(a_k.reshape(shp), qk_norm[1]), positions)
    v = a_v.reshape(shp)
    iq = rope(i_q.reshape(bsz, seq, IDX_HEADS, IDX_DIM), positions)
    ik = rope(i_k[:, :, None, :], positions)[:, :, 0]
    iw = i_w * IDX_HEADS ** -0.5
    out = dsa_sparse_attention(q, k, v, iq, ik, iw, k_sel)
    return out.reshape(bsz, seq, BRANCH_WIDTH)


def gla_chunked(q, k, v, log_a):
    bsz, seq, heads, dk = q.shape
    dv = v.shape[-1]
    n = seq // GLA_CHUNK

    def to_chunks(a):
        return a.reshape(bsz, n, GLA_CHUNK, heads, a.shape[-1]).transpose(1, 0, 3, 2, 4)

    qc, kc, vc = to_chunks(q), to_chunks(k), to_chunks(v)
    bc = jnp.cumsum(to_chunks(log_a), axis=3)
    causal = jnp.tril(jnp.ones((GLA_CHUNK, GLA_CHUNK), dtype=bool))

    def step(state, inp):
        qt, kt, vt, bt = inp
        diff = bt[:, :, :, None, :] - bt[:, :, None, :, :]
        decay = jnp.exp(jnp.where(causal[:, :, None], diff, -jnp.inf))
        attn = jnp.einsum('bhtd,bhsd,bhtsd->bhts', qt, kt, decay)
        o = jnp.einsum('bhts,bhsv->bhtv', attn, vt)
        o = o + jnp.einsum('bhtd,bhdv->bhtv', qt * jnp.exp(bt), state)
        b_last = bt[:, :, -1]
        k_dec = kt * jnp.exp(b_last[:, :, None, :] - bt)
        state = jnp.exp(b_last)[..., None] * state + jnp.einsum('bhsd,bhsv->bhdv', k_dec, vt)
        return state, o

    state0 = jnp.zeros((bsz, heads, dk, dv), jnp.float32)
    _, ys = lax.scan(step, state0, (qc, kc, vc, bc))
    return ys.transpose(1, 0, 3, 2, 4).reshape(bsz, seq, heads, dv)


def gla_branch(g_q, g_k, g_v, g_lr, g_r, w_gate2, b_gate2, out_norm):
    bsz, seq, _ = g_q.shape
    f32 = jnp.float32
    q = g_q.reshape(bsz, seq, GLA_HEADS, GLA_KEY_DIM).astype(f32) * GLA_KEY_DIM ** -0.5
    k = g_k.reshape(bsz, seq, GLA_HEADS, GLA_KEY_DIM).astype(f32)
    v = g_v.reshape(bsz, seq, GLA_HEADS, GLA_VAL_DIM).astype(f32)
    log_a = jax.nn.log_sigmoid((g_lr @ w_gate2 + b_gate2).astype(f32)) / GLA_TAU
    log_a = log_a.reshape(bsz, seq, GLA_HEADS, GLA_KEY_DIM)
    o = rms_norm(gla_chunked(q, k, v, log_a), out_norm)
    o = o.reshape(bsz, seq, BRANCH_WIDTH) * jax.nn.silu(g_r.astype(f32))
    return o.astype(g_q.dtype)


def _complex_linear_combine(e1, e2):
    a1r, a1i, b1r, b1i = e1
    a2r, a2i, b2r, b2i = e2
    ar = a1r * a2r - a1i * a2i
    ai = a1r * a2i + a1i * a2r
    br = a2r * b1r - a2i * b1i + b2r
    bi = a2r * b1i + a2i * b1r + b2i
    return ar, ai, br, bi


def s5_branch(u, a_re, a_im, log_dt, b_re, b_im, c_re, c_im, d_skip, w_glu, b_glu):
    bsz, seq, _ = u.shape
    f32 = jnp.float32
    uf = u.astype(f32).reshape(bsz, seq, S5_GROUPS, S5_GROUP)
    ar, ai = a_re.astype(f32), a_im.astype(f32)
    dt = jnp.exp(log_dt.astype(f32))[:, None]
    mag = jnp.exp(dt * ar)
    abar_re, abar_im = mag * jnp.cos(dt * ai), mag * jnp.sin(dt * ai)
    den = ar * ar + ai * ai
    nr, ni = abar_re - 1.0, abar_im
    z_re, z_im = (nr * ar + ni * ai) / den, (ni * ar - nr * ai) / den
    bu_re = jnp.einsum('bsgi,gpi->bsgp', uf, b_re.astype(f32))
    bu_im = jnp.einsum('bsgi,gpi->bsgp', uf, b_im.astype(f32))
    x_re = z_re * bu_re - z_im * bu_im
    x_im = z_re * bu_im + z_im * bu_re
    elems = (jnp.broadcast_to(abar_re, x_re.shape), jnp.broadcast_to(abar_im, x_re.shape), x_re, x_im)
    _, _, h_re, h_im = lax.associative_scan(_complex_linear_combine, elems, axis=1)
    y = jnp.einsum('bsgp,gip->bsgi', h_re, c_re.astype(f32)) - jnp.einsum('bsgp,gip->bsgi', h_im, c_im.astype(f32))
    y = y.reshape(bsz, seq, BRANCH_WIDTH) + d_skip.astype(f32) * uf.reshape(bsz, seq, BRANCH_WIDTH)
    y = jax.nn.gelu(y)
    y = y * jax.nn.sigmoid(y @ w_glu.astype(f32) + b_glu.astype(f32))
    return y.astype(u.dtype)


def dense_causal_attention(q, k, v):
    bsz, seq, heads, dqk = q.shape
    nb = seq // Q_BLOCK
    key_pos = jnp.arange(seq)

    def block(args):
        qb, t = args
        logits = jnp.einsum('bqhd,bshd->bhqs', qb, k).astype(jnp.float32) * dqk ** -0.5
        logits = jnp.where(key_pos[None, :] <= t[:, None], logits, -jnp.inf)
        p = jax.nn.softmax(logits, axis=-1).astype(v.dtype)
        return jnp.einsum('bhqs,bshv->bqhv', p, v)

    qb = q.reshape(bsz, nb, Q_BLOCK, heads, dqk).swapaxes(0, 1)
    out = lax.map(block, (qb, key_pos.reshape(nb, Q_BLOCK)))
    return out.swapaxes(0, 1).reshape(bsz, seq, heads, v.shape[-1])


def mla_branch(m_q, m_kv, m_kr, positions, q_lat_norm, kv_lat_norm, w_uq, w_ukv, qk_norm):
    bsz, seq, _ = m_q.shape
    q = (rms_norm(m_q, q_lat_norm) @ w_uq).reshape(bsz, seq, MLA_HEADS, MLA_NOPE_DIM + MLA_ROPE_DIM)
    kv = (rms_norm(m_kv, kv_lat_norm) @ w_ukv).reshape(bsz, seq, MLA_HEADS, MLA_NOPE_DIM + MLA_V_DIM)
    k_nope, v = kv[..., :MLA_NOPE_DIM], kv[..., MLA_NOPE_DIM:]
    q_nope = rms_norm(q[..., :MLA_NOPE_DIM], qk_norm[0, :MLA_NOPE_DIM])
    q_rope = rope(rms_norm(q[..., MLA_NOPE_DIM:], qk_norm[0, MLA_NOPE_DIM:]), positions)
    k_nope = rms_norm(k_nope, qk_norm[1, :MLA_NOPE_DIM])
    k_rope = rope(rms_norm(m_kr, qk_norm[1, MLA_NOPE_DIM:])[:, :, None, :], positions)
    q = jnp.concatenate([q_nope, q_rope], axis=-1)
    k = jnp.concatenate([k_nope, jnp.broadcast_to(k_rope, (bsz, seq, MLA_HEADS, MLA_ROPE_DIM))], axis=-1)
    out = dense_causal_attention(q, k, v)
    return out.reshape(bsz, seq, MLA_HEADS * MLA_V_DIM)


def setup_inputs(seed: int = 0) -> dict:
    key = jax.random.key(seed)
    ks = iter(jax.random.split(key, 40))
    f32 = jnp.float32
    L, D = DEPTH, D_MODEL
    G, P = S5_GROUPS, S5_STATE

    def nrm(shape, scale):
        return jax.random.normal(next(ks), shape, f32) * scale

    inp = {}
    inp['x'] = nrm((BATCH, SEQ, D), 1.0)
    inp['c'] = nrm((BATCH, D), 1.0)
    inp['positions'] = jnp.tile(jnp.arange(SEQ, dtype=jnp.int32)[None, :], (BATCH, 1))
    inp['w_ada'] = nrm((L, D, N_MOD * D), D ** -0.5)
    inp['b_ada'] = nrm((L, N_MOD * D), 0.02)
    inp['norm_g'] = 1.0 + nrm((L, 3, D), 0.02)
    inp['w_ff1_in'] = nrm((L, D, 2 * D_FF), D ** -0.5)
    inp['w_ff1_out'] = nrm((L, D_FF, D), D_FF ** -0.5)
    inp['w_ff2_in'] = nrm((L, D, 2 * D_FF), D ** -0.5)
    inp['w_ff2_out'] = nrm((L, D_FF, D), D_FF ** -0.5)
    inp['w_in'] = nrm((L, D, IN_COLS), D ** -0.5)
    inp['dsa_qk_norm'] = 1.0 + nrm((L, 2, DSA_HEAD_DIM), 0.02)
    inp['gla_w_gate'] = nrm((L, GLA_GATE_RANK, GLA_HEADS * GLA_KEY_DIM), GLA_GATE_RANK ** -0.5)
    inp['gla_b_gate'] = nrm((L, GLA_HEADS * GLA_KEY_DIM), 0.1)
    inp['gla_out_norm'] = 1.0 + nrm((L, GLA_VAL_DIM), 0.02)
    n_idx = jnp.arange(P, dtype=f32)
    inp['s5_a_re'] = -0.5 + nrm((L, G, P), 0.01)
    inp['s5_a_im'] = jnp.pi * n_idx + nrm((L, G, P), 0.01)
    inp['s5_log_dt'] = jax.random.uniform(next(ks), (L, G), f32, math.log(S5_DT_MIN), math.log(S5_DT_MAX))
    inp['s5_b_re'] = nrm((L, G, P, S5_GROUP), (2 * S5_GROUP) ** -0.5)
    inp['s5_b_im'] = nrm((L, G, P, S5_GROUP), (2 * S5_GROUP) ** -0.5)
    inp['s5_c_re'] = nrm((L, G, S5_GROUP, P), P ** -0.5)
    inp['s5_c_im'] = nrm((L, G, S5_GROUP, P), P ** -0.5)
    inp['s5_d'] = nrm((L, BRANCH_WIDTH), 0.5)
    inp['s5_w_glu'] = nrm((L, BRANCH_WIDTH, BRANCH_WIDTH), BRANCH_WIDTH ** -0.5)
    inp['s5_b_glu'] = nrm((L, BRANCH_WIDTH), 0.02)
    inp['mla_q_norm'] = 1.0 + nrm((L, MLA_Q_RANK), 0.02)
    inp['mla_kv_norm'] = 1.0 + nrm((L, MLA_KV_RANK), 0.02)
    inp['mla_w_uq'] = nrm((L, MLA_Q_RANK, MLA_HEADS * (MLA_NOPE_DIM + MLA_ROPE_DIM)), MLA_Q_RANK ** -0.5)
    inp['mla_w_ukv'] = nrm((L, MLA_KV_RANK, MLA_HEADS * (MLA_NOPE_DIM + MLA_V_DIM)), MLA_KV_RANK ** -0.5)
    inp['mla_qk_norm'] = 1.0 + nrm((L, 2, MLA_NOPE_DIM + MLA_ROPE_DIM), 0.02)
    inp['w_branch'] = nrm((L, N_BRANCHES, BRANCH_WIDTH, D), BRANCH_WIDTH ** -0.5)
    inp['w_gate'] = nrm((L, N_BRANCHES, D, D), D ** -0.5)
    inp['w_out'] = nrm((L, D, D), D ** -0.5)
    return inp


def reference(x, c, positions, w_ada, b_ada, norm_g, w_ff1_in, w_ff1_out, w_ff2_in, w_ff2_out,
              w_in, dsa_qk_norm, gla_w_gate, gla_b_gate, gla_out_norm,
              s5_a_re, s5_a_im, s5_log_dt, s5_b_re, s5_b_im, s5_c_re, s5_c_im, s5_d, s5_w_glu, s5_b_glu,
              mla_q_norm, mla_kv_norm, mla_w_uq, mla_w_ukv, mla_qk_norm,
              w_branch, w_gate, w_out):
    bsz, seq, _ = x.shape
    k_sel = min(INDEX_TOPK, seq // 4)
    cond = jax.nn.silu(c)
    for l in range(DEPTH):
        mod = (cond @ w_ada[l] + b_ada[l]).reshape(bsz, N_MOD, 1, D_MODEL).astype(x.dtype)
        h = ada_norm(x, norm_g[l, 0], mod[:, 0], mod[:, 1])
        x = x + 0.5 * mod[:, 2] * swiglu(h, w_ff1_in[l], w_ff1_out[l])
        h = ada_norm(x, norm_g[l, 1], mod[:, 3], mod[:, 4])
        (a_q, a_k, a_v, i_q, i_k, i_w, g_q, g_k, g_v, g_lr, g_r, s_u, m_q, m_kv, m_kr) = jnp.split(
            h @ w_in[l], IN_OFFSETS, axis=-1)
        br_a = dsa_branch(a_q, a_k, a_v, i_q, i_k, i_w, positions, dsa_qk_norm[l], k_sel)
        br_b = gla_branch(g_q, g_k, g_v, g_lr, g_r, gla_w_gate[l], gla_b_gate[l], gla_out_norm[l])
        br_c = s5_branch(s_u, s5_a_re[l], s5_a_im[l], s5_log_dt[l], s5_b_re[l], s5_b_im[l],
                         s5_c_re[l], s5_c_im[l], s5_d[l], s5_w_glu[l], s5_b_glu[l])
        br_d = mla_branch(m_q, m_kv, m_kr, positions, mla_q_norm[l], mla_kv_norm[l],
                          mla_w_uq[l], mla_w_ukv[l], mla_qk_norm[l])
        merged = jax.nn.sigmoid(h @ w_gate[l, 0]) * (br_a @ w_branch[l, 0])
        merged = merged + jax.nn.sigmoid(h @ w_gate[l, 1]) * (br_b @ w_branch[l, 1])
        merged = merged + jax.nn.sigmoid(h @ w_gate[l, 2]) * (br_c @ w_branch[l, 2])
        merged = merged + jax.nn.sigmoid(h @ w_gate[l, 3]) * (br_d @ w_branch[l, 3])
        x = x + mod[:, 5] * (merged @ w_out[l])
        h = ada_norm(x, norm_g[l, 2], mod[:, 6], mod[:, 7])
        x = x + 0.5 * mod[:, 8] * swiglu(h, w_ff2_in[l], w_ff2_out[l])
    return x
```

```python
import contextlib
import math
import numpy as np
import concourse.bass as bass
import concourse.mybir as mybir
from concourse.bass_utils import run_bass_kernel_spmd

F32 = mybir.dt.float32
BF16 = mybir.dt.bfloat16
I32 = mybir.dt.int32
AF = mybir.ActivationFunctionType
ALU = mybir.AluOpType
AX = mybir.AxisListType

D = 2048
S = 4096
L = 4
DFF = 5504
NCH = 16
NJ = 43
TT = 512
NTT = S // TT
EPS = 1e-6
INC = 4760
O_AQ, O_AK, O_AV, O_IQ, O_IK, O_IW = 0, 512, 1024, 1536, 2048, 2112
O_GQ, O_GK, O_GV, O_GLR, O_GR = 2120, 2376, 2632, 3144, 3160
O_SU = 3672
O_MQ, O_MKV, O_MKR = 4184, 4568, 4696
NXB = 53
NP = 272


class T:
    __slots__ = ("ap", "name", "w", "r")

    def __init__(self, ap, name=""):
        self.ap = ap
        self.name = name
        self.w = None
        self.r = {}


class FW:
    def __init__(self, nc):
        self.nc = nc
        self.engs = {}
        self.sems = {}
        self.tot = {}
        self.seen = {}
        self.isdma = {}
        self.ninst = 0
        self.nwait = 0
        for name, e in (("pe", nc.tensor), ("dve", nc.vector), ("act", nc.scalar),
                        ("pool", nc.gpsimd), ("sp", nc.sync)):
            self.engs[name] = e
            self.seen[name] = {}
            self._mksem(name, False)

    def _mksem(self, key, isdma):
        self.sems[key] = self.nc.alloc_semaphore("s_" + key)
        self.tot[key] = 0
        self.isdma[key] = isdma

    def _wait(self, en, key, val):
        if self.isdma[key]:
            val = self.tot[key]
        if self.seen[en].get(key, 0) >= val:
            return
        self.engs[en].wait_ge(self.sems[key], val)
        self.seen[en][key] = val
        self.nwait += 1

    def _deps(self, en, reads, writes, own):
        for t in reads:
            if t.w is not None:
                self._wait(en, *t.w)
        for t in writes:
            if t.w is not None and t.w[0] != own:
                self._wait(en, *t.w)
            for k, v in t.r.items():
                if k != own:
                    self._wait(en, k, v)

    def op(self, en, reads, writes, make):
        self._deps(en, reads, writes, en)
        ins = make(self.engs[en])
        self.tot[en] += 1
        ins.then_inc(self.sems[en], 1)
        v = self.tot[en]
        for t in reads:
            t.r[en] = v
        for t in writes:
            t.w = (en, v)
            t.r = {}
        self.ninst += 1

    def dma(self, en, key, out_t, in_t, out_ap=None, in_ap=None, **kw):
        if key not in self.sems:
            self._mksem(key, True)
        self._deps(en, [in_t], [out_t], None)
        ins = self.engs[en].dma_start(out=out_ap if out_ap is not None else out_t.ap,
                                      in_=in_ap if in_ap is not None else in_t.ap, **kw)
        self.tot[key] += 16
        ins.then_inc(self.sems[key], 16)
        v = self.tot[key]
        in_t.r[key] = v
        out_t.w = (key, v)
        out_t.r = {}
        self.ninst += 1
        return (key, v)

    def barrier(self):
        for en in self.engs:
            for key in self.sems:
                if key != en and self.tot[key] > 0:
                    self._wait(en, key, self.tot[key])


class Ctx:
    pass


def build(n_layers=L, stop=None, dump=None, stopsub=None):
    nc = bass.Bass("TRN2", target_bir_lowering=False)
    fw = FW(nc)
    g = Ctx()
    g.nc, g.fw = nc, fw
    g.stopsub = stopsub

    def din(name, shape, dt=F32):
        return nc.dram_tensor(name, list(shape), dt, kind="ExternalInput").ap()

    def dscr(name, shape, dt):
        return nc.dram_tensor(name, list(shape), dt, kind="Internal").ap()

    I = {}
    I["xT"] = din("xT", [D, S])
    I["condc"] = din("condc", [128, 16])
    I["pos"] = din("pos", [S], I32)
    I["kconst"] = din("kconst", [128, 4])
    I["pvec"] = din("pvec", [L, 128, NP])
    I["w_ada"] = din("w_ada", [L, D, 9 * D])
    I["w_ff1_in"] = din("w_ff1_in", [L, D, 2 * DFF])
    I["w_ff1_out"] = din("w_ff1_out", [L, DFF, D])
    I["w_ff2_in"] = din("w_ff2_in", [L, D, 2 * DFF])
    I["w_ff2_out"] = din("w_ff2_out", [L, DFF, D])
    I["w_inx"] = din("w_inx", [L, D, NXB * 128])
    I["w_uqx"] = din("w_uqx", [L, 384, 1024])
    I["w_ukvx"] = din("w_ukvx", [L, 128, 1024])
    I["gla_w_gate"] = din("gla_w_gate", [L, 16, 256])
    I["s5_w_glu"] = din("s5_w_glu", [L, 512, 512])
    I["BT"] = din("BT", [L, 2, 128, 16, 128])
    I["CT"] = din("CT", [L, 2, 128, 16, 128])
    I["w_branch"] = din("w_branch", [L, 4, 512, D])
    I["w_gate"] = din("w_gate", [L, 4, D, D])
    I["w_out"] = din("w_out", [L, D, D])
    g.I = I
    g.IT = {k: T(v, k) for k, v in I.items()}
    y = nc.dram_tensor("y", [D, S], F32, kind="ExternalOutput").ap()
    g.y = y
    g.yT = [T(y[:, t * TT:(t + 1) * TT], f"y{t}") for t in range(NTT)]
    g.dump = dump
    if dump is not None:
        g.dbg = nc.dram_tensor("dbg", list(dump[1]), dump[2], kind="ExternalOutput").ap()
        g.dbgT = T(g.dbg, "dbg")

    with contextlib.ExitStack() as st:
        uid = [0]

        def sb(name, shape, dt, stack=st):
            uid[0] += 1
            return T(stack.enter_context(nc.sbuf_tensor(f"{name}_{uid[0]}", list(shape), dt))[:], name)

        g.sb = sb
        g.ps = [T(st.enter_context(nc.psum_tensor(f"ps{i}", [128, 512], F32))[:], f"ps{i}") for i in range(8)]
        setup_consts(g, st)
        setup_masks(g)
        g.cv_f = [sb(f"cvf{i}", [128, 8, 256], F32) for i in range(2)]
        g.cv_b = [sb(f"cvb{i}", [128, 8, 256], BF16) for i in range(2)]
        g.cvn = 0
        g.W = {}
        for l in range(n_layers):
            declare_weights(g, l)
        declare_scratch(g)
        compute_mod(g, n_layers)
        rope_tables(g)
        convert_ffn(g, 0, 1)
        for l in range(n_layers):
            convert_mixer(g, l)
            ffn_phase(g, l, 1, first=(l == 0))
            if stop == ("ffn1", l):
                break
            convert_ffn(g, l, 2)
            mixer_phase(g, l)
            if stop == ("mix", l):
                break
            if l + 1 < n_layers:
                convert_ffn(g, l + 1, 1)
            ffn_phase(g, l, 2, first=False)
        if dump is not None:
            fw.barrier()
            fw.dma("sp", "dump", g.dbgT, g.DT[dump[0]], in_ap=dump[3](g.D[dump[0]]))
        fw.barrier()
    return nc


def setup_consts(g, st):
    nc, fw, sb = g.nc, g.fw, g.sb
    g.ones_bf = sb("ones_bf", [128, 128], BF16)
    fw.op("dve", [], [g.ones_bf], lambda e: e.memset(g.ones_bf.ap, 1.0))
    g.modT = [sb(f"modT{l}", [128, 144], F32) for l in range(L)]
    g.ngT = [sb(f"ngT{l}", [128, 48], F32) for l in range(L)]
    g.AB = [sb(f"AB{l}", [128, 9 * 16], F32) for l in range(L)]
    g.eps_t = sb("eps_t", [128, 1], F32)
    fw.op("dve", [], [g.eps_t], lambda e: e.memset(g.eps_t.ap, EPS))


def declare_weights(g, l):
    nc = g.nc

    def scr(name, nblk, nk, cw=128):
        ap = nc.dram_tensor(f"{name}_{l}", [nblk, 128, nk, cw], BF16, kind="Internal").ap()
        g.W[(name, l)] = (ap, [T(ap[b], f"{name}{l}_{b}") for b in range(nblk)])

    scr("ff1a", NJ, NCH)
    scr("ff1b", NJ, NCH)
    scr("ff1o", NCH, NJ)
    scr("ff2a", NJ, NCH)
    scr("ff2b", NJ, NCH)
    scr("ff2o", NCH, NJ)
    scr("winx", 44, NCH)
    scr("winv", 2, NCH, 512)
    scr("winw", 1, NCH)
    scr("uq", 8, 3)
    scr("ukvk", 4, 1)
    scr("ukvv", 1, 1, 512)
    scr("glu", 4, 4)
    for i in range(4):
        scr(f"wg{i}", NCH, NCH)
        scr(f"wb{i}", NCH, 4)
    scr("wo", NCH, NCH)


def convert(g, name, l, src, nk, ncols, col0=0, cw=128):
    fw = g.fw
    dap, dts = g.W[(name, l)]
    srcT = g.IT[src[0]]
    sap = src[1]
    for c0 in range(0, ncols, 256):
        w = min(256, ncols - c0)
        for k0 in range(0, nk, 8):
            kk = min(8, nk - k0)
            i = g.cvn % 2
            g.cvn += 1
            f, b = g.cv_f[i], g.cv_b[i]
            sview = sap[k0 * 128:(k0 + kk) * 128, col0 + c0:col0 + c0 + w].rearrange("(k p) c -> p k c", p=128)
            fw.dma("pool", f"cvl{i}", f, srcT, out_ap=f.ap[:, 0:kk, 0:w], in_ap=sview)
            fw.op("pool", [f], [b], lambda e, f=f, b=b, kk=kk, w=w: e.tensor_copy(out=b.ap[:, 0:kk, 0:w], in_=f.ap[:, 0:kk, 0:w]))
            for s0 in range(0, w, 128):
                col = c0 + s0
                blk, off = col // cw, col % cw
                fw.dma("pool", f"cvs{i}", dts[blk], b, out_ap=dap[blk, :, k0:k0 + kk, off:off + 128],
                       in_ap=b.ap[:, 0:kk, s0:s0 + 128])


def convert_mixer(g, l):
    I = g.I
    convert(g, "winx", l, ("w_inx", I["w_inx"][l]), NCH, 44 * 128, 0)
    convert(g, "winv", l, ("w_inx", I["w_inx"][l]), NCH, 1024, 44 * 128, cw=512)
    convert(g, "winw", l, ("w_inx", I["w_inx"][l]), NCH, 128, 52 * 128)
    convert(g, "uq", l, ("w_uqx", I["w_uqx"][l]), 3, 1024, 0)
    convert(g, "ukvk", l, ("w_ukvx", I["w_ukvx"][l]), 1, 512, 0)
    convert(g, "ukvv", l, ("w_ukvx", I["w_ukvx"][l]), 1, 512, 512, cw=512)
    convert(g, "glu", l, ("s5_w_glu", I["s5_w_glu"][l]), 4, 512, 0)
    for i in range(4):
        convert(g, f"wg{i}", l, ("w_gate", I["w_gate"][l, i]), NCH, D, 0)
        convert(g, f"wb{i}", l, ("w_branch", I["w_branch"][l, i]), 4, D, 0)
    convert(g, "wo", l, ("w_out", I["w_out"][l]), NCH, D, 0)


def convert_ffn(g, l, which):
    wi = g.I[f"w_ff{which}_in"][l]
    wo = g.I[f"w_ff{which}_out"][l]
    convert(g, f"ff{which}a", l, (f"w_ff{which}_in", wi), NCH, DFF, 0)
    convert(g, f"ff{which}b", l, (f"w_ff{which}_in", wi), NCH, DFF, DFF)
    convert(g, f"ff{which}o", l, (f"w_ff{which}_out", wo), NJ, D, 0)


def compute_mod(g, n_layers):
    nc, fw = g.nc, g.fw
    with contextlib.ExitStack() as st:
        sb = lambda n, s, d: g.sb(n, s, d, st)
        cT = sb("cT", [128, 16], F32)
        cond = sb("cond", [128, 16], F32)
        fw.dma("sp", "misc", cT, g.IT["condc"])
        fw.op("act", [cT], [cond], lambda e: e.activation(out=cond.ap, in_=cT.ap, func=AF.Silu))
        slabs = [sb(f"adas{i}", [128, 16, 128], F32) for i in range(4)]
        bT = sb("bT", [128, 144], F32)
        n = 0
        for l in range(n_layers):
            fw.dma("sp", "misc", bT, g.IT["pvec"], in_ap=g.I["pvec"][l][:, 0:144])
            fw.dma("sp", "misc2", g.ngT[l], g.IT["pvec"], in_ap=g.I["pvec"][l][:, 144:192])
            acc = g.ps[0]
            for ch in range(144):
                sl = slabs[n % 4]
                fw.dma("sp", f"ada{n % 4}", sl, g.IT["w_ada"],
                       in_ap=g.I["w_ada"][l][:, ch * 128:(ch + 1) * 128].rearrange("(k p) c -> p k c", p=128))
                n += 1
                for k in range(16):
                    fw.op("pe", [sl, cond], [acc], lambda e, sl=sl, k=k, ch=ch: e.matmul(
                        acc.ap[:, ch:ch + 1], sl.ap[:, k, :], cond.ap[:, k:k + 1], start=(k == 0), stop=(k == 15)))
            m = g.modT[l]
            fw.op("dve", [acc, bT], [m], lambda e, m=m: e.tensor_tensor(out=m.ap, in0=acc.ap[:, 0:144], in1=bT.ap, op=ALU.add))
            AB = g.AB[l]
            ng = g.ngT[l]
            for i in range(3):
                sh = m.ap[:, (3 * i) * 16:(3 * i + 1) * 16]
                sc = m.ap[:, (3 * i + 1) * 16:(3 * i + 2) * 16]
                gt = m.ap[:, (3 * i + 2) * 16:(3 * i + 3) * 16]
                A = AB.ap[:, (3 * i) * 16:(3 * i + 1) * 16]
                Bv = AB.ap[:, (3 * i + 1) * 16:(3 * i + 2) * 16]
                G = AB.ap[:, (3 * i + 2) * 16:(3 * i + 3) * 16]
                gi = ng.ap[:, i * 16:(i + 1) * 16]
                fw.op("dve", [m, ng], [AB], lambda e, A=A, sc=sc, gi=gi: e.scalar_tensor_tensor(
                    out=A, in0=sc, scalar=1.0, in1=gi, op0=ALU.add, op1=ALU.mult))
                fw.op("dve", [m], [AB], lambda e, Bv=Bv, sh=sh: e.tensor_copy(out=Bv, in_=sh))
                fw.op("dve", [m], [AB], lambda e, G=G, gt=gt, i=i: e.tensor_scalar(
                    out=G, in0=gt, scalar1=(1.0 if i == 1 else 0.5), scalar2=None, op0=ALU.mult))
        fw.barrier()


def ada_norm_tile(g, l, i, xt, h, sq, rstd, ps_ss):
    fw = g.fw
    AB = g.AB[l]
    for c in range(NCH):
        fw.op("act", [xt], [sq], lambda e, c=c: e.activation(out=sq.ap[:, c, :], in_=xt.ap[:, c, :], func=AF.Square))
    for c in range(NCH):
        fw.op("pe", [sq, g.ones_bf], [ps_ss], lambda e, c=c: e.matmul(
            ps_ss.ap, g.ones_bf.ap, sq.ap[:, c, :], start=(c == 0), stop=(c == NCH - 1)))
    fw.op("act", [ps_ss, g.eps_t], [rstd], lambda e: e.activation(
        out=rstd.ap, in_=ps_ss.ap, func=AF.Sqrt, bias=g.eps_t.ap[:, 0:1], scale=1.0 / D))
    fw.op("dve", [rstd], [rstd], lambda e: e.reciprocal(out=rstd.ap, in_=rstd.ap))
    for c in range(NCH):
        A = AB.ap[:, (3 * i) * 16 + c:(3 * i) * 16 + c + 1]
        Bv = AB.ap[:, (3 * i + 1) * 16 + c:(3 * i + 1) * 16 + c + 1]
        fw.op("dve", [xt, rstd, AB], [sq], lambda e, c=c, A=A: e.scalar_tensor_tensor(
            out=sq.ap[:, c, :], in0=xt.ap[:, c, :], scalar=A, in1=rstd.ap, op0=ALU.mult, op1=ALU.mult))
        fw.op("act", [sq, AB], [h], lambda e, c=c, Bv=Bv: e.activation(
            out=h.ap[:, c, :], in_=sq.ap[:, c, :], func=AF.Identity, bias=Bv, scale=1.0))


def ffn_phase(g, l, which, first):
    nc, fw = g.nc, g.fw
    i = 0 if which == 1 else 2
    _, wa = g.W[(f"ff{which}a", l)]
    _, wb = g.W[(f"ff{which}b", l)]
    _, wo = g.W[(f"ff{which}o", l)]
    AB = g.AB[l]
    with contextlib.ExitStack() as st:
        sb = lambda n, s, d: g.sb(n, s, d, st)
        xt = sb("f_xt", [128, NCH, TT], F32)
        sq = sb("f_sq", [128, NCH, TT], BF16)
        h = sb("f_h", [128, NCH, TT], BF16)
        u = sb("f_u", [128, NJ, TT], BF16)
        rstd = sb("f_rstd", [128, TT], F32)
        sil = [sb(f"f_sil{k}", [128, TT], F32) for k in range(2)]
        wab = [sb(f"f_wab{k}", [128, 2, NCH, 128], BF16) for k in range(3)]
        wos = [sb(f"f_wo{k}", [128, NJ, 128], BF16) for k in range(2)]
        nwa = 0
        nwo = 0
        for tt in range(NTT):
            src_t = g.IT["xT"] if first else g.yT[tt]
            src_ap = (g.I["xT"] if first else g.y)[:, tt * TT:(tt + 1) * TT].rearrange("(c p) t -> p c t", p=128)
            fw.dma("sp", "f_xl", xt, src_t, in_ap=src_ap)
            ada_norm_tile(g, l, i, xt, h, sq, rstd, g.ps[7])
            for j in range(NJ):
                wt = wab[nwa % 3]
                nwa += 1
                fw.dma("sp", f"f_wa{nwa % 3}", wt, wa[j], out_ap=wt.ap[:, 0])
                fw.dma("sp", f"f_wb{nwa % 3}", wt, wb[j], out_ap=wt.ap[:, 1])
                pa, pb = g.ps[(2 * j) % 6], g.ps[(2 * j + 1) % 6]
                for k in range(NCH):
                    fw.op("pe", [wt, h], [pa], lambda e, wt=wt, k=k, pa=pa: e.matmul(
                        pa.ap, wt.ap[:, 0, k, :], h.ap[:, k, :], start=(k == 0), stop=(k == NCH - 1)))
                for k in range(NCH):
                    fw.op("pe", [wt, h], [pb], lambda e, wt=wt, k=k, pb=pb: e.matmul(
                        pb.ap, wt.ap[:, 1, k, :], h.ap[:, k, :], start=(k == 0), stop=(k == NCH - 1)))
                sl = sil[j % 2]
                fw.op("act", [pa], [sl], lambda e, sl=sl, pa=pa: e.activation(out=sl.ap, in_=pa.ap, func=AF.Silu))
                fw.op("dve", [sl, pb], [u], lambda e, sl=sl, pb=pb, j=j: e.tensor_tensor(
                    out=u.ap[:, j, :], in0=sl.ap, in1=pb.ap, op=ALU.mult))
            for m in range(NCH):
                wt = wos[nwo % 2]
                nwo += 1
                fw.dma("sp", f"f_wo{nwo % 2}", wt, wo[m])
                po = g.ps[m % 6]
                for j in range(NJ):
                    fw.op("pe", [wt, u], [po], lambda e, wt=wt, j=j, po=po: e.matmul(
                        po.ap, wt.ap[:, j, :], u.ap[:, j, :], start=(j == 0), stop=(j == NJ - 1)))
                G = AB.ap[:, (3 * i + 2) * 16 + m:(3 * i + 2) * 16 + m + 1]
                fw.op("dve", [po, xt, AB], [xt], lambda e, po=po, m=m, G=G: e.scalar_tensor_tensor(
                    out=xt.ap[:, m, :], in0=po.ap, scalar=G, in1=xt.ap[:, m, :], op0=ALU.mult, op1=ALU.add))
            fw.dma("sp", "f_xs", g.yT[tt], xt,
                   out_ap=g.y[:, tt * TT:(tt + 1) * TT].rearrange("(c p) t -> p c t", p=128))
        fw.barrier()


def declare_scratch(g):
    nc = g.nc

    def scr(name, shape, dt):
        ap = nc.dram_tensor("sc_" + name, list(shape), dt, kind="Internal").ap()
        g.D[name] = ap
        g.DT[name] = T(ap, name)

    g.D, g.DT = {}, {}
    scr("tab", [4, 128, S], F32)
    scr("hT", [D, S], BF16)
    scr("qA", [4, 128, S], BF16)
    scr("kA", [4, 128, S], BF16)
    scr("vA", [S, 512], BF16)
    scr("iq", [4, 128, S], BF16)
    scr("ik", [128, S], BF16)
    scr("iw", [S, 8], F32)
    scr("gq", [4, 64, S], F32)
    scr("gk", [4, 64, S], F32)
    scr("gv", [S, 512], BF16)
    scr("glr", [16, S], BF16)
    scr("gr", [4, 128, S], BF16)
    scr("su", [4, 128, S], F32)
    scr("qn", [4, 128, S], BF16)
    scr("qr", [4, 64, S], BF16)
    scr("kn", [4, 128, S], BF16)
    scr("mv", [S, 512], BF16)
    scr("kr", [64, S], BF16)
    scr("br", [4, 4, 128, S], BF16)


TWO_PI = float(2 * math.pi)
PI = float(math.pi)


def sin_reduced(fw, ki, kf, x, out):
    fw.op("dve", [x], [kf], lambda e: e.tensor_scalar(out=kf.ap, in0=x.ap, scalar1=1.0 / TWO_PI, scalar2=None, op0=ALU.mult))
    fw.op("dve", [kf], [ki], lambda e: e.tensor_copy(out=ki.ap, in_=kf.ap))
    fw.op("dve", [ki], [kf], lambda e: e.tensor_copy(out=kf.ap, in_=ki.ap))
    fw.op("dve", [kf, x], [out], lambda e: e.scalar_tensor_tensor(out=out.ap, in0=kf.ap, scalar=-TWO_PI, in1=x.ap, op0=ALU.mult, op1=ALU.add))
    fw.op("dve", [out], [kf], lambda e: e.tensor_scalar(out=kf.ap, in0=out.ap, scalar1=PI, scalar2=-TWO_PI, op0=ALU.is_gt, op1=ALU.mult))
    fw.op("dve", [out, kf], [out], lambda e: e.tensor_tensor(out=out.ap, in0=out.ap, in1=kf.ap, op=ALU.add))
    fw.op("dve", [out], [kf], lambda e: e.tensor_scalar(out=kf.ap, in0=out.ap, scalar1=-PI, scalar2=TWO_PI, op0=ALU.is_lt, op1=ALU.mult))
    fw.op("dve", [out, kf], [out], lambda e: e.tensor_tensor(out=out.ap, in0=out.ap, in1=kf.ap, op=ALU.add))
    fw.op("act", [out], [out], lambda e: e.activation(out=out.ap, in_=out.ap, func=AF.Sin))


def rope_tables(g):
    nc, fw = g.nc, g.fw
    with contextlib.ExitStack() as st:
        sb = lambda n, s, d: g.sb(n, s, d, st)
        posi = sb("r_posi", [128, S], I32)
        posf = sb("r_posf", [128, S], F32)
        ang = sb("r_ang", [128, S], F32)
        ki = sb("r_ki", [128, S], I32)
        kf = sb("r_kf", [128, S], F32)
        out = sb("r_out", [128, S], F32)
        kc = sb("r_kc", [128, 4], F32)
        fw.dma("sp", "misc", kc, g.IT["kconst"])
        fw.dma("sp", "misc2", posi, g.IT["pos"], in_ap=g.I["pos"].partition_broadcast(128))
        fw.op("dve", [posi], [posf], lambda e: e.tensor_copy(out=posf.ap, in_=posi.ap))
        for t in range(2):
            invf = kc.ap[:, 2 * t:2 * t + 1]
            sgn = kc.ap[:, 2 * t + 1:2 * t + 2]
            fw.op("dve", [posf, kc], [ang], lambda e, invf=invf: e.tensor_scalar(out=ang.ap, in0=posf.ap, scalar1=invf, scalar2=None, op0=ALU.mult))
            sin_reduced(fw, ki, kf, ang, out)
            fw.op("dve", [out, kc], [out], lambda e, sgn=sgn: e.tensor_scalar(out=out.ap, in0=out.ap, scalar1=sgn, scalar2=None, op0=ALU.mult))
            fw.dma("sp", "r_st", g.DT["tab"], out, out_ap=g.D["tab"][2 * t + 1])
            fw.op("dve", [ang], [ang], lambda e: e.tensor_scalar(out=ang.ap, in0=ang.ap, scalar1=PI / 2, scalar2=None, op0=ALU.add))
            sin_reduced(fw, ki, kf, ang, out)
            fw.dma("sp", "r_st", g.DT["tab"], out, out_ap=g.D["tab"][2 * t])
        fw.barrier()


def setup_masks(g):
    fw = g.fw
    sb = g.sb
    onef = sb("c_onef", [128, 512], F32)
    fw.op("pool", [], [onef], lambda e: e.memset(onef.ap, 1.0))
    g.onef = onef
    zf = sb("c_zf", [128, 128], F32)
    fw.op("pool", [], [zf], lambda e: e.memset(zf.ap, 0.0))
    g.ident = sb("c_ident", [128, 128], BF16)
    fw.op("pool", [onef], [g.ident], lambda e: e.affine_select(
        out=g.ident.ap, in_=onef.ap[:, 0:128], pattern=[[1, 128]], compare_op=ALU.is_equal, fill=0.0, base=0, channel_multiplier=-1))
    g.cmask = []
    for r in range(4):
        m = sb(f"c_cm{r}", [128, 512], BF16)
        fw.op("pool", [onef], [m], lambda e, m=m, r=r: e.affine_select(
            out=m.ap, in_=onef.ap, pattern=[[1, 512]], compare_op=ALU.is_ge, fill=0.0, base=-128 * r, channel_multiplier=-1))
        g.cmask.append(m)
    g.gmask = sb("c_gm", [128, 128], BF16)
    fw.op("pool", [onef], [g.gmask], lambda e: e.affine_select(
        out=g.gmask.ap, in_=onef.ap[:, 0:128], pattern=[[1, 128]], compare_op=ALU.is_ge, fill=0.0, base=0, channel_multiplier=-1))
    fw.op("pool", [], [g.gmask], lambda e: e.memset(g.gmask.ap[0:64, 64:128], 0.0))
    g.dbias = sb("c_db", [128, 128], F32)
    fw.op("pool", [zf], [g.dbias], lambda e: e.affine_select(
        out=g.dbias.ap, in_=zf.ap, pattern=[[-1, 128]], compare_op=ALU.is_ge, fill=-1e30, base=0, channel_multiplier=1))
    g.rmask = sb("c_rm", [64, 512], F32)
    fw.op("pool", [], [g.rmask], lambda e: e.memset(g.rmask.ap, 1.0))
    fw.op("pool", [], [g.rmask], lambda e: e.memset(g.rmask.ap.rearrange("p (c j) -> p c j", j=64)[:, :, 0:1], 0.0))
    g.jrow = sb("c_jrow", [128, 128], F32)
    fw.op("pool", [], [g.jrow], lambda e: e.iota(g.jrow.ap, pattern=[[1, 128]], base=0, channel_multiplier=0,
                                                  allow_small_or_imprecise_dtypes=True))


def mixer_phase(g, l):
    proj_phase(g, l)
    if g.stopsub == "proj":
        return
    mla_phase(g, l)
    if g.stopsub == "mla":
        return
    dsa_phase(g, l)
    if g.stopsub == "dsa":
        return
    gla_phase(g, l)
    if g.stopsub == "gla":
        return
    s5_phase(g, l)
    if g.stopsub == "s5":
        return
    merge_phase(g, l)


def proj_phase(g, l):
    nc, fw = g.nc, g.fw
    D_, DT = g.D, g.DT
    Wx = g.W[("winx", l)][1]
    Wv = g.W[("winv", l)][1]
    Ww = g.W[("winw", l)][1]
    Wuq = g.W[("uq", l)][1]
    Wkk = g.W[("ukvk", l)][1]
    Wkv = g.W[("ukvv", l)][1]
    ps = g.ps
    with contextlib.ExitStack() as st:
        sb = lambda n, s, d: g.sb(n, s, d, st)
        xt = sb("p_xt", [128, NCH, TT], F32)
        sq = sb("p_sq", [128, NCH, TT], BF16)
        h = sb("p_h", [128, NCH, TT], BF16)
        rstd = sb("p_rstd", [128, TT], F32)
        wsl = [sb(f"p_wsl{i}", [128, NCH, 128], BF16) for i in range(3)]
        wvs = [sb(f"p_wv{i}", [128, NCH, 512], BF16) for i in range(2)]
        wws = sb("p_ww", [128, NCH, 128], BF16)
        wuk = sb("p_wuk", [128, 1, 512], BF16)
        tabs = sb("p_tabs", [128, 4, TT], F32)
        t1 = sb("p_t1", [128, TT], F32)
        t2 = sb("p_t2", [128, TT], F32)
        sqb = sb("p_sqb", [128, TT], BF16)
        rs = sb("p_rs", [128, TT], F32)
        obs = [sb(f"p_ob{i}", [128, TT], BF16) for i in range(3)]
        ofs = [sb(f"p_of{i}", [128, TT], F32) for i in range(2)]
        ql = sb("p_ql", [128, 3, TT], BF16)
        kvl = sb("p_kvl", [128, TT], BF16)
        gn = sb("p_gn", [128, 16], F32)
        fw.dma("sp", "misc", gn, g.IT["pvec"], in_ap=g.I["pvec"][l][:, 192:208])
        fw.dma("sp", "misc2", wws, Ww[0])
        fw.dma("sp", "misc3", wuk, Wkv[0])
        cnt = {"w": 0, "ob": 0, "of": 0, "wv": 0}

        def slab(blkT, nk=NCH):
            w = wsl[cnt["w"] % 3]
            fw.dma("sp", f"p_w{cnt['w'] % 3}", w, blkT, out_ap=w.ap[:, 0:nk, :])
            cnt["w"] += 1
            return w

        def mm(w, pst, M, rhs_fn, nk=NCH, c0=0):
            for k in range(nk):
                fw.op("pe", [w, h, ql, kvl], [pst], lambda e, k=k: e.matmul(
                    pst.ap[0:M, :], w.ap[:, k, c0:c0 + M], rhs_fn(k), start=(k == 0), stop=(k == nk - 1)))

        rh = lambda k: h.ap[:, k, :]

        def nob():
            o = obs[cnt["ob"] % 3]
            cnt["ob"] += 1
            return o, f"p_ob{cnt['ob'] % 3}"

        def nof():
            o = ofs[cnt["of"] % 2]
            cnt["of"] += 1
            return o, f"p_of{cnt['of'] % 2}"

        def norm_rs(px, M, d):
            fw.op("act", [px], [sqb], lambda e: e.activation(out=sqb.ap[0:M, :], in_=px.ap[0:M, :], func=AF.Square))
            fw.op("pe", [sqb, g.ones_bf], [ps[6]], lambda e: e.matmul(
                ps[6].ap[0:M, :], g.ones_bf.ap[0:M, 0:M], sqb.ap[0:M, :], start=True, stop=True))
            fw.op("act", [ps[6], g.eps_t], [rs], lambda e: e.activation(
                out=rs.ap[0:M, :], in_=ps[6].ap[0:M, :], func=AF.Sqrt, bias=g.eps_t.ap[0:M, 0:1], scale=1.0 / d))
            fw.op("dve", [rs], [rs], lambda e: e.reciprocal(out=rs.ap[0:M, :], in_=rs.ap[0:M, :]))

        def rope_out(px, pxs, M, gcol, ti, norm_d, dst_ap, dstT):
            if norm_d:
                norm_rs(px, M, norm_d)
            ga = gn.ap[0:M, gcol:gcol + 1] if gcol is not None else 1.0
            gs = gn.ap[0:M, gcol + 1:gcol + 2] if gcol is not None else 1.0
            fw.op("dve", [px, gn, tabs], [t1], lambda e: e.scalar_tensor_tensor(
                out=t1.ap[0:M, :], in0=px.ap[0:M, :], scalar=ga, in1=tabs.ap[0:M, ti, :], op0=ALU.mult, op1=ALU.mult))
            fw.op("dve", [pxs, gn, tabs], [t2], lambda e: e.scalar_tensor_tensor(
                out=t2.ap[0:M, :], in0=pxs.ap[0:M, :], scalar=gs, in1=tabs.ap[0:M, ti + 1, :], op0=ALU.mult, op1=ALU.mult))
            o, key = nob()
            if norm_d:
                fw.op("dve", [t1, t2], [t1], lambda e: e.tensor_tensor(out=t1.ap[0:M, :], in0=t1.ap[0:M, :], in1=t2.ap[0:M, :], op=ALU.add))
                fw.op("dve", [t1, rs], [o], lambda e: e.tensor_tensor(out=o.ap[0:M, :], in0=t1.ap[0:M, :], in1=rs.ap[0:M, :], op=ALU.mult))
            else:
                fw.op("dve", [t1, t2], [o], lambda e: e.tensor_tensor(out=o.ap[0:M, :], in0=t1.ap[0:M, :], in1=t2.ap[0:M, :], op=ALU.add))
            fw.dma("sp", key, dstT, o, out_ap=dst_ap, in_ap=o.ap[0:M, :])

        def norm_out(px, M, gcol, d, dst_ap, dstT, keep=None):
            norm_rs(px, M, d)
            if keep is not None:
                o, key = keep, None
            else:
                o, key = nob()
            fw.op("dve", [px, gn, rs], [o], lambda e: e.scalar_tensor_tensor(
                out=(o.ap[0:M, :] if keep is None else keep.ap), in0=px.ap[0:M, :], scalar=gn.ap[0:M, gcol:gcol + 1],
                in1=rs.ap[0:M, :], op0=ALU.mult, op1=ALU.mult))
            if keep is None:
                fw.dma("sp", key, dstT, o, out_ap=dst_ap, in_ap=o.ap[0:M, :])

        for tt in range(NTT):
            cs = slice(tt * TT, (tt + 1) * TT)
            fw.dma("sp", "p_xl", xt, g.yT[tt], in_ap=g.y[:, cs].rearrange("(c p) t -> p c t", p=128))
            fw.dma("sp", "p_tab", tabs, DT["tab"], in_ap=D_["tab"][:, :, cs].rearrange("f p t -> p f t"))
            ada_norm_tile(g, l, 1, xt, h, sq, rstd, ps[7])
            fw.dma("sp", "p_hs", DT["hT"], h, out_ap=D_["hT"][:, cs].rearrange("(c p) t -> p c t", p=128))
            for nm, b0, gcol in (("qA", 0, 0), ("kA", 8, 2)):
                for hd in range(4):
                    w = slab(Wx[b0 + hd]); mm(w, ps[0], 128, rh)
                    w = slab(Wx[b0 + 4 + hd]); mm(w, ps[1], 128, rh)
                    rope_out(ps[0], ps[1], 128, gcol, 0, 128, D_[nm][hd][:, cs], DT[nm])
            for c in range(4):
                w = slab(Wx[16 + c]); mm(w, ps[0], 128, rh)
                w = slab(Wx[20 + c]); mm(w, ps[1], 128, rh)
                rope_out(ps[0], ps[1], 128, None, 2, 0, D_["iq"][c][:, cs], DT["iq"])
            w = slab(Wx[24]); mm(w, ps[0], 128, rh)
            w = slab(Wx[25]); mm(w, ps[1], 128, rh)
            rope_out(ps[0], ps[1], 128, None, 2, 0, D_["ik"][:, cs], DT["ik"])
            for c in range(4):
                w = slab(Wx[26 + c]); mm(w, ps[2 + c % 2], 128, rh)
                o, key = nob()
                fw.op("act", [ps[2 + c % 2]], [o], lambda e, o=o, c=c: e.activation(out=o.ap, in_=ps[2 + c % 2].ap, func=AF.Silu))
                fw.dma("sp", key, DT["gr"], o, out_ap=D_["gr"][c][:, cs])
            for c in range(4):
                w = slab(Wx[30 + c]); mm(w, ps[2 + c % 2], 128, rh)
                o, key = nof()
                fw.op("act", [ps[2 + c % 2]], [o], lambda e, o=o, c=c: e.activation(out=o.ap, in_=ps[2 + c % 2].ap, func=AF.Copy))
                fw.dma("sp", key, DT["su"], o, out_ap=D_["su"][c][:, cs])
            for nm, b0, scl in (("gq", 39, 0.125), ("gk", 41, 1.0)):
                for hd in range(4):
                    if hd % 2 == 0:
                        w = slab(Wx[b0 + hd // 2])
                    pp = ps[2 + hd % 2]
                    mm(w, pp, 64, rh, c0=(hd % 2) * 64)
                    o, key = nof()
                    fw.op("act", [pp], [o], lambda e, o=o, pp=pp, scl=scl: e.activation(
                        out=o.ap[0:64, :], in_=pp.ap[0:64, :], func=AF.Copy, scale=scl))
                    fw.dma("sp", key, DT[nm], o, out_ap=D_[nm][hd][:, cs], in_ap=o.ap[0:64, :])
            w = slab(Wx[43]); mm(w, ps[2], 16, rh)
            o, key = nob()
            fw.op("act", [ps[2]], [o], lambda e, o=o: e.activation(out=o.ap[0:16, :], in_=ps[2].ap[0:16, :], func=AF.Copy))
            fw.dma("sp", key, DT["glr"], o, out_ap=D_["glr"][:, cs], in_ap=o.ap[0:16, :])
            for vi, nm in ((0, "vA"), (1, "gv")):
                wv = wvs[cnt["wv"] % 2]
                fw.dma("sp", f"p_wv{cnt['wv'] % 2}", wv, Wv[vi])
                cnt["wv"] += 1
                for sub in range(4):
                    pp = ps[2 + sub % 2]
                    for k in range(NCH):
                        fw.op("pe", [wv, h], [pp], lambda e, k=k, pp=pp, sub=sub, wv=wv: e.matmul(
                            pp.ap, h.ap[:, k, sub * 128:(sub + 1) * 128], wv.ap[:, k, :], start=(k == 0), stop=(k == NCH - 1)))
                    o, key = nob()
                    fw.op("act", [pp], [o], lambda e, o=o, pp=pp: e.activation(out=o.ap, in_=pp.ap, func=AF.Copy))
                    fw.dma("sp", key, DT[nm], o, out_ap=D_[nm][tt * TT + sub * 128:tt * TT + (sub + 1) * 128, :])
            for sub in range(4):
                pp = ps[2 + sub % 2]
                for k in range(NCH):
                    fw.op("pe", [wws, h], [pp], lambda e, k=k, pp=pp, sub=sub: e.matmul(
                        pp.ap[:, 0:8], h.ap[:, k, sub * 128:(sub + 1) * 128], wws.ap[:, k, 0:8], start=(k == 0), stop=(k == NCH - 1)))
                o, key = nof()
                fw.op("act", [pp], [o], lambda e, o=o, pp=pp: e.activation(
                    out=o.ap[:, 0:8], in_=pp.ap[:, 0:8], func=AF.Copy, scale=float(8 ** -0.5 * 64 ** -0.5)))
                fw.dma("sp", key, DT["iw"], o, out_ap=D_["iw"][tt * TT + sub * 128:tt * TT + (sub + 1) * 128, :], in_ap=o.ap[:, 0:8])
            for c in range(3):
                w = slab(Wx[34 + c]); mm(w, ps[c], 128, rh)
            for c in range(3):
                fw.op("act", [ps[c]], [sqb], lambda e, c=c: e.activation(out=sqb.ap, in_=ps[c].ap, func=AF.Square))
                fw.op("pe", [sqb, g.ones_bf], [ps[6]], lambda e, c=c: e.matmul(
                    ps[6].ap, g.ones_bf.ap, sqb.ap, start=(c == 0), stop=(c == 2)))
            fw.op("act", [ps[6], g.eps_t], [rs], lambda e: e.activation(
                out=rs.ap, in_=ps[6].ap, func=AF.Sqrt, bias=g.eps_t.ap[:, 0:1], scale=1.0 / 384))
            fw.op("dve", [rs], [rs], lambda e: e.reciprocal(out=rs.ap, in_=rs.ap))
            for c in range(3):
                fw.op("dve", [ps[c], gn, rs], [ql], lambda e, c=c: e.scalar_tensor_tensor(
                    out=ql.ap[:, c, :], in0=ps[c].ap, scalar=gn.ap[:, 4 + c:5 + c], in1=rs.ap, op0=ALU.mult, op1=ALU.mult))
            rq = lambda k: ql.ap[:, k, :]
            for hd in range(4):
                w = slab(Wuq[hd], 3); mm(w, ps[0], 128, rq, nk=3)
                norm_out(ps[0], 128, 8, 128, D_["qn"][hd][:, cs], DT["qn"])
                w = slab(Wuq[4 + hd], 3)
                mm(w, ps[0], 64, rq, nk=3, c0=0)
                mm(w, ps[1], 64, rq, nk=3, c0=64)
                rope_out(ps[0], ps[1], 64, 9, 2, 64, D_["qr"][hd][:, cs], DT["qr"])
            w = slab(Wx[37]); mm(w, ps[0], 128, rh)
            norm_out(ps[0], 128, 7, 128, None, None, keep=kvl)
            rk = lambda k: kvl.ap
            for hd in range(4):
                w = slab(Wkk[hd], 1); mm(w, ps[hd % 2], 128, rk, nk=1)
                norm_out(ps[hd % 2], 128, 11, 128, D_["kn"][hd][:, cs], DT["kn"])
            for sub in range(4):
                pp = ps[2 + sub % 2]
                fw.op("pe", [wuk, kvl], [pp], lambda e, pp=pp, sub=sub: e.matmul(
                    pp.ap, kvl.ap[:, sub * 128:(sub + 1) * 128], wuk.ap[:, 0, :], start=True, stop=True))
                o, key = nob()
                fw.op("act", [pp], [o], lambda e, o=o, pp=pp: e.activation(out=o.ap, in_=pp.ap, func=AF.Copy))
                fw.dma("sp", key, DT["mv"], o, out_ap=D_["mv"][tt * TT + sub * 128:tt * TT + (sub + 1) * 128, :])
            w = slab(Wx[38])
            mm(w, ps[0], 64, rh, c0=0)
            mm(w, ps[1], 64, rh, c0=64)
            rope_out(ps[0], ps[1], 64, 12, 2, 64, D_["kr"][:, cs], DT["kr"])
        fw.barrier()


def attn_core(g, st, name, qT_loader, kT, kr, v, nq_parts, scale, bri, qt_mask_fn):
    pass


def mla_phase(g, l):
    nc, fw = g.nc, g.fw
    D_, DT = g.D, g.DT
    ps = g.ps
    sc = float(192 ** -0.5)
    with contextlib.ExitStack() as st:
        sb = lambda n, s, d: g.sb(n, s, d, st)
        kn = sb("m_kn", [128, 4, S], BF16)
        kr = sb("m_kr", [64, S], BF16)
        v = sb("m_v", [128, 32, 512], BF16)
        qn = [sb(f"m_qn{i}", [128, 4, TT], BF16) for i in range(2)]
        qr = [sb(f"m_qr{i}", [64, 4, TT], BF16) for i in range(2)]
        pt = [sb(f"m_pt{i}", [128, TT], BF16) for i in range(3)]
        rd = sb("m_rd", [128, TT], F32)
        ob = [sb(f"m_ob{i}", [128, TT], BF16) for i in range(2)]
        fw.dma("sp", "m_l0", kn, DT["kn"], in_ap=D_["kn"].rearrange("h p t -> p h t"))
        fw.dma("sp", "m_l1", kr, DT["kr"])
        fw.dma("sp", "m_l2", v, DT["mv"], in_ap=D_["mv"].rearrange("(k p) c -> p k c", p=128))
        np_ = 0
        no = 0
        for qt in range(NTT):
            cs = slice(qt * TT, (qt + 1) * TT)
            q1, q2 = qn[qt % 2], qr[qt % 2]
            fw.dma("sp", f"m_q{qt % 2}", q1, DT["qn"], in_ap=D_["qn"][:, :, cs].rearrange("h p t -> p h t"))
            fw.dma("sp", f"m_r{qt % 2}", q2, DT["qr"], in_ap=D_["qr"][:, :, cs].rearrange("h p t -> p h t"))
            nkt = 4 * qt + 4
            for hd in range(4):
                O, Dn = ps[2 + hd % 2], ps[4 + hd % 2]
                for kt in range(nkt):
                    Lp = ps[kt % 2]
                    ks = slice(kt * 128, (kt + 1) * 128)
                    fw.op("pe", [kn, q1], [Lp], lambda e, Lp=Lp, ks=ks, hd=hd, q1=q1: e.matmul(
                        Lp.ap, kn.ap[:, hd, ks], q1.ap[:, hd, :], start=True, stop=False))
                    fw.op("pe", [kr, q2], [Lp], lambda e, Lp=Lp, ks=ks, hd=hd, q2=q2: e.matmul(
                        Lp.ap, kr.ap[:, ks], q2.ap[:, hd, :], start=False, stop=True))
                    P = pt[np_ % 3]
                    np_ += 1
                    fw.op("act", [Lp], [P], lambda e, Lp=Lp, P=P: e.activation(out=P.ap, in_=Lp.ap, func=AF.Exp, scale=sc))
                    if kt >= 4 * qt:
                        cm = g.cmask[kt - 4 * qt]
                        fw.op("dve", [P, cm], [P], lambda e, P=P, cm=cm: e.tensor_tensor(out=P.ap, in0=P.ap, in1=cm.ap, op=ALU.mult))
                    fw.op("pe", [v, P], [O], lambda e, O=O, P=P, kt=kt, hd=hd: e.matmul(
                        O.ap, v.ap[:, kt, hd * 128:(hd + 1) * 128], P.ap, start=(kt == 0), stop=(kt == nkt - 1)))
                    fw.op("pe", [g.ones_bf, P], [Dn], lambda e, Dn=Dn, P=P, kt=kt: e.matmul(
                        Dn.ap, g.ones_bf.ap, P.ap, start=(kt == 0), stop=(kt == nkt - 1)))
                fw.op("dve", [Dn], [rd], lambda e, Dn=Dn: e.reciprocal(out=rd.ap, in_=Dn.ap))
                o = ob[no % 2]
                fw.op("dve", [O, rd], [o], lambda e, O=O, o=o: e.tensor_tensor(out=o.ap, in0=O.ap, in1=rd.ap, op=ALU.mult))
                fw.dma("sp", f"m_o{no % 2}", DT["br"], o, out_ap=D_["br"][3, hd][:, cs])
                no += 1
        fw.barrier()


NITER = 20


def dsa_phase(g, l):
    nc, fw = g.nc, g.fw
    D_, DT = g.D, g.DT
    ps = g.ps
    sc = float(128 ** -0.5)
    with contextlib.ExitStack() as st:
        sb = lambda n, s, d: g.sb(n, s, d, st)
        kA = sb("a_k", [128, 4, S], BF16)
        vA = sb("a_v", [128, 32, 512], BF16)
        ik = sb("a_ik", [128, S], BF16)
        iqs = [sb(f"a_iq{i}", [128, 4, 128], BF16) for i in range(2)]
        maskT = sb("a_mT", [128, 32, TT], BF16)
        score = sb("a_sc", [128, S], F32)
        maskq = sb("a_mq", [128, S], BF16)
        junk = maskq
        rl = [sb(f"a_rl{i}", [128, 512], F32) for i in range(2)]
        iw = [sb(f"a_iw{i}", [128, 8], F32) for i in range(2)]
        sm = sb("a_sm", [128, 8], F32)
        qA = [sb(f"a_q{i}", [128, 4, TT], BF16) for i in range(2)]
        pt = [sb(f"a_pt{i}", [128, TT], BF16) for i in range(3)]
        rd = sb("a_rd", [128, TT], F32)
        ob = [sb(f"a_ob{i}", [128, TT], BF16) for i in range(2)]
        fw.dma("sp", "a_l0", kA, DT["kA"], in_ap=D_["kA"].rearrange("h p t -> p h t"))
        fw.dma("sp", "a_l1", vA, DT["vA"], in_ap=D_["vA"].rearrange("(k p) c -> p k c", p=128))
        fw.dma("sp", "a_l2", ik, DT["ik"])
        d0, lo, mid, cn, stp = (sm.ap[:, i:i + 1] for i in range(5))
        np_ = 0
        no = 0
        nr = 0
        ptb = ps[6].ap.bitcast(BF16)
        for qt in range(NTT):
            cs = slice(qt * TT, (qt + 1) * TT)
            q1 = qA[qt % 2]
            fw.dma("sp", f"a_q{qt % 2}", q1, DT["qA"], in_ap=D_["qA"][:, :, cs].rearrange("h p t -> p h t"))
            fw.op("dve", [], [maskT], lambda e, qt=qt: e.memset(maskT.ap[:, 4 * qt:4 * qt + 4, :], 0.0))
            for qi in range(4):
                i = 4 * qt + qi
                nk = (i + 1) * 128
                qs = slice(i * 128, (i + 1) * 128)
                iwt = iw[i % 2]
                fw.dma("sp", f"a_iw{i % 2}", iwt, DT["iw"], in_ap=D_["iw"][qs, :])
                iq = iqs[i % 2]
                fw.dma("sp", f"a_iq{i % 2}", iq, DT["iq"], in_ap=D_["iq"][:, :, qs].rearrange("h p t -> p h t"))
                for kk in range((nk + 511) // 512):
                    w = min(512, nk - kk * 512)
                    ks = slice(kk * 512, kk * 512 + w)
                    for hh in range(8):
                        c, half = hh // 2, hh % 2
                        pr = slice(half * 64, half * 64 + 64)
                        Sp = ps[hh % 2]
                        fw.op("pe", [iq, ik], [Sp], lambda e, Sp=Sp, pr=pr, c=c, iq=iq, ks=ks, w=w: e.matmul(
                            Sp.ap[:, 0:w], iq.ap[pr, c, :], ik.ap[pr, ks], start=True, stop=True))
                        r = rl[nr % 2]
                        nr += 1
                        fw.op("act", [Sp], [r], lambda e, Sp=Sp, r=r, w=w: e.activation(out=r.ap[:, 0:w], in_=Sp.ap[:, 0:w], func=AF.Relu))
                        if hh == 0:
                            fw.op("dve", [r, iwt], [score], lambda e, r=r, w=w, ks=ks, iwt=iwt: e.tensor_scalar(
                                out=score.ap[:, ks], in0=r.ap[:, 0:w], scalar1=iwt.ap[:, 0:1], scalar2=None, op0=ALU.mult))
                        else:
                            fw.op("dve", [r, iwt, score], [score], lambda e, r=r, w=w, ks=ks, iwt=iwt, hh=hh: e.scalar_tensor_tensor(
                                out=score.ap[:, ks], in0=r.ap[:, 0:w], scalar=iwt.ap[:, hh:hh + 1], in1=score.ap[:, ks],
                                op0=ALU.mult, op1=ALU.add))
                fw.op("dve", [score], [sm], lambda e, nk=nk: e.reduce_max(
                    out=d0, in_=score.ap[:, 0:nk], axis=AX.X, apply_absolute_value=True))
                fw.op("dve", [score, g.dbias], [score], lambda e, qs=qs: e.tensor_tensor(
                    out=score.ap[:, qs], in0=score.ap[:, qs], in1=g.dbias.ap, op=ALU.add))
                fw.op("dve", [sm], [sm], lambda e: e.tensor_scalar(out=d0, in0=d0, scalar1=1.0, scalar2=None, op0=ALU.add))
                fw.op("dve", [sm], [sm], lambda e: e.tensor_scalar(out=lo, in0=d0, scalar1=-1.0, scalar2=None, op0=ALU.mult))
                for it in range(NITER):
                    f = float(2.0 ** (-it))
                    fw.op("dve", [sm], [sm], lambda e, f=f: e.scalar_tensor_tensor(
                        out=mid, in0=d0, scalar=f, in1=lo, op0=ALU.mult, op1=ALU.add))
                    fw.op("dve", [score, sm], [maskq, sm], lambda e, nk=nk: e.tensor_scalar(
                        out=junk.ap[:, 0:nk], in0=score.ap[:, 0:nk], scalar1=mid, scalar2=0.0, op0=ALU.is_ge, op1=ALU.add, accum_out=cn))
                    fw.op("dve", [sm], [sm], lambda e: e.scalar_tensor_tensor(
                        out=stp, in0=cn, scalar=255.5, in1=d0, op0=ALU.is_ge, op1=ALU.mult))
                    fw.op("dve", [sm], [sm], lambda e, f=f: e.scalar_tensor_tensor(
                        out=lo, in0=stp, scalar=f, in1=lo, op0=ALU.mult, op1=ALU.add))
                fw.op("dve", [score, sm], [maskq], lambda e, nk=nk: e.tensor_scalar(
                    out=maskq.ap[:, 0:nk], in0=score.ap[:, 0:nk], scalar1=lo, scalar2=None, op0=ALU.is_ge))
                for kb0 in range(0, i + 1, 4):
                    nb = min(4, i + 1 - kb0)
                    for b in range(nb):
                        kb = kb0 + b
                        fw.op("pe", [maskq, g.ident], [ps[6]], lambda e, kb=kb, b=b: e.transpose(
                            ptb[:, b * 128:(b + 1) * 128], maskq.ap[:, kb * 128:(kb + 1) * 128], g.ident.ap))
                    fw.op("act", [ps[6]], [maskT], lambda e, kb0=kb0, nb=nb, qi=qi: e.activation(
                        out=maskT.ap[:, kb0:kb0 + nb, qi * 128:(qi + 1) * 128],
                        in_=ptb[:, 0:nb * 128].rearrange("p (b c) -> p b c", c=128), func=AF.Copy))
            nkt = 4 * qt + 4
            for hd in range(4):
                O, Dn = ps[2 + hd % 2], ps[4 + hd % 2]
                for kt in range(nkt):
                    Lp = ps[kt % 2]
                    ks = slice(kt * 128, (kt + 1) * 128)
                    fw.op("pe", [kA, q1], [Lp], lambda e, Lp=Lp, ks=ks, hd=hd, q1=q1: e.matmul(
                        Lp.ap, kA.ap[:, hd, ks], q1.ap[:, hd, :], start=True, stop=True))
                    P = pt[np_ % 3]
                    np_ += 1
                    fw.op("act", [Lp], [P], lambda e, Lp=Lp, P=P: e.activation(out=P.ap, in_=Lp.ap, func=AF.Exp, scale=sc))
                    fw.op("dve", [P, maskT], [P], lambda e, P=P, kt=kt: e.tensor_tensor(
                        out=P.ap, in0=P.ap, in1=maskT.ap[:, kt, :], op=ALU.mult))
                    fw.op("pe", [vA, P], [O], lambda e, O=O, P=P, kt=kt, hd=hd: e.matmul(
                        O.ap, vA.ap[:, kt, hd * 128:(hd + 1) * 128], P.ap, start=(kt == 0), stop=(kt == nkt - 1)))
                    fw.op("pe", [g.ones_bf, P], [Dn], lambda e, Dn=Dn, P=P, kt=kt: e.matmul(
                        Dn.ap, g.ones_bf.ap, P.ap, start=(kt == 0), stop=(kt == nkt - 1)))
                fw.op("dve", [Dn], [rd], lambda e, Dn=Dn: e.reciprocal(out=rd.ap, in_=Dn.ap))
                o = ob[no % 2]
                fw.op("dve", [O, rd], [o], lambda e, O=O, o=o: e.tensor_tensor(out=o.ap, in0=O.ap, in1=rd.ap, op=ALU.mult))
                fw.dma("sp", f"a_o{no % 2}", DT["br"], o, out_ap=D_["br"][0, hd][:, cs])
                no += 1
        fw.barrier()


def gla_phase(g, l):
    nc, fw = g.nc, g.fw
    D_, DT = g.D, g.DT
    ps = g.ps
    with contextlib.ExitStack() as st:
        sb = lambda n, s, d: g.sb(n, s, d, st)
        gv = sb("g_v", [128, 32, 512], BF16)
        glr = sb("g_lr", [16, S], BF16)
        wgf = sb("g_wgf", [16, 256], F32)
        wgb = sb("g_wgb", [16, 256], BF16)
        pv = sb("g_pv", [128, 8], F32)
        nbg = sb("g_nbg", [64, 4], F32)
        q = sb("g_q", [64, S], F32)
        k = sb("g_k", [64, S], F32)
        cs_ = sb("g_cs", [64, S], F32)
        eb = sb("g_eb", [64, S], F32)
        qt_ = sb("g_qt", [64, S], BF16)
        ktb = sb("g_ktb", [64, S], BF16)
        khat = sb("g_kh", [64, S], BF16)
        khT = sb("g_khT", [128, 32, 64], BF16)
        Ofm = sb("g_O", [128, S], F32)
        Ab = [sb(f"g_Ab{i}", [128, 128], BF16) for i in range(2)]
        Sst = sb("g_S", [64, 128], F32)
        Sb = sb("g_Sb", [64, 128], BF16)
        sqb = sb("g_sqb", [128, TT], BF16)
        rs = sb("g_rs", [128, TT], F32)
        grt = [sb(f"g_gr{i}", [128, TT], BF16) for i in range(2)]
        t1 = sb("g_t1", [128, TT], F32)
        ob = [sb(f"g_ob{i}", [128, TT], BF16) for i in range(2)]
        fw.dma("sp", "g_l0", gv, DT["gv"], in_ap=D_["gv"].rearrange("(k p) c -> p k c", p=128))
        fw.dma("sp", "g_l1", glr, DT["glr"])
        fw.dma("sp", "g_l2", wgf, g.IT["gla_w_gate"], in_ap=g.I["gla_w_gate"][l])
        fw.dma("sp", "g_l3", pv, g.IT["pvec"], in_ap=g.I["pvec"][l][:, 206:214])
        fw.op("dve", [wgf], [wgb], lambda e: e.tensor_copy(out=wgb.ap, in_=wgf.ap))
        fw.op("dve", [pv], [nbg], lambda e: e.tensor_scalar(out=nbg.ap, in0=pv.ap[0:64, 0:4], scalar1=-1.0, scalar2=None, op0=ALU.mult))
        ptb = ps[6].ap.bitcast(BF16)
        no = 0
        for hd in range(4):
            fw.dma("sp", "g_lq", q, DT["gq"], in_ap=D_["gq"][hd])
            fw.dma("sp", "g_lk", k, DT["gk"], in_ap=D_["gk"][hd])
            for tt in range(NTT):
                cs = slice(tt * TT, (tt + 1) * TT)
                pz = ps[tt % 2]
                fw.op("pe", [wgb, glr], [pz], lambda e, pz=pz, cs=cs, hd=hd: e.matmul(
                    pz.ap[0:64, :], wgb.ap[:, hd * 64:(hd + 1) * 64], glr.ap[:, cs], start=True, stop=True))
                fw.op("act", [pz, nbg], [eb], lambda e, pz=pz, cs=cs, hd=hd: e.activation(
                    out=eb.ap[:, cs], in_=pz.ap[0:64, :], func=AF.Exp, bias=nbg.ap[:, hd:hd + 1], scale=-1.0))
                fw.op("act", [eb], [eb], lambda e, cs=cs: e.activation(
                    out=eb.ap[:, cs], in_=eb.ap[:, cs], func=AF.Ln, bias=1.0, scale=1.0))
                fw.op("dve", [eb, g.rmask], [cs_], lambda e, cs=cs: e.tensor_tensor_scan(
                    out=cs_.ap[:, cs], data0=g.rmask.ap, data1=eb.ap[:, cs], initial=0.0, op0=ALU.mult, op1=ALU.add))
            fw.op("act", [cs_], [eb], lambda e: e.activation(out=eb.ap, in_=cs_.ap, func=AF.Exp, scale=-1.0 / 16))
            fw.op("dve", [q, eb], [qt_], lambda e: e.tensor_tensor(out=qt_.ap, in0=q.ap, in1=eb.ap, op=ALU.mult))
            fw.op("act", [cs_], [cs_], lambda e: e.activation(out=cs_.ap, in_=cs_.ap, func=AF.Exp, scale=1.0 / 16))
            fw.op("dve", [k, cs_], [k], lambda e: e.tensor_tensor(out=k.ap, in0=k.ap, in1=cs_.ap, op=ALU.mult))
            fw.op("act", [k], [ktb], lambda e: e.activation(out=ktb.ap, in_=k.ap, func=AF.Copy))
            ebl = eb.ap.rearrange("p (c j) -> p c j", j=64)[:, :, 63:64]
            fw.op("dve", [k, eb], [khat], lambda e: e.tensor_tensor(
                out=khat.ap.rearrange("p (c j) -> p c j", j=64), in0=k.ap.rearrange("p (c j) -> p c j", j=64),
                in1=ebl.to_broadcast([64, 64, 64]), op=ALU.mult))
            for t0 in range(0, 32, 8):
                for b in range(8):
                    fw.op("pe", [khat, g.ident], [ps[6]], lambda e, t0=t0, b=b: e.transpose(
                        ptb[:, b * 64:(b + 1) * 64], khat.ap[:, (t0 + b) * 128:(t0 + b + 1) * 128], g.ident.ap[0:64, 0:64]))
                fw.op("act", [ps[6]], [khT], lambda e, t0=t0: e.activation(
                    out=khT.ap[:, t0:t0 + 8, :], in_=ptb[:, 0:512].rearrange("p (b c) -> p b c", c=64), func=AF.Copy))
            fw.op("dve", [], [Sst], lambda e: e.memset(Sst.ap, 0.0))
            for tl in range(32):
                ts_ = slice(tl * 128, (tl + 1) * 128)
                pa = ps[tl % 2]
                fw.op("pe", [ktb, qt_], [pa], lambda e, pa=pa, ts_=ts_: e.matmul(
                    pa.ap[:, 0:128], ktb.ap[:, ts_], qt_.ap[:, ts_], start=True, stop=True))
                A = Ab[tl % 2]
                fw.op("dve", [pa, g.gmask], [A], lambda e, pa=pa, A=A: e.tensor_tensor(
                    out=A.ap, in0=pa.ap[:, 0:128], in1=g.gmask.ap, op=ALU.mult))
                O = ps[2 + tl % 2]
                last_is_inter = True
                fw.op("pe", [gv, A], [O], lambda e, O=O, A=A, tl=tl, hd=hd: e.matmul(
                    O.ap[:, 0:128], gv.ap[:, tl, hd * 128:(hd + 1) * 128], A.ap, start=True, stop=False))
                for ci in range(2):
                    c = 2 * tl + ci
                    cc = slice(c * 64, (c + 1) * 64)
                    if c > 0:
                        fw.op("pe", [Sb, qt_], [O], lambda e, O=O, ci=ci, cc=cc: e.matmul(
                            O.ap[:, ci * 64:(ci + 1) * 64], Sb.ap, qt_.ap[:, cc], start=False, stop=(ci == 1)))
                    U = ps[4 + c % 2]
                    pr = slice(ci * 64, (ci + 1) * 64)
                    fw.op("pe", [khT, gv], [U], lambda e, U=U, pr=pr, tl=tl, hd=hd: e.matmul(
                        U.ap[0:64, 0:128], khT.ap[pr, tl, :], gv.ap[pr, tl, hd * 128:(hd + 1) * 128], start=True, stop=True))
                    fw.op("dve", [U, Sst, eb], [Sst], lambda e, U=U, c=c: e.scalar_tensor_tensor(
                        out=Sst.ap, in0=Sst.ap, scalar=eb.ap[:, c * 64 + 63:c * 64 + 64], in1=U.ap[0:64, 0:128],
                        op0=ALU.mult, op1=ALU.add))
                    fw.op("act", [Sst], [Sb], lambda e: e.activation(out=Sb.ap, in_=Sst.ap, func=AF.Copy))
                fw.op("act", [O], [Ofm], lambda e, O=O, ts_=ts_: e.activation(out=Ofm.ap[:, ts_], in_=O.ap[:, 0:128], func=AF.Copy))
            for tt in range(NTT):
                cs = slice(tt * TT, (tt + 1) * TT)
                gr = grt[tt % 2]
                fw.dma("sp", f"g_gr{tt % 2}", gr, DT["gr"], in_ap=D_["gr"][hd][:, cs])
                fw.op("act", [Ofm], [sqb], lambda e, cs=cs: e.activation(out=sqb.ap, in_=Ofm.ap[:, cs], func=AF.Square))
                fw.op("pe", [sqb, g.ones_bf], [ps[7]], lambda e: e.matmul(ps[7].ap, g.ones_bf.ap, sqb.ap, start=True, stop=True))
                fw.op("act", [ps[7], g.eps_t], [rs], lambda e: e.activation(
                    out=rs.ap, in_=ps[7].ap, func=AF.Sqrt, bias=g.eps_t.ap[:, 0:1], scale=1.0 / 128))
                fw.op("dve", [rs], [rs], lambda e: e.reciprocal(out=rs.ap, in_=rs.ap))
                fw.op("dve", [Ofm, pv, rs], [t1], lambda e, cs=cs: e.scalar_tensor_tensor(
                    out=t1.ap, in0=Ofm.ap[:, cs], scalar=pv.ap[:, 4:5], in1=rs.ap, op0=ALU.mult, op1=ALU.mult))
                o = ob[no % 2]
                fw.op("dve", [t1, gr], [o], lambda e, o=o, gr=gr: e.tensor_tensor(out=o.ap, in0=t1.ap, in1=gr.ap, op=ALU.mult))
                fw.dma("sp", f"g_o{no % 2}", DT["br"], o, out_ap=D_["br"][1, hd][:, cs])
                no += 1
        fw.barrier()


T5 = 128


def s5_phase(g, l):
    nc, fw = g.nc, g.fw
    D_, DT = g.D, g.DT
    ps = g.ps
    Wglu = g.W[("glu", l)][1]
    with contextlib.ExitStack() as st:
        sb = lambda n, s, d: g.sb(n, s, d, st)
        pv = sb("s_pv", [128, 56], F32)
        sm = sb("s_sm", [128, 16, 12], F32)
        smi = sb("s_smi", [128, 16], I32)
        BTf = sb("s_BTf", [128, 16, 128], F32)
        BTr = sb("s_BTr", [128, 16, 128], BF16)
        BTi = sb("s_BTi", [128, 16, 128], BF16)
        CTr = sb("s_CTr", [128, 16, 128], BF16)
        CTi = sb("s_CTi", [128, 16, 128], BF16)
        wgl = sb("s_wgl", [128, 4, 4, 128], BF16)
        Cj = sb("s_Cj", [128, 16, T5], F32)
        Sj = sb("s_Sj", [128, 16, T5], F32)
        Er = sb("s_Er", [128, 16, T5], F32)
        Ei = sb("s_Ei", [128, 16, T5], F32)
        magT = sb("s_mag", [128, 16, T5], F32)
        tki = sb("s_tki", [128, 16, T5], I32)
        uf = [sb(f"s_uf{i}", [128, 4, T5], F32) for i in range(2)]
        ub = sb("s_ub", [128, 4, T5], BF16)
        xr = sb("s_xr", [128, 16, T5], F32)
        xi = sb("s_xi", [128, 16, T5], F32)
        gr_ = sb("s_gr", [128, 16, T5], F32)
        gi_ = sb("s_gi", [128, 16, T5], F32)
        ta = sb("s_ta", [128, 4, T5], F32)
        tb = sb("s_tb", [128, 4, T5], F32)
        hr = sb("s_hr", [128, 16, T5], BF16)
        hi = sb("s_hi", [128, 16, T5], BF16)
        init = sb("s_init", [128, 16, 4], F32)
        ygf = sb("s_ygf", [128, 4, T5], F32)
        ygb = sb("s_ygb", [128, 4, T5], BF16)
        sg = sb("s_sg", [128, T5], F32)
        brc = sb("s_brc", [128, 4, S], BF16)
        fw.dma("sp", "s_l0", pv, g.IT["pvec"], in_ap=g.I["pvec"][l][:, 211:267])
        for i, Wt in enumerate(Wglu):
            fw.dma("sp", "s_l1", wgl, Wt, out_ap=wgl.ap[:, i])
        for (src, idx, dst, scl) in (("BT", 0, BTr, 1.0), ("BT", 1, BTi, 1.0), ("CT", 0, CTr, 1.0), ("CT", 1, CTi, -1.0)):
            fw.dma("sp", "s_l2", BTf, g.IT[src], in_ap=g.I[src][l, idx])
            fw.op("act", [BTf], [dst], lambda e, dst=dst, scl=scl: e.activation(out=dst.ap, in_=BTf.ap, func=AF.Copy, scale=scl))
        col = lambda i: sm.ap[:, :, i]
        are, aim, ldt = pv.ap[:, 0:16], pv.ap[:, 16:32], pv.ap[:, 32:48]
        DTc, MAG, TH, ABR, ABI, ZR, ZI, RR, RI, TMP, TMP2, TMP3 = (col(i) for i in range(12))
        smT = sm

        def sop(fn, reads=(), eng="dve"):
            fw.op(eng, [smT, pv] + list(reads), [smT], fn)

        sop(lambda e: e.activation(out=DTc, in_=ldt, func=AF.Exp), eng="act")
        sop(lambda e: e.tensor_tensor(out=TMP, in0=DTc, in1=are, op=ALU.mult))
        sop(lambda e: e.activation(out=MAG, in_=TMP, func=AF.Exp), eng="act")
        sop(lambda e: e.tensor_tensor(out=TH, in0=DTc, in1=aim, op=ALU.mult))

        def small_sin(x_ap, out_ap, shift):
            sop(lambda e: e.tensor_scalar(out=TMP, in0=x_ap, scalar1=shift, scalar2=1.0 / TWO_PI, op0=ALU.add, op1=ALU.mult))
            fw.op("dve", [smT], [smi], lambda e: e.tensor_copy(out=smi.ap, in_=TMP))
            fw.op("dve", [smi], [smT], lambda e: e.tensor_copy(out=TMP2, in_=smi.ap))
            sop(lambda e: e.tensor_scalar(out=TMP, in0=x_ap, scalar1=shift, scalar2=None, op0=ALU.add))
            sop(lambda e: e.scalar_tensor_tensor(out=TMP, in0=TMP2, scalar=-TWO_PI, in1=TMP, op0=ALU.mult, op1=ALU.add))
            sop(lambda e: e.tensor_scalar(out=TMP2, in0=TMP, scalar1=PI, scalar2=-TWO_PI, op0=ALU.is_gt, op1=ALU.mult))
            sop(lambda e: e.tensor_tensor(out=TMP, in0=TMP, in1=TMP2, op=ALU.add))
            sop(lambda e: e.tensor_scalar(out=TMP2, in0=TMP, scalar1=-PI, scalar2=TWO_PI, op0=ALU.is_lt, op1=ALU.mult))
            sop(lambda e: e.tensor_tensor(out=TMP, in0=TMP, in1=TMP2, op=ALU.add))
            sop(lambda e: e.activation(out=out_ap, in_=TMP, func=AF.Sin), eng="act")

        small_sin(TH, ABI, 0.0)
        small_sin(TH, ABR, PI / 2)
        sop(lambda e: e.tensor_tensor(out=ABI, in0=ABI, in1=MAG, op=ALU.mult))
        sop(lambda e: e.tensor_tensor(out=ABR, in0=ABR, in1=MAG, op=ALU.mult))
        sop(lambda e: e.tensor_tensor(out=TMP, in0=are, in1=are, op=ALU.mult))
        sop(lambda e: e.tensor_tensor(out=TMP2, in0=aim, in1=aim, op=ALU.mult))
        sop(lambda e: e.tensor_tensor(out=TMP, in0=TMP, in1=TMP2, op=ALU.add))
        sop(lambda e: e.reciprocal(out=TMP, in_=TMP))
        sop(lambda e: e.tensor_scalar(out=TMP3, in0=ABR, scalar1=-1.0, scalar2=None, op0=ALU.add))
        sop(lambda e: e.tensor_tensor(out=ZR, in0=TMP3, in1=are, op=ALU.mult))
        sop(lambda e: e.tensor_tensor(out=TMP2, in0=ABI, in1=aim, op=ALU.mult))
        sop(lambda e: e.tensor_tensor(out=ZR, in0=ZR, in1=TMP2, op=ALU.add))
        sop(lambda e: e.tensor_tensor(out=ZR, in0=ZR, in1=TMP, op=ALU.mult))
        sop(lambda e: e.tensor_tensor(out=ZI, in0=ABI, in1=are, op=ALU.mult))
        sop(lambda e: e.tensor_tensor(out=TMP2, in0=TMP3, in1=aim, op=ALU.mult))
        sop(lambda e: e.tensor_tensor(out=ZI, in0=ZI, in1=TMP2, op=ALU.subtract))
        sop(lambda e: e.tensor_tensor(out=ZI, in0=ZI, in1=TMP, op=ALU.mult))
        sop(lambda e: e.tensor_scalar(out=TMP3, in0=TH, scalar1=float(T5), scalar2=None, op0=ALU.mult))
        small_sin(TMP3, RI, 0.0)
        sop(lambda e: e.tensor_scalar(out=TMP3, in0=TH, scalar1=float(T5), scalar2=None, op0=ALU.mult))
        small_sin(TMP3, RR, PI / 2)
        for sc in range(16):
            fw.op("dve", [smT, g.jrow], [xr], lambda e, sc=sc: e.tensor_scalar(
                out=xr.ap[:, sc, :], in0=g.jrow.ap, scalar1=sm.ap[:, sc, 2:3], scalar2=None, op0=ALU.mult))
            fw.op("dve", [smT, g.onef], [magT], lambda e, sc=sc: e.tensor_scalar(
                out=magT.ap[:, sc, :], in0=g.onef.ap[:, 0:T5], scalar1=sm.ap[:, sc, 1:2], scalar2=None, op0=ALU.mult))
        sin_reduced(fw, tki, xi, xr, Sj)
        fw.op("dve", [xr], [xr], lambda e: e.tensor_scalar(out=xr.ap, in0=xr.ap, scalar1=PI / 2, scalar2=None, op0=ALU.add))
        sin_reduced(fw, tki, xi, xr, Cj)
        for sc in range(16):
            zr, zi = sm.ap[:, sc, 5:6], sm.ap[:, sc, 6:7]
            fw.op("dve", [smT, Cj], [Er], lambda e, sc=sc, zr=zr: e.tensor_scalar(out=Er.ap[:, sc, :], in0=Cj.ap[:, sc, :], scalar1=zr, scalar2=None, op0=ALU.mult))
            fw.op("dve", [smT, Sj, Er], [Er], lambda e, sc=sc, zi=zi: e.scalar_tensor_tensor(
                out=Er.ap[:, sc, :], in0=Sj.ap[:, sc, :], scalar=zi, in1=Er.ap[:, sc, :], op0=ALU.mult, op1=ALU.add))
            fw.op("dve", [smT, Cj], [Ei], lambda e, sc=sc, zi=zi: e.tensor_scalar(out=Ei.ap[:, sc, :], in0=Cj.ap[:, sc, :], scalar1=zi, scalar2=None, op0=ALU.mult))
            fw.op("dve", [smT, Sj], [xr], lambda e, sc=sc, zr=zr: e.tensor_scalar(out=xr.ap[:, sc, :], in0=Sj.ap[:, sc, :], scalar1=zr, scalar2=None, op0=ALU.mult))
        fw.op("dve", [Ei, xr], [Ei], lambda e: e.tensor_tensor(out=Ei.ap, in0=Ei.ap, in1=xr.ap, op=ALU.subtract))
        fw.op("dve", [], [init], lambda e: e.memset(init.ap, 0.0))
        ntile = S // T5
        for tl in range(ntile):
            ts_ = slice(tl * T5, (tl + 1) * T5)
            u = uf[tl % 2]
            fw.dma("sp", f"s_u{tl % 2}", u, DT["su"], in_ap=D_["su"][:, :, ts_].rearrange("c p t -> p c t"))
            fw.op("act", [u], [ub], lambda e, u=u: e.activation(out=ub.ap, in_=u.ap, func=AF.Copy))
            for g4 in range(4):
                pr_, pi_ = ps[(2 * g4) % 4], ps[(2 * g4 + 1) % 4]
                for s4 in range(4):
                    sc = 4 * g4 + s4
                    fw.op("pe", [BTr, ub], [pr_], lambda e, pr_=pr_, sc=sc, s4=s4: e.matmul(
                        pr_.ap[:, s4 * T5:(s4 + 1) * T5], BTr.ap[:, sc, :], ub.ap[:, sc // 4, :], start=True, stop=True))
                    fw.op("pe", [BTi, ub], [pi_], lambda e, pi_=pi_, sc=sc, s4=s4: e.matmul(
                        pi_.ap[:, s4 * T5:(s4 + 1) * T5], BTi.ap[:, sc, :], ub.ap[:, sc // 4, :], start=True, stop=True))
                scs = slice(4 * g4, 4 * g4 + 4)
                v3 = lambda t: t.ap.rearrange("p (a b) -> p a b", b=T5)
                fw.op("dve", [Er, pr_], [ta], lambda e, pr_=pr_, scs=scs: e.tensor_tensor(out=ta.ap, in0=Er.ap[:, scs, :], in1=v3(pr_), op=ALU.mult))
                fw.op("dve", [Ei, pi_], [tb], lambda e, pi_=pi_, scs=scs: e.tensor_tensor(out=tb.ap, in0=Ei.ap[:, scs, :], in1=v3(pi_), op=ALU.mult))
                fw.op("dve", [ta, tb], [xr], lambda e, scs=scs: e.tensor_tensor(out=xr.ap[:, scs, :], in0=ta.ap, in1=tb.ap, op=ALU.subtract))
                fw.op("dve", [Er, pi_], [ta], lambda e, pi_=pi_, scs=scs: e.tensor_tensor(out=ta.ap, in0=Er.ap[:, scs, :], in1=v3(pi_), op=ALU.mult))
                fw.op("dve", [Ei, pr_], [tb], lambda e, pr_=pr_, scs=scs: e.tensor_tensor(out=tb.ap, in0=Ei.ap[:, scs, :], in1=v3(pr_), op=ALU.mult))
                fw.op("dve", [ta, tb], [xi], lambda e, scs=scs: e.tensor_tensor(out=xi.ap[:, scs, :], in0=ta.ap, in1=tb.ap, op=ALU.add))
            for sc in range(16):
                fw.op("dve", [magT, xr, init], [gr_], lambda e, sc=sc: e.tensor_tensor_scan(
                    out=gr_.ap[:, sc, :], data0=magT.ap[:, sc, :], data1=xr.ap[:, sc, :], initial=init.ap[:, sc, 0:1], op0=ALU.mult, op1=ALU.add))
                fw.op("dve", [magT, xi, init], [gi_], lambda e, sc=sc: e.tensor_tensor_scan(
                    out=gi_.ap[:, sc, :], data0=magT.ap[:, sc, :], data1=xi.ap[:, sc, :], initial=init.ap[:, sc, 1:2], op0=ALU.mult, op1=ALU.add))
            glr_, gli_ = gr_.ap[:, :, T5 - 1], gi_.ap[:, :, T5 - 1]
            i0, i1, i2, i3 = (init.ap[:, :, j] for j in range(4))
            fw.op("dve", [gr_, smT], [init], lambda e: e.tensor_tensor(out=i2, in0=glr_, in1=RR, op=ALU.mult))
            fw.op("dve", [gi_, smT], [init], lambda e: e.tensor_tensor(out=i3, in0=gli_, in1=RI, op=ALU.mult))
            fw.op("dve", [init], [init], lambda e: e.tensor_tensor(out=i0, in0=i2, in1=i3, op=ALU.subtract))
            fw.op("dve", [gi_, smT], [init], lambda e: e.tensor_tensor(out=i2, in0=gli_, in1=RR, op=ALU.mult))
            fw.op("dve", [gr_, smT], [init], lambda e: e.tensor_tensor(out=i3, in0=glr_, in1=RI, op=ALU.mult))
            fw.op("dve", [init], [init], lambda e: e.tensor_tensor(out=i1, in0=i2, in1=i3, op=ALU.add))
            for g4 in range(4):
                scs = slice(4 * g4, 4 * g4 + 4)
                fw.op("dve", [Cj, gr_], [ta], lambda e, scs=scs: e.tensor_tensor(out=ta.ap, in0=Cj.ap[:, scs, :], in1=gr_.ap[:, scs, :], op=ALU.mult))
                fw.op("dve", [Sj, gi_], [tb], lambda e, scs=scs: e.tensor_tensor(out=tb.ap, in0=Sj.ap[:, scs, :], in1=gi_.ap[:, scs, :], op=ALU.mult))
                fw.op("dve", [ta, tb], [hr], lambda e, scs=scs: e.tensor_tensor(out=hr.ap[:, scs, :], in0=ta.ap, in1=tb.ap, op=ALU.subtract))
                fw.op("dve", [Cj, gi_], [ta], lambda e, scs=scs: e.tensor_tensor(out=ta.ap, in0=Cj.ap[:, scs, :], in1=gi_.ap[:, scs, :], op=ALU.mult))
                fw.op("dve", [Sj, gr_], [tb], lambda e, scs=scs: e.tensor_tensor(out=tb.ap, in0=Sj.ap[:, scs, :], in1=gr_.ap[:, scs, :], op=ALU.mult))
                fw.op("dve", [ta, tb], [hi], lambda e, scs=scs: e.tensor_tensor(out=hi.ap[:, scs, :], in0=ta.ap, in1=tb.ap, op=ALU.add))
            for oc in range(4):
                Y = ps[4 + oc % 2]
                for s4 in range(4):
                    sc = 4 * oc + s4
                    fw.op("pe", [CTr, hr], [Y], lambda e, Y=Y, sc=sc, s4=s4: e.matmul(
                        Y.ap[:, 0:T5], CTr.ap[:, sc, :], hr.ap[:, sc, :], start=(s4 == 0), stop=False))
                    fw.op("pe", [CTi, hi], [Y], lambda e, Y=Y, sc=sc, s4=s4: e.matmul(
                        Y.ap[:, 0:T5], CTi.ap[:, sc, :], hi.ap[:, sc, :], start=False, stop=(s4 == 3)))
                fw.op("dve", [u, pv, Y], [ygf], lambda e, Y=Y, oc=oc, u=u: e.scalar_tensor_tensor(
                    out=ygf.ap[:, oc, :], in0=u.ap[:, oc, :], scalar=pv.ap[:, 48 + oc:49 + oc], in1=Y.ap[:, 0:T5], op0=ALU.mult, op1=ALU.add))
            fw.op("act", [ygf], [ygf], lambda e: e.activation(out=ygf.ap, in_=ygf.ap, func=AF.Gelu))
            fw.op("act", [ygf], [ygb], lambda e: e.activation(out=ygb.ap, in_=ygf.ap, func=AF.Copy))
            for oc2 in range(4):
                Z = ps[6 + oc2 % 2]
                for kc in range(4):
                    fw.op("pe", [wgl, ygb], [Z], lambda e, Z=Z, oc2=oc2, kc=kc: e.matmul(
                        Z.ap[:, 0:T5], wgl.ap[:, oc2, kc, :], ygb.ap[:, kc, :], start=(kc == 0), stop=(kc == 3)))
                fw.op("act", [Z, pv], [sg], lambda e, Z=Z, oc2=oc2: e.activation(
                    out=sg.ap, in_=Z.ap[:, 0:T5], func=AF.Sigmoid, bias=pv.ap[:, 52 + oc2:53 + oc2], scale=1.0))
                fw.op("dve", [ygf, sg], [brc], lambda e, oc2=oc2, ts_=ts_: e.tensor_tensor(
                    out=brc.ap[:, oc2, ts_], in0=ygf.ap[:, oc2, :], in1=sg.ap, op=ALU.mult))
        for oc in range(4):
            fw.dma("sp", "s_o", DT["br"], brc, out_ap=D_["br"][2, oc], in_ap=brc.ap[:, oc, :])
        fw.barrier()


def merge_phase(g, l):
    nc, fw = g.nc, g.fw
    D_, DT = g.D, g.DT
    ps = g.ps
    AB = g.AB[l]
    Wg = [g.W[(f"wg{i}", l)][1] for i in range(4)]
    Wb = [g.W[(f"wb{i}", l)][1] for i in range(4)]
    Wo = g.W[("wo", l)][1]
    with contextlib.ExitStack() as st:
        sb = lambda n, s, d: g.sb(n, s, d, st)
        xt = sb("e_xt", [128, NCH, TT], F32)
        h = sb("e_h", [128, NCH, TT], BF16)
        br = sb("e_br", [128, 16, TT], BF16)
        mg = sb("e_mg", [128, NCH, TT], F32)
        mgb = sb("e_mgb", [128, NCH, TT], BF16)
        wgs = [sb(f"e_wg{i}", [128, NCH, 128], BF16) for i in range(3)]
        wbs = [sb(f"e_wb{i}", [128, 4, 128], BF16) for i in range(3)]
        sgt = [sb(f"e_sg{i}", [128, TT], F32) for i in range(2)]
        tmp = [sb(f"e_tmp{i}", [128, TT], F32) for i in range(2)]
        n = 0
        for tt in range(NTT):
            cs = slice(tt * TT, (tt + 1) * TT)
            fw.dma("sp", "e_xl", xt, g.yT[tt], in_ap=g.y[:, cs].rearrange("(c p) t -> p c t", p=128))
            fw.dma("sp", "e_hl", h, DT["hT"], in_ap=D_["hT"][:, cs].rearrange("(c p) t -> p c t", p=128))
            fw.dma("sp", "e_bl", br, DT["br"], in_ap=D_["br"][:, :, :, cs].rearrange("b c p t -> p (b c) t"))
            for i in range(4):
                for m in range(NCH):
                    wg_, wb_ = wgs[n % 3], wbs[n % 3]
                    fw.dma("sp", f"e_wg{n % 3}", wg_, Wg[i][m])
                    fw.dma("sp", f"e_wb{n % 3}", wb_, Wb[i][m])
                    n += 1
                    pg, pb = ps[(2 * m) % 6], ps[(2 * m + 1) % 6]
                    for k in range(NCH):
                        fw.op("pe", [wg_, h], [pg], lambda e, wg_=wg_, pg=pg, k=k: e.matmul(
                            pg.ap, wg_.ap[:, k, :], h.ap[:, k, :], start=(k == 0), stop=(k == NCH - 1)))
                    for k in range(4):
                        fw.op("pe", [wb_, br], [pb], lambda e, wb_=wb_, pb=pb, k=k, i=i: e.matmul(
                            pb.ap, wb_.ap[:, k, :], br.ap[:, 4 * i + k, :], start=(k == 0), stop=(k == 3)))
                    s_ = sgt[m % 2]
                    fw.op("act", [pg], [s_], lambda e, pg=pg, s_=s_: e.activation(out=s_.ap, in_=pg.ap, func=AF.Sigmoid))
                    if i == 0:
                        fw.op("dve", [s_, pb], [mg], lambda e, s_=s_, pb=pb, m=m: e.tensor_tensor(
                            out=mg.ap[:, m, :], in0=s_.ap, in1=pb.ap, op=ALU.mult))
                    else:
                        t_ = tmp[m % 2]
                        fw.op("dve", [s_, pb], [t_], lambda e, s_=s_, pb=pb, t_=t_: e.tensor_tensor(
                            out=t_.ap, in0=s_.ap, in1=pb.ap, op=ALU.mult))
                        fw.op("dve", [t_, mg], [mg], lambda e, t_=t_, m=m: e.tensor_tensor(
                            out=mg.ap[:, m, :], in0=mg.ap[:, m, :], in1=t_.ap, op=ALU.add))
            for c in range(NCH):
                fw.op("act", [mg], [mgb], lambda e, c=c: e.activation(out=mgb.ap[:, c, :], in_=mg.ap[:, c, :], func=AF.Copy))
            for m in range(NCH):
                wo_ = wgs[n % 3]
                fw.dma("sp", f"e_wg{n % 3}", wo_, Wo[m])
                n += 1
                po = ps[m % 6]
                for k in range(NCH):
                    fw.op("pe", [wo_, mgb], [po], lambda e, wo_=wo_, po=po, k=k: e.matmul(
                        po.ap, wo_.ap[:, k, :], mgb.ap[:, k, :], start=(k == 0), stop=(k == NCH - 1)))
                G = AB.ap[:, 5 * 16 + m:5 * 16 + m + 1]
                fw.op("dve", [po, xt, AB], [xt], lambda e, po=po, m=m, G=G: e.scalar_tensor_tensor(
                    out=xt.ap[:, m, :], in0=po.ap, scalar=G, in1=xt.ap[:, m, :], op0=ALU.mult, op1=ALU.add))
            fw.dma("sp", "e_xs", g.yT[tt], xt, out_ap=g.y[:, cs].rearrange("(c p) t -> p c t", p=128))
        fw.barrier()


def _swap(a, half):
    sh = a.shape
    b = a.reshape(sh[:-1] + (sh[-1] // (2 * half), 2, half))
    return np.ascontiguousarray(b[..., ::-1, :]).reshape(sh)


def pack_shared(inputs):
    f = lambda k: np.asarray(inputs[k], dtype=np.float32)
    out = {}
    for k in ("w_ada", "w_ff1_in", "w_ff1_out", "w_ff2_in", "w_ff2_out", "gla_w_gate", "s5_w_glu",
              "w_branch", "w_gate", "w_out"):
        out[k] = np.ascontiguousarray(f(k))
    w_in = f("w_in")
    seg = lambda o, n: w_in[:, :, o:o + n]
    blocks = []
    aq, ak = seg(O_AQ, 512), seg(O_AK, 512)
    blocks += [aq, _swap(aq, 64), ak, _swap(ak, 64)]
    iq = seg(O_IQ, 512)
    blocks += [iq, _swap(iq, 32)]
    ik = seg(O_IK, 64)
    blocks += [ik, ik, _swap(ik, 32), _swap(ik, 32)]
    blocks += [seg(O_GR, 512), seg(O_SU, 512), seg(O_MQ, 384), seg(O_MKV, 128)]
    kr = seg(O_MKR, 64)
    blocks += [kr, _swap(kr, 32)]
    blocks += [seg(O_GQ, 256), seg(O_GK, 256)]
    z112 = np.zeros((L, D, 112), np.float32)
    blocks += [seg(O_GLR, 16), z112]
    blocks += [seg(O_AV, 512), seg(O_GV, 512)]
    z120 = np.zeros((L, D, 120), np.float32)
    blocks += [seg(O_IW, 8), z120]
    out["w_inx"] = np.ascontiguousarray(np.concatenate(blocks, axis=-1))
    assert out["w_inx"].shape[-1] == NXB * 128
    uq = f("mla_w_uq").reshape(L, 384, 4, 192)
    nope = uq[..., :128].reshape(L, 384, 512)
    rope = uq[..., 128:]
    ropex = np.concatenate([rope, _swap(rope, 32)], axis=-1).reshape(L, 384, 512)
    out["w_uqx"] = np.ascontiguousarray(np.concatenate([nope, ropex], axis=-1))
    ukv = f("mla_w_ukv").reshape(L, 128, 4, 256)
    out["w_ukvx"] = np.ascontiguousarray(np.concatenate(
        [ukv[..., :128].reshape(L, 128, 512), ukv[..., 128:].reshape(L, 128, 512)], axis=-1))
    pv = np.zeros((L, 128, NP), np.float32)
    pc = lambda v: v.reshape(L, -1, 128).transpose(0, 2, 1)
    pv[:, :, 0:144] = pc(f("b_ada"))
    pv[:, :, 144:192] = pc(f("norm_g").reshape(L, 3 * D))
    dq = f("dsa_qk_norm")
    pv[:, :, 192] = dq[:, 0]; pv[:, :, 193] = _swap(dq[:, 0], 64)
    pv[:, :, 194] = dq[:, 1]; pv[:, :, 195] = _swap(dq[:, 1], 64)
    pv[:, :, 196:199] = pc(f("mla_q_norm"))
    pv[:, :, 199] = f("mla_kv_norm")
    mq = f("mla_qk_norm")
    dup = lambda v: np.concatenate([v, v], axis=-1)
    pv[:, :, 200] = mq[:, 0, :128]; pv[:, :, 201] = dup(mq[:, 0, 128:]); pv[:, :, 202] = dup(_swap(mq[:, 0, 128:], 32))
    pv[:, :, 203] = mq[:, 1, :128]; pv[:, :, 204] = dup(mq[:, 1, 128:]); pv[:, :, 205] = dup(_swap(mq[:, 1, 128:], 32))
    bg = f("gla_b_gate").reshape(L, 4, 64)
    for hd in range(4):
        pv[:, :, 206 + hd] = dup(bg[:, hd])
    pv[:, :, 210] = f("gla_out_norm")
    pv[:, :, 211:227] = pc(f("s5_a_re").reshape(L, 2048))
    pv[:, :, 227:243] = pc(f("s5_a_im").reshape(L, 2048))
    pv[:, :, 243:259] = pc(np.repeat(f("s5_log_dt"), 64, axis=-1))
    pv[:, :, 259:263] = pc(f("s5_d"))
    pv[:, :, 263:267] = pc(f("s5_b_glu"))
    out["pvec"] = pv
    BT = np.zeros((L, 2, 128, 16, 128), np.float32)
    CT = np.zeros((L, 2, 128, 16, 128), np.float32)
    for ri, (bn, cn) in enumerate((("s5_b_re", "s5_c_re"), ("s5_b_im", "s5_c_im"))):
        Bm = f(bn).reshape(L, 32, 64, 16)
        Cm = f(cn).reshape(L, 32, 16, 64)
        for sc in range(16):
            for gl in range(2):
                gg = 2 * sc + gl
                r0 = 32 * (sc % 4) + 16 * gl
                BT[:, ri, r0:r0 + 16, sc, 64 * gl:64 * gl + 64] = Bm[:, gg].transpose(0, 2, 1)
                CT[:, ri, 64 * gl:64 * gl + 64, sc, r0:r0 + 16] = Cm[:, gg].transpose(0, 2, 1)
    out["BT"], out["CT"] = BT, CT
    kc = np.zeros((128, 4), np.float32)
    p = np.arange(128)
    kc[:, 0] = np.power(np.float32(10000.0), -(p % 64).astype(np.float32) / 64).astype(np.float32)
    kc[:, 1] = np.where(p < 64, -1.0, 1.0)
    kc[:, 2] = np.power(np.float32(10000.0), -(p % 32).astype(np.float32) / 32).astype(np.float32)
    kc[:, 3] = np.where((p % 64) < 32, -1.0, 1.0)
    out["kconst"] = kc
    return out


def make_in_maps(inputs, cores, shared=None):
    if shared is None:
        shared = pack_shared(inputs)
    x = np.asarray(inputs["x"], dtype=np.float32)
    c = np.asarray(inputs["c"], dtype=np.float32)
    pos = np.asarray(inputs["positions"]).astype(np.int32)
    maps = []
    for b in cores:
        m = dict(shared)
        m["xT"] = np.ascontiguousarray(x[b].T)
        m["condc"] = np.ascontiguousarray(c[b].reshape(16, 128).T)
        m["pos"] = np.ascontiguousarray(pos[b])
        maps.append(m)
    return maps


def kernel(**inputs):
    nc = build()
    in_maps = make_in_maps(inputs, list(range(8)))
    res = run_bass_kernel_spmd(nc, in_maps, core_ids=list(range(8)))
    out = np.stack([np.ascontiguousarray(np.asarray(r["y"]).T) for r in res.results], axis=0)
    return out.astype(np.float32)
```

```python
import contextlib
import math
import numpy as np
import concourse.bass as bass
import concourse.mybir as mybir
from concourse.bass_utils import run_bass_kernel_spmd

F32 = mybir.dt.float32
BF16 = mybir.dt.bfloat16
I32 = mybir.dt.int32
AF = mybir.ActivationFunctionType
ALU = mybir.AluOpType
AX = mybir.AxisListType

D = 2048
S = 4096
L = 4
DFF = 5504
NCH = 16
NJ = 43
TT = 512
NTT = S // TT
EPS = 1e-6
INC = 4760
O_AQ, O_AK, O_AV, O_IQ, O_IK, O_IW = 0, 512, 1024, 1536, 2048, 2112
O_GQ, O_GK, O_GV, O_GLR, O_GR = 2120, 2376, 2632, 3144, 3160
O_SU = 3672
O_MQ, O_MKV, O_MKR = 4184, 4568, 4696
NXB = 53
NP = 272


class T:
    __slots__ = ("ap", "name", "w", "r")

    def __init__(self, ap, name=""):
        self.ap = ap
        self.name = name
        self.w = None
        self.r = {}


class FW:
    def __init__(self, nc):
        self.nc = nc
        self.engs = {}
        self.sems = {}
        self.tot = {}
        self.seen = {}
        self.isdma = {}
        self.ninst = 0
        self.nwait = 0
        for name, e in (("pe", nc.tensor), ("dve", nc.vector), ("act", nc.scalar),
                        ("pool", nc.gpsimd), ("sp", nc.sync)):
            self.engs[name] = e
            self.seen[name] = {}
            self._mksem(name, False)

    def _mksem(self, key, isdma):
        self.sems[key] = self.nc.alloc_semaphore("s_" + key)
        self.tot[key] = 0
        self.isdma[key] = isdma

    def _wait(self, en, key, val):
        if self.isdma[key]:
            val = self.tot[key]
        if self.seen[en].get(key, 0) >= val:
            return
        self.engs[en].wait_ge(self.sems[key], val)
        self.seen[en][key] = val
        self.nwait += 1

    def _deps(self, en, reads, writes, own):
        for t in reads:
            if t.w is not None:
                self._wait(en, *t.w)
        for t in writes:
            if t.w is not None and t.w[0] != own:
                self._wait(en, *t.w)
            for k, v in t.r.items():
                if k != own:
                    self._wait(en, k, v)

    def op(self, en, reads, writes, make):
        self._deps(en, reads, writes, en)
        ins = make(self.engs[en])
        self.tot[en] += 1
        ins.then_inc(self.sems[en], 1)
        v = self.tot[en]
        for t in reads:
            t.r[en] = v
        for t in writes:
            t.w = (en, v)
            t.r = {}
        self.ninst += 1

    def dma(self, en, key, out_t, in_t, out_ap=None, in_ap=None, **kw):
        if key not in self.sems:
            self._mksem(key, True)
        self._deps(en, [in_t], [out_t], None)
        ins = self.engs[en].dma_start(out=out_ap if out_ap is not None else out_t.ap,
                                      in_=in_ap if in_ap is not None else in_t.ap, **kw)
        self.tot[key] += 16
        ins.then_inc(self.sems[key], 16)
        v = self.tot[key]
        in_t.r[key] = v
        out_t.w = (key, v)
        out_t.r = {}
        self.ninst += 1
        return (key, v)

    def barrier(self):
        for en in self.engs:
            if en == "pool":
                continue
            for key in self.sems:
                if key == "pool" or key.startswith("cv"):
                    continue
                if key != en and self.tot[key] > 0:
                    self._wait(en, key, self.tot[key])


class Ctx:
    pass


def build(n_layers=L, stop=None, dump=None, stopsub=None):
    nc = bass.Bass("TRN2", target_bir_lowering=False)
    fw = FW(nc)
    g = Ctx()
    g.nc, g.fw = nc, fw
    g.stopsub = stopsub

    def din(name, shape, dt=F32):
        return nc.dram_tensor(name, list(shape), dt, kind="ExternalInput").ap()

    def dscr(name, shape, dt):
        return nc.dram_tensor(name, list(shape), dt, kind="Internal").ap()

    I = {}
    I["xT"] = din("xT", [D, S])
    I["condc"] = din("condc", [128, 16])
    I["pos"] = din("pos", [S], I32)
    I["kconst"] = din("kconst", [128, 4])
    I["pvec"] = din("pvec", [L, 128, NP])
    I["w_ada"] = din("w_ada", [L, D, 9 * D])
    I["w_ff1_in"] = din("w_ff1_in", [L, D, 2 * DFF])
    I["w_ff1_out"] = din("w_ff1_out", [L, DFF, D])
    I["w_ff2_in"] = din("w_ff2_in", [L, D, 2 * DFF])
    I["w_ff2_out"] = din("w_ff2_out", [L, DFF, D])
    I["w_inx"] = din("w_inx", [L, D, NXB * 128])
    I["w_uqx"] = din("w_uqx", [L, 384, 1024])
    I["w_ukvx"] = din("w_ukvx", [L, 128, 1024])
    I["gla_w_gate"] = din("gla_w_gate", [L, 16, 256])
    I["s5_w_glu"] = din("s5_w_glu", [L, 512, 512])
    I["BT"] = din("BT", [L, 2, 128, 16, 128])
    I["CT"] = din("CT", [L, 2, 128, 16, 128])
    I["w_branch"] = din("w_branch", [L, 4, 512, D])
    I["w_gate"] = din("w_gate", [L, 4, D, D])
    I["w_out"] = din("w_out", [L, D, D])
    g.I = I
    g.IT = {k: T(v, k) for k, v in I.items()}
    y = nc.dram_tensor("y", [D, S], F32, kind="ExternalOutput").ap()
    g.y = y
    g.yT = [T(y[:, t * TT:(t + 1) * TT], f"y{t}") for t in range(NTT)]
    g.dump = dump
    if dump is not None:
        g.dbg = nc.dram_tensor("dbg", list(dump[1]), dump[2], kind="ExternalOutput").ap()
        g.dbgT = T(g.dbg, "dbg")

    with contextlib.ExitStack() as st:
        uid = [0]

        def sb(name, shape, dt, stack=st):
            uid[0] += 1
            return T(stack.enter_context(nc.sbuf_tensor(f"{name}_{uid[0]}", list(shape), dt))[:], name)

        g.sb = sb
        g.ps = [T(st.enter_context(nc.psum_tensor(f"ps{i}", [128, 512], F32))[:], f"ps{i}") for i in range(8)]
        setup_consts(g, st)
        setup_masks(g)
        g.cv_f = [sb(f"cvf{i}", [128, 8, 256], F32) for i in range(2)]
        g.cv_b = [sb(f"cvb{i}", [128, 8, 256], BF16) for i in range(2)]
        g.cvn = 0
        g.W = {}
        for l in range(n_layers):
            declare_weights(g, l)
        declare_scratch(g)
        compute_mod(g, n_layers)
        rope_tables(g)
        convert_ffn(g, 0, 1)
        for l in range(n_layers):
            convert_mixer(g, l)
            ffn_phase(g, l, 1, first=(l == 0))
            if stop == ("ffn1", l):
                break
            convert_ffn(g, l, 2)
            mixer_phase(g, l)
            if stop == ("mix", l):
                break
            if l + 1 < n_layers:
                convert_ffn(g, l + 1, 1)
            ffn_phase(g, l, 2, first=False)
        if dump is not None:
            fw.barrier()
            fw.dma("sp", "dump", g.dbgT, g.DT[dump[0]], in_ap=dump[3](g.D[dump[0]]))
        fw.barrier()
    return nc


def setup_consts(g, st):
    nc, fw, sb = g.nc, g.fw, g.sb
    g.ones_bf = sb("ones_bf", [128, 128], BF16)
    fw.op("dve", [], [g.ones_bf], lambda e: e.memset(g.ones_bf.ap, 1.0))
    g.modT = [sb(f"modT{l}", [128, 144], F32) for l in range(L)]
    g.ngT = [sb(f"ngT{l}", [128, 48], F32) for l in range(L)]
    g.AB = [sb(f"AB{l}", [128, 9 * 16], F32) for l in range(L)]
    g.eps_t = sb("eps_t", [128, 1], F32)
    fw.op("dve", [], [g.eps_t], lambda e: e.memset(g.eps_t.ap, EPS))


def declare_weights(g, l):
    nc = g.nc

    def scr(name, nblk, nk, cw=128):
        ap = nc.dram_tensor(f"{name}_{l}", [nblk, 128, nk, cw], BF16, kind="Internal").ap()
        g.W[(name, l)] = (ap, [T(ap[b], f"{name}{l}_{b}") for b in range(nblk)])

    scr("ff1a", NJ, NCH)
    scr("ff1b", NJ, NCH)
    scr("ff1o", NCH, NJ)
    scr("ff2a", NJ, NCH)
    scr("ff2b", NJ, NCH)
    scr("ff2o", NCH, NJ)
    scr("winx", 44, NCH)
    scr("winv", 2, NCH, 512)
    scr("winw", 1, NCH)
    scr("uq", 8, 3)
    scr("ukvk", 4, 1)
    scr("ukvv", 1, 1, 512)
    scr("glu", 4, 4)
    for i in range(4):
        scr(f"wg{i}", NCH, NCH)
        scr(f"wb{i}", NCH, 4)
    scr("wo", NCH, NCH)


def convert(g, name, l, src, nk, ncols, col0=0, cw=128):
    fw = g.fw
    dap, dts = g.W[(name, l)]
    srcT = g.IT[src[0]]
    sap = src[1]
    for c0 in range(0, ncols, 256):
        w = min(256, ncols - c0)
        for k0 in range(0, nk, 8):
            kk = min(8, nk - k0)
            i = g.cvn % 2
            g.cvn += 1
            f, b = g.cv_f[i], g.cv_b[i]
            sview = sap[k0 * 128:(k0 + kk) * 128, col0 + c0:col0 + c0 + w].rearrange("(k p) c -> p k c", p=128)
            fw.dma("pool", f"cvl{i}", f, srcT, out_ap=f.ap[:, 0:kk, 0:w], in_ap=sview)
            fw.op("pool", [f], [b], lambda e, f=f, b=b, kk=kk, w=w: e.tensor_copy(out=b.ap[:, 0:kk, 0:w], in_=f.ap[:, 0:kk, 0:w]))
            for s0 in range(0, w, 128):
                col = c0 + s0
                blk, off = col // cw, col % cw
                fw.dma("pool", f"cvs{i}", dts[blk], b, out_ap=dap[blk, :, k0:k0 + kk, off:off + 128],
                       in_ap=b.ap[:, 0:kk, s0:s0 + 128])


def convert_mixer(g, l):
    I = g.I
    convert(g, "winx", l, ("w_inx", I["w_inx"][l]), NCH, 44 * 128, 0)
    convert(g, "winv", l, ("w_inx", I["w_inx"][l]), NCH, 1024, 44 * 128, cw=512)
    convert(g, "winw", l, ("w_inx", I["w_inx"][l]), NCH, 128, 52 * 128)
    convert(g, "uq", l, ("w_uqx", I["w_uqx"][l]), 3, 1024, 0)
    convert(g, "ukvk", l, ("w_ukvx", I["w_ukvx"][l]), 1, 512, 0)
    convert(g, "ukvv", l, ("w_ukvx", I["w_ukvx"][l]), 1, 512, 512, cw=512)
    convert(g, "glu", l, ("s5_w_glu", I["s5_w_glu"][l]), 4, 512, 0)
    for i in range(4):
        convert(g, f"wg{i}", l, ("w_gate", I["w_gate"][l, i]), NCH, D, 0)
        convert(g, f"wb{i}", l, ("w_branch", I["w_branch"][l, i]), 4, D, 0)
    convert(g, "wo", l, ("w_out", I["w_out"][l]), NCH, D, 0)


def convert_ffn(g, l, which):
    wi = g.I[f"w_ff{which}_in"][l]
    wo = g.I[f"w_ff{which}_out"][l]
    convert(g, f"ff{which}a", l, (f"w_ff{which}_in", wi), NCH, DFF, 0)
    convert(g, f"ff{which}b", l, (f"w_ff{which}_in", wi), NCH, DFF, DFF)
    convert(g, f"ff{which}o", l, (f"w_ff{which}_out", wo), NJ, D, 0)


def compute_mod(g, n_layers):
    nc, fw = g.nc, g.fw
    with contextlib.ExitStack() as st:
        sb = lambda n, s, d: g.sb(n, s, d, st)
        cT = sb("cT", [128, 16], F32)
        cond = sb("cond", [128, 16], F32)
        fw.dma("sp", "misc", cT, g.IT["condc"])
        fw.op("act", [cT], [cond], lambda e: e.activation(out=cond.ap, in_=cT.ap, func=AF.Silu))
        slabs = [sb(f"adas{i}", [128, 16, 128], F32) for i in range(4)]
        bT = sb("bT", [128, 144], F32)
        n = 0
        for l in range(n_layers):
            fw.dma("sp", "misc", bT, g.IT["pvec"], in_ap=g.I["pvec"][l][:, 0:144])
            fw.dma("sp", "misc2", g.ngT[l], g.IT["pvec"], in_ap=g.I["pvec"][l][:, 144:192])
            acc = g.ps[0]
            for ch in range(144):
                sl = slabs[n % 4]
                fw.dma("sp", f"ada{n % 4}", sl, g.IT["w_ada"],
                       in_ap=g.I["w_ada"][l][:, ch * 128:(ch + 1) * 128].rearrange("(k p) c -> p k c", p=128))
                n += 1
                for k in range(16):
                    fw.op("pe", [sl, cond], [acc], lambda e, sl=sl, k=k, ch=ch: e.matmul(
                        acc.ap[:, ch:ch + 1], sl.ap[:, k, :], cond.ap[:, k:k + 1], start=(k == 0), stop=(k == 15)))
            m = g.modT[l]
            fw.op("dve", [acc, bT], [m], lambda e, m=m: e.tensor_tensor(out=m.ap, in0=acc.ap[:, 0:144], in1=bT.ap, op=ALU.add))
            AB = g.AB[l]
            ng = g.ngT[l]
            for i in range(3):
                sh = m.ap[:, (3 * i) * 16:(3 * i + 1) * 16]
                sc = m.ap[:, (3 * i + 1) * 16:(3 * i + 2) * 16]
                gt = m.ap[:, (3 * i + 2) * 16:(3 * i + 3) * 16]
                A = AB.ap[:, (3 * i) * 16:(3 * i + 1) * 16]
                Bv = AB.ap[:, (3 * i + 1) * 16:(3 * i + 2) * 16]
                G = AB.ap[:, (3 * i + 2) * 16:(3 * i + 3) * 16]
                gi = ng.ap[:, i * 16:(i + 1) * 16]
                fw.op("dve", [m, ng], [AB], lambda e, A=A, sc=sc, gi=gi: e.scalar_tensor_tensor(
                    out=A, in0=sc, scalar=1.0, in1=gi, op0=ALU.add, op1=ALU.mult))
                fw.op("dve", [m], [AB], lambda e, Bv=Bv, sh=sh: e.tensor_copy(out=Bv, in_=sh))
                fw.op("dve", [m], [AB], lambda e, G=G, gt=gt, i=i: e.tensor_scalar(
                    out=G, in0=gt, scalar1=(1.0 if i == 1 else 0.5), scalar2=None, op0=ALU.mult))
        fw.barrier()


def ada_norm_tile(g, l, i, xt, h, sq, rstd, ps_ss):
    fw = g.fw
    AB = g.AB[l]
    for c in range(NCH):
        fw.op("act", [xt], [sq], lambda e, c=c: e.activation(out=sq.ap[:, c, :], in_=xt.ap[:, c, :], func=AF.Square))
    for c in range(NCH):
        fw.op("pe", [sq, g.ones_bf], [ps_ss], lambda e, c=c: e.matmul(
            ps_ss.ap, g.ones_bf.ap, sq.ap[:, c, :], start=(c == 0), stop=(c == NCH - 1)))
    fw.op("act", [ps_ss, g.eps_t], [rstd], lambda e: e.activation(
        out=rstd.ap, in_=ps_ss.ap, func=AF.Sqrt, bias=g.eps_t.ap[:, 0:1], scale=1.0 / D))
    fw.op("dve", [rstd], [rstd], lambda e: e.reciprocal(out=rstd.ap, in_=rstd.ap))
    for c in range(NCH):
        A = AB.ap[:, (3 * i) * 16 + c:(3 * i) * 16 + c + 1]
        Bv = AB.ap[:, (3 * i + 1) * 16 + c:(3 * i + 1) * 16 + c + 1]
        fw.op("dve", [xt, rstd, AB], [sq], lambda e, c=c, A=A: e.scalar_tensor_tensor(
            out=sq.ap[:, c, :], in0=xt.ap[:, c, :], scalar=A, in1=rstd.ap, op0=ALU.mult, op1=ALU.mult))
        fw.op("act", [sq, AB], [h], lambda e, c=c, Bv=Bv: e.activation(
            out=h.ap[:, c, :], in_=sq.ap[:, c, :], func=AF.Identity, bias=Bv, scale=1.0))


def ffn_phase(g, l, which, first):
    nc, fw = g.nc, g.fw
    i = 0 if which == 1 else 2
    _, wa = g.W[(f"ff{which}a", l)]
    _, wb = g.W[(f"ff{which}b", l)]
    _, wo = g.W[(f"ff{which}o", l)]
    AB = g.AB[l]
    with contextlib.ExitStack() as st:
        sb = lambda n, s, d: g.sb(n, s, d, st)
        xt = sb("f_xt", [128, NCH, TT], F32)
        sq = sb("f_sq", [128, NCH, TT], BF16)
        h = sb("f_h", [128, NCH, TT], BF16)
        u = sb("f_u", [128, NJ, TT], BF16)
        rstd = sb("f_rstd", [128, TT], F32)
        sil = [sb(f"f_sil{k}", [128, TT], F32) for k in range(2)]
        wab = [sb(f"f_wab{k}", [128, 2, NCH, 128], BF16) for k in range(3)]
        wos = [sb(f"f_wo{k}", [128, NJ, 128], BF16) for k in range(2)]
        nwa = 0
        nwo = 0
        for tt in range(NTT):
            src_t = g.IT["xT"] if first else g.yT[tt]
            src_ap = (g.I["xT"] if first else g.y)[:, tt * TT:(tt + 1) * TT].rearrange("(c p) t -> p c t", p=128)
            fw.dma("sp", "f_xl", xt, src_t, in_ap=src_ap)
            ada_norm_tile(g, l, i, xt, h, sq, rstd, g.ps[7])
            for j in range(NJ):
                wt = wab[nwa % 3]
                nwa += 1
                fw.dma("sp", f"f_wa{nwa % 3}", wt, wa[j], out_ap=wt.ap[:, 0])
                fw.dma("sp", f"f_wb{nwa % 3}", wt, wb[j], out_ap=wt.ap[:, 1])
                pa, pb = g.ps[(2 * j) % 6], g.ps[(2 * j + 1) % 6]
                for k in range(NCH):
                    fw.op("pe", [wt, h], [pa], lambda e, wt=wt, k=k, pa=pa: e.matmul(
                        pa.ap, wt.ap[:, 0, k, :], h.ap[:, k, :], start=(k == 0), stop=(k == NCH - 1)))
                for k in range(NCH):
                    fw.op("pe", [wt, h], [pb], lambda e, wt=wt, k=k, pb=pb: e.matmul(
                        pb.ap, wt.ap[:, 1, k, :], h.ap[:, k, :], start=(k == 0), stop=(k == NCH - 1)))
                sl = sil[j % 2]
                fw.op("act", [pa], [sl], lambda e, sl=sl, pa=pa: e.activation(out=sl.ap, in_=pa.ap, func=AF.Silu))
                fw.op("dve", [sl, pb], [u], lambda e, sl=sl, pb=pb, j=j: e.tensor_tensor(
                    out=u.ap[:, j, :], in0=sl.ap, in1=pb.ap, op=ALU.mult))
            for m in range(NCH):
                wt = wos[nwo % 2]
                nwo += 1
                fw.dma("sp", f"f_wo{nwo % 2}", wt, wo[m])
                po = g.ps[m % 6]
                for j in range(NJ):
                    fw.op("pe", [wt, u], [po], lambda e, wt=wt, j=j, po=po: e.matmul(
                        po.ap, wt.ap[:, j, :], u.ap[:, j, :], start=(j == 0), stop=(j == NJ - 1)))
                G = AB.ap[:, (3 * i + 2) * 16 + m:(3 * i + 2) * 16 + m + 1]
                fw.op("dve", [po, xt, AB], [xt], lambda e, po=po, m=m, G=G: e.scalar_tensor_tensor(
                    out=xt.ap[:, m, :], in0=po.ap, scalar=G, in1=xt.ap[:, m, :], op0=ALU.mult, op1=ALU.add))
            fw.dma("sp", "f_xs", g.yT[tt], xt,
                   out_ap=g.y[:, tt * TT:(tt + 1) * TT].rearrange("(c p) t -> p c t", p=128))
        fw.barrier()


def declare_scratch(g):
    nc = g.nc

    def scr(name, shape, dt):
        ap = nc.dram_tensor("sc_" + name, list(shape), dt, kind="Internal").ap()
        g.D[name] = ap
        g.DT[name] = T(ap, name)

    g.D, g.DT = {}, {}
    scr("tab", [4, 128, S], F32)
    scr("hT", [D, S], BF16)
    scr("qA", [4, 128, S], BF16)
    scr("kA", [4, 128, S], BF16)
    scr("vA", [S, 512], BF16)
    scr("iq", [4, 128, S], BF16)
    scr("ik", [128, S], BF16)
    scr("iw", [S, 8], F32)
    scr("gq", [4, 64, S], F32)
    scr("gk", [4, 64, S], F32)
    scr("gv", [S, 512], BF16)
    scr("glr", [16, S], BF16)
    scr("gr", [4, 128, S], BF16)
    scr("su", [4, 128, S], F32)
    scr("qn", [4, 128, S], BF16)
    scr("qr", [4, 64, S], BF16)
    scr("kn", [4, 128, S], BF16)
    scr("mv", [S, 512], BF16)
    scr("kr", [64, S], BF16)
    scr("br", [4, 4, 128, S], BF16)


TWO_PI = float(2 * math.pi)
PI = float(math.pi)


def sin_reduced(fw, ki, kf, x, out):
    fw.op("dve", [x], [kf], lambda e: e.tensor_scalar(out=kf.ap, in0=x.ap, scalar1=1.0 / TWO_PI, scalar2=None, op0=ALU.mult))
    fw.op("dve", [kf], [ki], lambda e: e.tensor_copy(out=ki.ap, in_=kf.ap))
    fw.op("dve", [ki], [kf], lambda e: e.tensor_copy(out=kf.ap, in_=ki.ap))
    fw.op("dve", [kf, x], [out], lambda e: e.scalar_tensor_tensor(out=out.ap, in0=kf.ap, scalar=-TWO_PI, in1=x.ap, op0=ALU.mult, op1=ALU.add))
    fw.op("dve", [out], [kf], lambda e: e.tensor_scalar(out=kf.ap, in0=out.ap, scalar1=PI, scalar2=-TWO_PI, op0=ALU.is_gt, op1=ALU.mult))
    fw.op("dve", [out, kf], [out], lambda e: e.tensor_tensor(out=out.ap, in0=out.ap, in1=kf.ap, op=ALU.add))
    fw.op("dve", [out], [kf], lambda e: e.tensor_scalar(out=kf.ap, in0=out.ap, scalar1=-PI, scalar2=TWO_PI, op0=ALU.is_lt, op1=ALU.mult))
    fw.op("dve", [out, kf], [out], lambda e: e.tensor_tensor(out=out.ap, in0=out.ap, in1=kf.ap, op=ALU.add))
    fw.op("act", [out], [out], lambda e: e.activation(out=out.ap, in_=out.ap, func=AF.Sin))


def rope_tables(g):
    nc, fw = g.nc, g.fw
    with contextlib.ExitStack() as st:
        sb = lambda n, s, d: g.sb(n, s, d, st)
        posi = sb("r_posi", [128, S], I32)
        posf = sb("r_posf", [128, S], F32)
        ang = sb("r_ang", [128, S], F32)
        ki = sb("r_ki", [128, S], I32)
        kf = sb("r_kf", [128, S], F32)
        out = sb("r_out", [128, S], F32)
        kc = sb("r_kc", [128, 4], F32)
        fw.dma("sp", "misc", kc, g.IT["kconst"])
        fw.dma("sp", "misc2", posi, g.IT["pos"], in_ap=g.I["pos"].partition_broadcast(128))
        fw.op("dve", [posi], [posf], lambda e: e.tensor_copy(out=posf.ap, in_=posi.ap))
        for t in range(2):
            invf = kc.ap[:, 2 * t:2 * t + 1]
            sgn = kc.ap[:, 2 * t + 1:2 * t + 2]
            fw.op("dve", [posf, kc], [ang], lambda e, invf=invf: e.tensor_scalar(out=ang.ap, in0=posf.ap, scalar1=invf, scalar2=None, op0=ALU.mult))
            sin_reduced(fw, ki, kf, ang, out)
            fw.op("dve", [out, kc], [out], lambda e, sgn=sgn: e.tensor_scalar(out=out.ap, in0=out.ap, scalar1=sgn, scalar2=None, op0=ALU.mult))
            fw.dma("sp", "r_st", g.DT["tab"], out, out_ap=g.D["tab"][2 * t + 1])
            fw.op("dve", [ang], [ang], lambda e: e.tensor_scalar(out=ang.ap, in0=ang.ap, scalar1=PI / 2, scalar2=None, op0=ALU.add))
            sin_reduced(fw, ki, kf, ang, out)
            fw.dma("sp", "r_st", g.DT["tab"], out, out_ap=g.D["tab"][2 * t])
        fw.barrier()


def setup_masks(g):
    fw = g.fw
    sb = g.sb
    onef = sb("c_onef", [128, 512], F32)
    fw.op("pool", [], [onef], lambda e: e.memset(onef.ap, 1.0))
    g.onef = onef
    zf = sb("c_zf", [128, 128], F32)
    fw.op("pool", [], [zf], lambda e: e.memset(zf.ap, 0.0))
    g.ident = sb("c_ident", [128, 128], BF16)
    fw.op("pool", [onef], [g.ident], lambda e: e.affine_select(
        out=g.ident.ap, in_=onef.ap[:, 0:128], pattern=[[1, 128]], compare_op=ALU.is_equal, fill=0.0, base=0, channel_multiplier=-1))
    g.cmask = []
    for r in range(4):
        m = sb(f"c_cm{r}", [128, 512], BF16)
        fw.op("pool", [onef], [m], lambda e, m=m, r=r: e.affine_select(
            out=m.ap, in_=onef.ap, pattern=[[1, 512]], compare_op=ALU.is_ge, fill=0.0, base=-128 * r, channel_multiplier=-1))
        g.cmask.append(m)
    g.gmask = sb("c_gm", [128, 128], BF16)
    fw.op("pool", [onef], [g.gmask], lambda e: e.affine_select(
        out=g.gmask.ap, in_=onef.ap[:, 0:128], pattern=[[1, 128]], compare_op=ALU.is_ge, fill=0.0, base=0, channel_multiplier=-1))
    fw.op("pool", [], [g.gmask], lambda e: e.memset(g.gmask.ap[0:64, 64:128], 0.0))
    g.dbias = sb("c_db", [128, 128], F32)
    fw.op("pool", [zf], [g.dbias], lambda e: e.affine_select(
        out=g.dbias.ap, in_=zf.ap, pattern=[[-1, 128]], compare_op=ALU.is_ge, fill=-1e30, base=0, channel_multiplier=1))
    g.rmask = sb("c_rm", [64, 512], F32)
    fw.op("pool", [], [g.rmask], lambda e: e.memset(g.rmask.ap, 1.0))
    fw.op("pool", [], [g.rmask], lambda e: e.memset(g.rmask.ap.rearrange("p (c j) -> p c j", j=64)[:, :, 0:1], 0.0))
    g.pw2 = sb("c_pw2", [128, NITER + 2], F32)
    for k in range(NITER + 2):
        fw.op("pool", [], [g.pw2], lambda e, k=k: e.memset(g.pw2.ap[:, k:k + 1], float(2.0 ** (1 - k))))
    g.jrow = sb("c_jrow", [128, 128], F32)
    fw.op("pool", [], [g.jrow], lambda e: e.iota(g.jrow.ap, pattern=[[1, 128]], base=0, channel_multiplier=0,
                                                  allow_small_or_imprecise_dtypes=True))


def mixer_phase(g, l):
    proj_phase(g, l)
    if g.stopsub == "proj":
        return
    mla_phase(g, l)
    if g.stopsub == "mla":
        return
    dsa_phase(g, l)
    if g.stopsub == "dsa":
        return
    gla_phase(g, l)
    if g.stopsub == "gla":
        return
    s5_phase(g, l)
    if g.stopsub == "s5":
        return
    merge_phase(g, l)


def proj_phase(g, l):
    nc, fw = g.nc, g.fw
    D_, DT = g.D, g.DT
    Wx = g.W[("winx", l)][1]
    Wv = g.W[("winv", l)][1]
    Ww = g.W[("winw", l)][1]
    Wuq = g.W[("uq", l)][1]
    Wkk = g.W[("ukvk", l)][1]
    Wkv = g.W[("ukvv", l)][1]
    ps = g.ps
    with contextlib.ExitStack() as st:
        sb = lambda n, s, d: g.sb(n, s, d, st)
        xt = sb("p_xt", [128, NCH, TT], F32)
        sq = sb("p_sq", [128, NCH, TT], BF16)
        h = sb("p_h", [128, NCH, TT], BF16)
        rstd = sb("p_rstd", [128, TT], F32)
        wsl = [sb(f"p_wsl{i}", [128, NCH, 128], BF16) for i in range(3)]
        wvs = [sb(f"p_wv{i}", [128, NCH, 512], BF16) for i in range(2)]
        wws = sb("p_ww", [128, NCH, 128], BF16)
        wuk = sb("p_wuk", [128, 1, 512], BF16)
        tabs = sb("p_tabs", [128, 4, TT], F32)
        t1 = sb("p_t1", [128, TT], F32)
        t2 = sb("p_t2", [128, TT], F32)
        sqb = sb("p_sqb", [128, TT], BF16)
        rs = sb("p_rs", [128, TT], F32)
        obs = [sb(f"p_ob{i}", [128, TT], BF16) for i in range(6)]
        ofs = [sb(f"p_of{i}", [128, TT], F32) for i in range(4)]
        ql = sb("p_ql", [128, 3, TT], BF16)
        kvl = sb("p_kvl", [128, TT], BF16)
        gn = sb("p_gn", [128, 16], F32)
        fw.dma("sp", "misc", gn, g.IT["pvec"], in_ap=g.I["pvec"][l][:, 192:208])
        fw.dma("sp", "misc2", wws, Ww[0])
        fw.dma("sp", "misc3", wuk, Wkv[0])
        cnt = {"w": 0, "ob": 0, "of": 0, "wv": 0}

        pend = []
        sets = [(ps[0], ps[1], ps[6]), (ps[2], ps[3], ps[5])]
        cur = {"s": 0}

        def nxt():
            cur["s"] ^= 1
            return sets[cur["s"]]

        def flush(min_age):
            while pend and pend[0][0] >= min_age:
                pend.pop(0)[1]()

        def defer_dma(key, dstT, o, out_ap=None, in_ap=None):
            pend.append([0, lambda: fw.dma("sp", key, dstT, o, out_ap=out_ap, in_ap=in_ap)])
            while len(pend) > 3:
                pend.pop(0)[1]()

        def slab(blkT, nk=NCH):
            w = wsl[cnt["w"] % 3]
            fw.dma("sp", f"p_w{cnt['w'] % 3}", w, blkT, out_ap=w.ap[:, 0:nk, :])
            cnt["w"] += 1
            for e_ in pend:
                e_[0] += 1
            flush(2)
            return w

        def mm(w, pst, M, rhs_fn, nk=NCH, c0=0):
            for k in range(nk):
                fw.op("pe", [w, h, ql, kvl], [pst], lambda e, k=k: e.matmul(
                    pst.ap[0:M, :], w.ap[:, k, c0:c0 + M], rhs_fn(k), start=(k == 0), stop=(k == nk - 1)))

        rh = lambda k: h.ap[:, k, :]

        def nob():
            o = obs[cnt["ob"] % 6]
            cnt["ob"] += 1
            return o, f"p_ob{cnt['ob'] % 6}"

        def nof():
            o = ofs[cnt["of"] % 4]
            cnt["of"] += 1
            return o, f"p_of{cnt['of'] % 4}"

        def norm_rs(px, M, d, pss=None):
            pss = pss if pss is not None else ps[6]
            fw.op("act", [px], [sqb], lambda e: e.activation(out=sqb.ap[0:M, :], in_=px.ap[0:M, :], func=AF.Square))
            fw.op("pe", [sqb, g.ones_bf], [pss], lambda e: e.matmul(
                pss.ap[0:M, :], g.ones_bf.ap[0:M, 0:M], sqb.ap[0:M, :], start=True, stop=True))
            fw.op("act", [pss, g.eps_t], [rs], lambda e: e.activation(
                out=rs.ap[0:M, :], in_=pss.ap[0:M, :], func=AF.Sqrt, bias=g.eps_t.ap[0:M, 0:1], scale=1.0 / d))
            fw.op("dve", [rs], [rs], lambda e: e.reciprocal(out=rs.ap[0:M, :], in_=rs.ap[0:M, :]))

        def rope_out(px, pxs, M, gcol, ti, norm_d, dst_ap, dstT, pss=None):
            if norm_d:
                norm_rs(px, M, norm_d, pss)
            ga = gn.ap[0:M, gcol:gcol + 1] if gcol is not None else 1.0
            gs = gn.ap[0:M, gcol + 1:gcol + 2] if gcol is not None else 1.0
            fw.op("dve", [px, gn, tabs], [t1], lambda e: e.scalar_tensor_tensor(
                out=t1.ap[0:M, :], in0=px.ap[0:M, :], scalar=ga, in1=tabs.ap[0:M, ti, :], op0=ALU.mult, op1=ALU.mult))
            fw.op("dve", [pxs, gn, tabs], [t2], lambda e: e.scalar_tensor_tensor(
                out=t2.ap[0:M, :], in0=pxs.ap[0:M, :], scalar=gs, in1=tabs.ap[0:M, ti + 1, :], op0=ALU.mult, op1=ALU.mult))
            o, key = nob()
            if norm_d:
                fw.op("dve", [t1, t2], [t1], lambda e: e.tensor_tensor(out=t1.ap[0:M, :], in0=t1.ap[0:M, :], in1=t2.ap[0:M, :], op=ALU.add))
                fw.op("dve", [t1, rs], [o], lambda e: e.tensor_tensor(out=o.ap[0:M, :], in0=t1.ap[0:M, :], in1=rs.ap[0:M, :], op=ALU.mult))
            else:
                fw.op("dve", [t1, t2], [o], lambda e: e.tensor_tensor(out=o.ap[0:M, :], in0=t1.ap[0:M, :], in1=t2.ap[0:M, :], op=ALU.add))
            defer_dma(key, dstT, o, out_ap=dst_ap, in_ap=o.ap[0:M, :])

        def norm_out(px, M, gcol, d, dst_ap, dstT, keep=None):
            norm_rs(px, M, d)
            if keep is not None:
                o, key = keep, None
            else:
                o, key = nob()
            fw.op("dve", [px, gn, rs], [o], lambda e: e.scalar_tensor_tensor(
                out=(o.ap[0:M, :] if keep is None else keep.ap), in0=px.ap[0:M, :], scalar=gn.ap[0:M, gcol:gcol + 1],
                in1=rs.ap[0:M, :], op0=ALU.mult, op1=ALU.mult))
            if keep is None:
                defer_dma(key, dstT, o, out_ap=dst_ap, in_ap=o.ap[0:M, :])

        for tt in range(NTT):
            flush(0)
            cs = slice(tt * TT, (tt + 1) * TT)
            fw.dma("sp", "p_xl", xt, g.yT[tt], in_ap=g.y[:, cs].rearrange("(c p) t -> p c t", p=128))
            fw.dma("sp", "p_tab", tabs, DT["tab"], in_ap=D_["tab"][:, :, cs].rearrange("f p t -> p f t"))
            ada_norm_tile(g, l, 1, xt, h, sq, rstd, ps[7])
            fw.dma("sp", "p_hs", DT["hT"], h, out_ap=D_["hT"][:, cs].rearrange("(c p) t -> p c t", p=128))
            for nm, b0, gcol in (("qA", 0, 0), ("kA", 8, 2)):
                for hd in range(4):
                    pa_, pb_, pc_ = nxt()
                    w = slab(Wx[b0 + hd]); mm(w, pa_, 128, rh)
                    w = slab(Wx[b0 + 4 + hd]); mm(w, pb_, 128, rh)
                    rope_out(pa_, pb_, 128, gcol, 0, 128, D_[nm][hd][:, cs], DT[nm], pc_)
            for c in range(4):
                pa_, pb_, pc_ = nxt()
                w = slab(Wx[16 + c]); mm(w, pa_, 128, rh)
                w = slab(Wx[20 + c]); mm(w, pb_, 128, rh)
                rope_out(pa_, pb_, 128, None, 2, 0, D_["iq"][c][:, cs], DT["iq"], pc_)
            pa_, pb_, pc_ = nxt()
            w = slab(Wx[24]); mm(w, pa_, 128, rh)
            w = slab(Wx[25]); mm(w, pb_, 128, rh)
            rope_out(pa_, pb_, 128, None, 2, 0, D_["ik"][:, cs], DT["ik"], pc_)
            for c in range(4):
                w = slab(Wx[26 + c]); mm(w, ps[2 + c % 2], 128, rh)
                o, key = nob()
                fw.op("act", [ps[2 + c % 2]], [o], lambda e, o=o, c=c: e.activation(out=o.ap, in_=ps[2 + c % 2].ap, func=AF.Silu))
                defer_dma(key, DT["gr"], o, out_ap=D_["gr"][c][:, cs])
            for c in range(4):
                w = slab(Wx[30 + c]); mm(w, ps[2 + c % 2], 128, rh)
                o, key = nof()
                fw.op("act", [ps[2 + c % 2]], [o], lambda e, o=o, c=c: e.activation(out=o.ap, in_=ps[2 + c % 2].ap, func=AF.Copy))
                defer_dma(key, DT["su"], o, out_ap=D_["su"][c][:, cs])
            for nm, b0, scl in (("gq", 39, 0.125), ("gk", 41, 1.0)):
                for hd in range(4):
                    if hd % 2 == 0:
                        w = slab(Wx[b0 + hd // 2])
                    pp = ps[2 + hd % 2]
                    mm(w, pp, 64, rh, c0=(hd % 2) * 64)
                    o, key = nof()
                    fw.op("act", [pp], [o], lambda e, o=o, pp=pp, scl=scl: e.activation(
                        out=o.ap[0:64, :], in_=pp.ap[0:64, :], func=AF.Copy, scale=scl))
                    defer_dma(key, DT[nm], o, out_ap=D_[nm][hd][:, cs], in_ap=o.ap[0:64, :])
            w = slab(Wx[43]); mm(w, ps[2], 16, rh)
            o, key = nob()
            fw.op("act", [ps[2]], [o], lambda e, o=o: e.activation(out=o.ap[0:16, :], in_=ps[2].ap[0:16, :], func=AF.Copy))
            defer_dma(key, DT["glr"], o, out_ap=D_["glr"][:, cs], in_ap=o.ap[0:16, :])
            for vi, nm in ((0, "vA"), (1, "gv")):
                wv = wvs[cnt["wv"] % 2]
                fw.dma("sp", f"p_wv{cnt['wv'] % 2}", wv, Wv[vi])
                cnt["wv"] += 1
                for sub in range(4):
                    pp = ps[2 + sub % 2]
                    for k in range(NCH):
                        fw.op("pe", [wv, h], [pp], lambda e, k=k, pp=pp, sub=sub, wv=wv: e.matmul(
                            pp.ap, h.ap[:, k, sub * 128:(sub + 1) * 128], wv.ap[:, k, :], start=(k == 0), stop=(k == NCH - 1)))
                    o, key = nob()
                    fw.op("act", [pp], [o], lambda e, o=o, pp=pp: e.activation(out=o.ap, in_=pp.ap, func=AF.Copy))
                    defer_dma(key, DT[nm], o, out_ap=D_[nm][tt * TT + sub * 128:tt * TT + (sub + 1) * 128, :])
            for sub in range(4):
                pp = ps[2 + sub % 2]
                for k in range(NCH):
                    fw.op("pe", [wws, h], [pp], lambda e, k=k, pp=pp, sub=sub: e.matmul(
                        pp.ap[:, 0:8], h.ap[:, k, sub * 128:(sub + 1) * 128], wws.ap[:, k, 0:8], start=(k == 0), stop=(k == NCH - 1)))
                o, key = nof()
                fw.op("act", [pp], [o], lambda e, o=o, pp=pp: e.activation(
                    out=o.ap[:, 0:8], in_=pp.ap[:, 0:8], func=AF.Copy, scale=float(8 ** -0.5 * 64 ** -0.5)))
                defer_dma(key, DT["iw"], o, out_ap=D_["iw"][tt * TT + sub * 128:tt * TT + (sub + 1) * 128, :], in_ap=o.ap[:, 0:8])
            for c in range(3):
                w = slab(Wx[34 + c]); mm(w, ps[c], 128, rh)
            for c in range(3):
                fw.op("act", [ps[c]], [sqb], lambda e, c=c: e.activation(out=sqb.ap, in_=ps[c].ap, func=AF.Square))
                fw.op("pe", [sqb, g.ones_bf], [ps[6]], lambda e, c=c: e.matmul(
                    ps[6].ap, g.ones_bf.ap, sqb.ap, start=(c == 0), stop=(c == 2)))
            fw.op("act", [ps[6], g.eps_t], [rs], lambda e: e.activation(
                out=rs.ap, in_=ps[6].ap, func=AF.Sqrt, bias=g.eps_t.ap[:, 0:1], scale=1.0 / 384))
            fw.op("dve", [rs], [rs], lambda e: e.reciprocal(out=rs.ap, in_=rs.ap))
            for c in range(3):
                fw.op("dve", [ps[c], gn, rs], [ql], lambda e, c=c: e.scalar_tensor_tensor(
                    out=ql.ap[:, c, :], in0=ps[c].ap, scalar=gn.ap[:, 4 + c:5 + c], in1=rs.ap, op0=ALU.mult, op1=ALU.mult))
            rq = lambda k: ql.ap[:, k, :]
            for hd in range(4):
                w = slab(Wuq[hd], 3); mm(w, ps[0], 128, rq, nk=3)
                norm_out(ps[0], 128, 8, 128, D_["qn"][hd][:, cs], DT["qn"])
                w = slab(Wuq[4 + hd], 3)
                mm(w, ps[0], 64, rq, nk=3, c0=0)
                mm(w, ps[1], 64, rq, nk=3, c0=64)
                rope_out(ps[0], ps[1], 64, 9, 2, 64, D_["qr"][hd][:, cs], DT["qr"])
            w = slab(Wx[37]); mm(w, ps[0], 128, rh)
            norm_out(ps[0], 128, 7, 128, None, None, keep=kvl)
            rk = lambda k: kvl.ap
            for hd in range(4):
                w = slab(Wkk[hd], 1); mm(w, ps[hd % 2], 128, rk, nk=1)
                norm_out(ps[hd % 2], 128, 11, 128, D_["kn"][hd][:, cs], DT["kn"])
            for sub in range(4):
                pp = ps[2 + sub % 2]
                fw.op("pe", [wuk, kvl], [pp], lambda e, pp=pp, sub=sub: e.matmul(
                    pp.ap, kvl.ap[:, sub * 128:(sub + 1) * 128], wuk.ap[:, 0, :], start=True, stop=True))
                o, key = nob()
                fw.op("act", [pp], [o], lambda e, o=o, pp=pp: e.activation(out=o.ap, in_=pp.ap, func=AF.Copy))
                defer_dma(key, DT["mv"], o, out_ap=D_["mv"][tt * TT + sub * 128:tt * TT + (sub + 1) * 128, :])
            w = slab(Wx[38])
            mm(w, ps[0], 64, rh, c0=0)
            mm(w, ps[1], 64, rh, c0=64)
            rope_out(ps[0], ps[1], 64, 12, 2, 64, D_["kr"][:, cs], DT["kr"])
        flush(0)
        fw.barrier()


def attn_core(g, st, name, qT_loader, kT, kr, v, nq_parts, scale, bri, qt_mask_fn):
    pass


def mla_phase(g, l):
    nc, fw = g.nc, g.fw
    D_, DT = g.D, g.DT
    ps = g.ps
    sc = float(192 ** -0.5)
    with contextlib.ExitStack() as st:
        sb = lambda n, s, d: g.sb(n, s, d, st)
        kn = sb("m_kn", [128, 4, S], BF16)
        kr = sb("m_kr", [64, S], BF16)
        v = sb("m_v", [128, 32, 512], BF16)
        qn = [sb(f"m_qn{i}", [128, 4, TT], BF16) for i in range(2)]
        qr = [sb(f"m_qr{i}", [64, 4, TT], BF16) for i in range(2)]
        pt = [sb(f"m_pt{i}", [128, TT], BF16) for i in range(3)]
        rd = sb("m_rd", [128, TT], F32)
        ob = [sb(f"m_ob{i}", [128, TT], BF16) for i in range(2)]
        fw.dma("sp", "m_l0", kn, DT["kn"], in_ap=D_["kn"].rearrange("h p t -> p h t"))
        fw.dma("sp", "m_l1", kr, DT["kr"])
        fw.dma("sp", "m_l2", v, DT["mv"], in_ap=D_["mv"].rearrange("(k p) c -> p k c", p=128))
        np_ = 0
        no = 0
        for qt in range(NTT):
            cs = slice(qt * TT, (qt + 1) * TT)
            q1, q2 = qn[qt % 2], qr[qt % 2]
            fw.dma("sp", f"m_q{qt % 2}", q1, DT["qn"], in_ap=D_["qn"][:, :, cs].rearrange("h p t -> p h t"))
            fw.dma("sp", f"m_r{qt % 2}", q2, DT["qr"], in_ap=D_["qr"][:, :, cs].rearrange("h p t -> p h t"))
            nkt = 4 * qt + 4
            for hd in range(4):
                O, Dn = ps[2 + hd % 2], ps[4 + hd % 2]
                for kt in range(nkt):
                    Lp = ps[kt % 2]
                    ks = slice(kt * 128, (kt + 1) * 128)
                    fw.op("pe", [kn, q1], [Lp], lambda e, Lp=Lp, ks=ks, hd=hd, q1=q1: e.matmul(
                        Lp.ap, kn.ap[:, hd, ks], q1.ap[:, hd, :], start=True, stop=False))
                    fw.op("pe", [kr, q2], [Lp], lambda e, Lp=Lp, ks=ks, hd=hd, q2=q2: e.matmul(
                        Lp.ap, kr.ap[:, ks], q2.ap[:, hd, :], start=False, stop=True))
                    P = pt[np_ % 3]
                    np_ += 1
                    fw.op("act", [Lp], [P], lambda e, Lp=Lp, P=P: e.activation(out=P.ap, in_=Lp.ap, func=AF.Exp, scale=sc))
                    if kt >= 4 * qt:
                        cm = g.cmask[kt - 4 * qt]
                        fw.op("dve", [P, cm], [P], lambda e, P=P, cm=cm: e.tensor_tensor(out=P.ap, in0=P.ap, in1=cm.ap, op=ALU.mult))
                    fw.op("pe", [v, P], [O], lambda e, O=O, P=P, kt=kt, hd=hd: e.matmul(
                        O.ap, v.ap[:, kt, hd * 128:(hd + 1) * 128], P.ap, start=(kt == 0), stop=(kt == nkt - 1)))
                    fw.op("pe", [g.ones_bf, P], [Dn], lambda e, Dn=Dn, P=P, kt=kt: e.matmul(
                        Dn.ap, g.ones_bf.ap, P.ap, start=(kt == 0), stop=(kt == nkt - 1)))
                fw.op("dve", [Dn], [rd], lambda e, Dn=Dn: e.reciprocal(out=rd.ap, in_=Dn.ap))
                o = ob[no % 2]
                fw.op("dve", [O, rd], [o], lambda e, O=O, o=o: e.tensor_tensor(out=o.ap, in0=O.ap, in1=rd.ap, op=ALU.mult))
                fw.dma("sp", f"m_o{no % 2}", DT["br"], o, out_ap=D_["br"][3, hd][:, cs])
                no += 1
        fw.barrier()


NITER = 14


def dsa_phase(g, l):
    nc, fw = g.nc, g.fw
    D_, DT = g.D, g.DT
    ps = g.ps
    sc = float(128 ** -0.5)
    with contextlib.ExitStack() as st:
        sb = lambda n, s, d: g.sb(n, s, d, st)
        kA = sb("a_k", [128, 4, S], BF16)
        vA = sb("a_v", [128, 32, 512], BF16)
        ik = sb("a_ik", [128, S], BF16)
        iqs = [sb(f"a_iq{i}", [128, 4, 128], BF16) for i in range(2)]
        maskT = sb("a_mT", [128, 32, TT], BF16)
        score = sb("a_sc", [128, S], F32)
        maskq = sb("a_mq", [128, S], BF16)
        junk = maskq
        rl = [sb(f"a_rl{i}", [128, 512], F32) for i in range(2)]
        iw = [sb(f"a_iw{i}", [128, 8], F32) for i in range(2)]
        sm = sb("a_sm", [128, 8], F32)
        dk = sb("a_dk", [128, NITER + 2], F32)
        qA = [sb(f"a_q{i}", [128, 4, TT], BF16) for i in range(2)]
        pt = [sb(f"a_pt{i}", [128, TT], BF16) for i in range(3)]
        rd = sb("a_rd", [128, TT], F32)
        ob = [sb(f"a_ob{i}", [128, TT], BF16) for i in range(2)]
        fw.dma("sp", "a_l0", kA, DT["kA"], in_ap=D_["kA"].rearrange("h p t -> p h t"))
        fw.dma("sp", "a_l1", vA, DT["vA"], in_ap=D_["vA"].rearrange("(k p) c -> p k c", p=128))
        fw.dma("sp", "a_l2", ik, DT["ik"])
        d0, lo, mid, cn, stp = (sm.ap[:, i:i + 1] for i in range(5))
        np_ = 0
        no = 0
        nr = 0
        ptb = ps[6].ap.bitcast(BF16)
        for qt in range(NTT):
            cs = slice(qt * TT, (qt + 1) * TT)
            q1 = qA[qt % 2]
            fw.dma("sp", f"a_q{qt % 2}", q1, DT["qA"], in_ap=D_["qA"][:, :, cs].rearrange("h p t -> p h t"))
            fw.op("dve", [], [maskT], lambda e, qt=qt: e.memset(maskT.ap[:, 4 * qt:4 * qt + 4, :], 0.0))
            for qi in range(4):
                i = 4 * qt + qi
                nk = (i + 1) * 128
                qs = slice(i * 128, (i + 1) * 128)
                iwt = iw[i % 2]
                fw.dma("sp", f"a_iw{i % 2}", iwt, DT["iw"], in_ap=D_["iw"][qs, :])
                iq = iqs[i % 2]
                fw.dma("sp", f"a_iq{i % 2}", iq, DT["iq"], in_ap=D_["iq"][:, :, qs].rearrange("h p t -> p h t"))
                for kk in range((nk + 511) // 512):
                    w = min(512, nk - kk * 512)
                    ks = slice(kk * 512, kk * 512 + w)
                    for hh in range(8):
                        c, half = hh // 2, hh % 2
                        pr = slice(half * 64, half * 64 + 64)
                        Sp = ps[hh % 2]
                        fw.op("pe", [iq, ik], [Sp], lambda e, Sp=Sp, pr=pr, c=c, iq=iq, ks=ks, w=w: e.matmul(
                            Sp.ap[:, 0:w], iq.ap[pr, c, :], ik.ap[pr, ks], start=True, stop=True))
                        r = rl[nr % 2]
                        nr += 1
                        fw.op("act", [Sp], [r], lambda e, Sp=Sp, r=r, w=w: e.activation(out=r.ap[:, 0:w], in_=Sp.ap[:, 0:w], func=AF.Relu))
                        if hh == 0:
                            fw.op("dve", [r, iwt], [score], lambda e, r=r, w=w, ks=ks, iwt=iwt: e.tensor_scalar(
                                out=score.ap[:, ks], in0=r.ap[:, 0:w], scalar1=iwt.ap[:, 0:1], scalar2=None, op0=ALU.mult))
                        else:
                            fw.op("dve", [r, iwt, score], [score], lambda e, r=r, w=w, ks=ks, iwt=iwt, hh=hh: e.scalar_tensor_tensor(
                                out=score.ap[:, ks], in0=r.ap[:, 0:w], scalar=iwt.ap[:, hh:hh + 1], in1=score.ap[:, ks],
                                op0=ALU.mult, op1=ALU.add))
                fw.op("dve", [score], [sm], lambda e, nk=nk: e.reduce_max(
                    out=d0, in_=score.ap[:, 0:nk], axis=AX.X, apply_absolute_value=True))
                fw.op("dve", [score, g.dbias], [score], lambda e, qs=qs: e.tensor_tensor(
                    out=score.ap[:, qs], in0=score.ap[:, qs], in1=g.dbias.ap, op=ALU.add))
                fw.op("dve", [sm], [sm], lambda e: e.tensor_scalar(out=d0, in0=d0, scalar1=1.0, scalar2=None, op0=ALU.add))
                fw.op("dve", [sm, g.pw2], [dk], lambda e: e.tensor_scalar(out=dk.ap, in0=g.pw2.ap, scalar1=d0, scalar2=None, op0=ALU.mult))
                fw.op("dve", [], [sm], lambda e: e.memset(mid, 0.0))
                for it in range(1, NITER + 1):
                    fw.op("dve", [score, sm], [maskq, sm], lambda e, nk=nk: e.tensor_scalar(
                        out=junk.ap[:, 0:nk], in0=score.ap[:, 0:nk], scalar1=mid, scalar2=0.0, op0=ALU.is_ge, op1=ALU.add, accum_out=cn))
                    fw.op("dve", [sm], [sm], lambda e: e.tensor_scalar(
                        out=stp, in0=cn, scalar1=255.5, scalar2=0.5, op0=ALU.is_ge, op1=ALU.subtract))
                    fw.op("dve", [sm, dk], [sm], lambda e, it=it: e.scalar_tensor_tensor(
                        out=mid, in0=stp, scalar=dk.ap[:, it:it + 1], in1=mid, op0=ALU.mult, op1=ALU.add))
                fw.op("dve", [sm, dk], [sm], lambda e: e.scalar_tensor_tensor(
                    out=lo, in0=dk.ap[:, NITER:NITER + 1], scalar=-0.5, in1=mid, op0=ALU.mult, op1=ALU.add))
                fw.op("dve", [score, sm], [maskq], lambda e, nk=nk: e.tensor_scalar(
                    out=maskq.ap[:, 0:nk], in0=score.ap[:, 0:nk], scalar1=lo, scalar2=None, op0=ALU.is_ge))
                for kb0 in range(0, i + 1, 4):
                    nb = min(4, i + 1 - kb0)
                    for b in range(nb):
                        kb = kb0 + b
                        fw.op("pe", [maskq, g.ident], [ps[6]], lambda e, kb=kb, b=b: e.transpose(
                            ptb[:, b * 128:(b + 1) * 128], maskq.ap[:, kb * 128:(kb + 1) * 128], g.ident.ap))
                    fw.op("act", [ps[6]], [maskT], lambda e, kb0=kb0, nb=nb, qi=qi: e.activation(
                        out=maskT.ap[:, kb0:kb0 + nb, qi * 128:(qi + 1) * 128],
                        in_=ptb[:, 0:nb * 128].rearrange("p (b c) -> p b c", c=128), func=AF.Copy))
            nkt = 4 * qt + 4
            for hd in range(4):
                O, Dn = ps[2 + hd % 2], ps[4 + hd % 2]
                for kt in range(nkt):
                    Lp = ps[kt % 2]
                    ks = slice(kt * 128, (kt + 1) * 128)
                    fw.op("pe", [kA, q1], [Lp], lambda e, Lp=Lp, ks=ks, hd=hd, q1=q1: e.matmul(
                        Lp.ap, kA.ap[:, hd, ks], q1.ap[:, hd, :], start=True, stop=True))
                    P = pt[np_ % 3]
                    np_ += 1
                    fw.op("act", [Lp], [P], lambda e, Lp=Lp, P=P: e.activation(out=P.ap, in_=Lp.ap, func=AF.Exp, scale=sc))
                    fw.op("dve", [P, maskT], [P], lambda e, P=P, kt=kt: e.tensor_tensor(
                        out=P.ap, in0=P.ap, in1=maskT.ap[:, kt, :], op=ALU.mult))
                    fw.op("pe", [vA, P], [O], lambda e, O=O, P=P, kt=kt, hd=hd: e.matmul(
                        O.ap, vA.ap[:, kt, hd * 128:(hd + 1) * 128], P.ap, start=(kt == 0), stop=(kt == nkt - 1)))
                    fw.op("pe", [g.ones_bf, P], [Dn], lambda e, Dn=Dn, P=P, kt=kt: e.matmul(
                        Dn.ap, g.ones_bf.ap, P.ap, start=(kt == 0), stop=(kt == nkt - 1)))
                fw.op("dve", [Dn], [rd], lambda e, Dn=Dn: e.reciprocal(out=rd.ap, in_=Dn.ap))
                o = ob[no % 2]
                fw.op("dve", [O, rd], [o], lambda e, O=O, o=o: e.tensor_tensor(out=o.ap, in0=O.ap, in1=rd.ap, op=ALU.mult))
                fw.dma("sp", f"a_o{no % 2}", DT["br"], o, out_ap=D_["br"][0, hd][:, cs])
                no += 1
        fw.barrier()


def gla_phase(g, l):
    nc, fw = g.nc, g.fw
    D_, DT = g.D, g.DT
    ps = g.ps
    with contextlib.ExitStack() as st:
        sb = lambda n, s, d: g.sb(n, s, d, st)
        gv = sb("g_v", [128, 32, 512], BF16)
        glr = sb("g_lr", [16, S], BF16)
        wgf = sb("g_wgf", [16, 256], F32)
        wgb = sb("g_wgb", [16, 256], BF16)
        pv = sb("g_pv", [128, 8], F32)
        nbg = sb("g_nbg", [64, 4], F32)
        q = sb("g_q", [64, S], F32)
        k = sb("g_k", [64, S], F32)
        cs_ = sb("g_cs", [64, S], F32)
        eb = sb("g_eb", [64, S], F32)
        qt_ = sb("g_qt", [64, S], BF16)
        ktb = sb("g_ktb", [64, S], BF16)
        khat = sb("g_kh", [64, S], BF16)
        khT = sb("g_khT", [128, 32, 64], BF16)
        Ofm = sb("g_O", [128, S], F32)
        Ab = [sb(f"g_Ab{i}", [128, 128], BF16) for i in range(2)]
        Sst = sb("g_S", [64, 128], F32)
        Sb = sb("g_Sb", [64, 128], BF16)
        sqb = sb("g_sqb", [128, TT], BF16)
        rs = sb("g_rs", [128, TT], F32)
        grt = [sb(f"g_gr{i}", [128, TT], BF16) for i in range(2)]
        t1 = sb("g_t1", [128, TT], F32)
        ob = [sb(f"g_ob{i}", [128, TT], BF16) for i in range(2)]
        fw.dma("sp", "g_l0", gv, DT["gv"], in_ap=D_["gv"].rearrange("(k p) c -> p k c", p=128))
        fw.dma("sp", "g_l1", glr, DT["glr"])
        fw.dma("sp", "g_l2", wgf, g.IT["gla_w_gate"], in_ap=g.I["gla_w_gate"][l])
        fw.dma("sp", "g_l3", pv, g.IT["pvec"], in_ap=g.I["pvec"][l][:, 206:214])
        fw.op("dve", [wgf], [wgb], lambda e: e.tensor_copy(out=wgb.ap, in_=wgf.ap))
        fw.op("dve", [pv], [nbg], lambda e: e.tensor_scalar(out=nbg.ap, in0=pv.ap[0:64, 0:4], scalar1=-1.0, scalar2=None, op0=ALU.mult))
        ptb = ps[6].ap.bitcast(BF16)
        no = 0
        for hd in range(4):
            fw.dma("sp", "g_lq", q, DT["gq"], in_ap=D_["gq"][hd])
            fw.dma("sp", "g_lk", k, DT["gk"], in_ap=D_["gk"][hd])
            for tt in range(NTT):
                cs = slice(tt * TT, (tt + 1) * TT)
                pz = ps[tt % 2]
                fw.op("pe", [wgb, glr], [pz], lambda e, pz=pz, cs=cs, hd=hd: e.matmul(
                    pz.ap[0:64, :], wgb.ap[:, hd * 64:(hd + 1) * 64], glr.ap[:, cs], start=True, stop=True))
                fw.op("act", [pz, nbg], [eb], lambda e, pz=pz, cs=cs, hd=hd: e.activation(
                    out=eb.ap[:, cs], in_=pz.ap[0:64, :], func=AF.Exp, bias=nbg.ap[:, hd:hd + 1], scale=-1.0))
                fw.op("act", [eb], [eb], lambda e, cs=cs: e.activation(
                    out=eb.ap[:, cs], in_=eb.ap[:, cs], func=AF.Ln, bias=1.0, scale=1.0))
                fw.op("dve", [eb, g.rmask], [cs_], lambda e, cs=cs: e.tensor_tensor_scan(
                    out=cs_.ap[:, cs], data0=g.rmask.ap, data1=eb.ap[:, cs], initial=0.0, op0=ALU.mult, op1=ALU.add))
            fw.op("act", [cs_], [eb], lambda e: e.activation(out=eb.ap, in_=cs_.ap, func=AF.Exp, scale=-1.0 / 16))
            fw.op("dve", [q, eb], [qt_], lambda e: e.tensor_tensor(out=qt_.ap, in0=q.ap, in1=eb.ap, op=ALU.mult))
            fw.op("act", [cs_], [cs_], lambda e: e.activation(out=cs_.ap, in_=cs_.ap, func=AF.Exp, scale=1.0 / 16))
            fw.op("dve", [k, cs_], [k], lambda e: e.tensor_tensor(out=k.ap, in0=k.ap, in1=cs_.ap, op=ALU.mult))
            fw.op("act", [k], [ktb], lambda e: e.activation(out=ktb.ap, in_=k.ap, func=AF.Copy))
            ebl = eb.ap.rearrange("p (c j) -> p c j", j=64)[:, :, 63:64]
            fw.op("dve", [k, eb], [khat], lambda e: e.tensor_tensor(
                out=khat.ap.rearrange("p (c j) -> p c j", j=64), in0=k.ap.rearrange("p (c j) -> p c j", j=64),
                in1=ebl.to_broadcast([64, 64, 64]), op=ALU.mult))
            for t0 in range(0, 32, 8):
                for b in range(8):
                    fw.op("pe", [khat, g.ident], [ps[6]], lambda e, t0=t0, b=b: e.transpose(
                        ptb[:, b * 64:(b + 1) * 64], khat.ap[:, (t0 + b) * 128:(t0 + b + 1) * 128], g.ident.ap[0:64, 0:64]))
                fw.op("act", [ps[6]], [khT], lambda e, t0=t0: e.activation(
                    out=khT.ap[:, t0:t0 + 8, :], in_=ptb[:, 0:512].rearrange("p (b c) -> p b c", c=64), func=AF.Copy))
            fw.op("dve", [], [Sst], lambda e: e.memset(Sst.ap, 0.0))
            for tl in range(32):
                ts_ = slice(tl * 128, (tl + 1) * 128)
                pa = ps[tl % 2]
                fw.op("pe", [ktb, qt_], [pa], lambda e, pa=pa, ts_=ts_: e.matmul(
                    pa.ap[:, 0:128], ktb.ap[:, ts_], qt_.ap[:, ts_], start=True, stop=True))
                A = Ab[tl % 2]
                fw.op("dve", [pa, g.gmask], [A], lambda e, pa=pa, A=A: e.tensor_tensor(
                    out=A.ap, in0=pa.ap[:, 0:128], in1=g.gmask.ap, op=ALU.mult))
                O = ps[2 + tl % 2]
                last_is_inter = True
                fw.op("pe", [gv, A], [O], lambda e, O=O, A=A, tl=tl, hd=hd: e.matmul(
                    O.ap[:, 0:128], gv.ap[:, tl, hd * 128:(hd + 1) * 128], A.ap, start=True, stop=False))
                for ci in range(2):
                    c = 2 * tl + ci
                    cc = slice(c * 64, (c + 1) * 64)
                    if c > 0:
                        fw.op("pe", [Sb, qt_], [O], lambda e, O=O, ci=ci, cc=cc: e.matmul(
                            O.ap[:, ci * 64:(ci + 1) * 64], Sb.ap, qt_.ap[:, cc], start=False, stop=(ci == 1)))
                    U = ps[4 + c % 2]
                    pr = slice(ci * 64, (ci + 1) * 64)
                    fw.op("pe", [khT, gv], [U], lambda e, U=U, pr=pr, tl=tl, hd=hd: e.matmul(
                        U.ap[0:64, 0:128], khT.ap[pr, tl, :], gv.ap[pr, tl, hd * 128:(hd + 1) * 128], start=True, stop=True))
                    fw.op("dve", [U, Sst, eb], [Sst], lambda e, U=U, c=c: e.scalar_tensor_tensor(
                        out=Sst.ap, in0=Sst.ap, scalar=eb.ap[:, c * 64 + 63:c * 64 + 64], in1=U.ap[0:64, 0:128],
                        op0=ALU.mult, op1=ALU.add))
                    fw.op("act", [Sst], [Sb], lambda e: e.activation(out=Sb.ap, in_=Sst.ap, func=AF.Copy))
                fw.op("act", [O], [Ofm], lambda e, O=O, ts_=ts_: e.activation(out=Ofm.ap[:, ts_], in_=O.ap[:, 0:128], func=AF.Copy))
            for tt in range(NTT):
                cs = slice(tt * TT, (tt + 1) * TT)
                gr = grt[tt % 2]
                fw.dma("sp", f"g_gr{tt % 2}", gr, DT["gr"], in_ap=D_["gr"][hd][:, cs])
                fw.op("act", [Ofm], [sqb], lambda e, cs=cs: e.activation(out=sqb.ap, in_=Ofm.ap[:, cs], func=AF.Square))
                fw.op("pe", [sqb, g.ones_bf], [ps[7]], lambda e: e.matmul(ps[7].ap, g.ones_bf.ap, sqb.ap, start=True, stop=True))
                fw.op("act", [ps[7], g.eps_t], [rs], lambda e: e.activation(
                    out=rs.ap, in_=ps[7].ap, func=AF.Sqrt, bias=g.eps_t.ap[:, 0:1], scale=1.0 / 128))
                fw.op("dve", [rs], [rs], lambda e: e.reciprocal(out=rs.ap, in_=rs.ap))
                fw.op("dve", [Ofm, pv, rs], [t1], lambda e, cs=cs: e.scalar_tensor_tensor(
                    out=t1.ap, in0=Ofm.ap[:, cs], scalar=pv.ap[:, 4:5], in1=rs.ap, op0=ALU.mult, op1=ALU.mult))
                o = ob[no % 2]
                fw.op("dve", [t1, gr], [o], lambda e, o=o, gr=gr: e.tensor_tensor(out=o.ap, in0=t1.ap, in1=gr.ap, op=ALU.mult))
                fw.dma("sp", f"g_o{no % 2}", DT["br"], o, out_ap=D_["br"][1, hd][:, cs])
                no += 1
        fw.barrier()


T5 = 128


def s5_phase(g, l):
    nc, fw = g.nc, g.fw
    D_, DT = g.D, g.DT
    ps = g.ps
    Wglu = g.W[("glu", l)][1]
    with contextlib.ExitStack() as st:
        sb = lambda n, s, d: g.sb(n, s, d, st)
        pv = sb("s_pv", [128, 56], F32)
        sm = sb("s_sm", [128, 16, 12], F32)
        smi = sb("s_smi", [128, 16], I32)
        BTf = sb("s_BTf", [128, 16, 128], F32)
        BTr = sb("s_BTr", [128, 16, 128], BF16)
        BTi = sb("s_BTi", [128, 16, 128], BF16)
        CTr = sb("s_CTr", [128, 16, 128], BF16)
        CTi = sb("s_CTi", [128, 16, 128], BF16)
        wgl = sb("s_wgl", [128, 4, 4, 128], BF16)
        Cj = sb("s_Cj", [128, 16, T5], F32)
        Sj = sb("s_Sj", [128, 16, T5], F32)
        Er = sb("s_Er", [128, 16, T5], F32)
        Ei = sb("s_Ei", [128, 16, T5], F32)
        magT = sb("s_mag", [128, 16, T5], F32)
        tki = sb("s_tki", [128, 16, T5], I32)
        uf = [sb(f"s_uf{i}", [128, 4, T5], F32) for i in range(2)]
        ub = sb("s_ub", [128, 4, T5], BF16)
        xr = sb("s_xr", [128, 16, T5], F32)
        xi = sb("s_xi", [128, 16, T5], F32)
        gr_ = sb("s_gr", [128, 16, T5], F32)
        gi_ = sb("s_gi", [128, 16, T5], F32)
        ta = sb("s_ta", [128, 4, T5], F32)
        tb = sb("s_tb", [128, 4, T5], F32)
        hr = sb("s_hr", [128, 16, T5], BF16)
        hi = sb("s_hi", [128, 16, T5], BF16)
        init = sb("s_init", [128, 16, 4], F32)
        ygf = sb("s_ygf", [128, 4, T5], F32)
        ygb = sb("s_ygb", [128, 4, T5], BF16)
        sg = sb("s_sg", [128, T5], F32)
        brc = sb("s_brc", [128, 4, S], BF16)
        fw.dma("sp", "s_l0", pv, g.IT["pvec"], in_ap=g.I["pvec"][l][:, 211:267])
        for i, Wt in enumerate(Wglu):
            fw.dma("sp", "s_l1", wgl, Wt, out_ap=wgl.ap[:, i])
        for (src, idx, dst, scl) in (("BT", 0, BTr, 1.0), ("BT", 1, BTi, 1.0), ("CT", 0, CTr, 1.0), ("CT", 1, CTi, -1.0)):
            fw.dma("sp", "s_l2", BTf, g.IT[src], in_ap=g.I[src][l, idx])
            fw.op("act", [BTf], [dst], lambda e, dst=dst, scl=scl: e.activation(out=dst.ap, in_=BTf.ap, func=AF.Copy, scale=scl))
        col = lambda i: sm.ap[:, :, i]
        are, aim, ldt = pv.ap[:, 0:16], pv.ap[:, 16:32], pv.ap[:, 32:48]
        DTc, MAG, TH, ABR, ABI, ZR, ZI, RR, RI, TMP, TMP2, TMP3 = (col(i) for i in range(12))
        smT = sm

        def sop(fn, reads=(), eng="dve"):
            fw.op(eng, [smT, pv] + list(reads), [smT], fn)

        sop(lambda e: e.activation(out=DTc, in_=ldt, func=AF.Exp), eng="act")
        sop(lambda e: e.tensor_tensor(out=TMP, in0=DTc, in1=are, op=ALU.mult))
        sop(lambda e: e.activation(out=MAG, in_=TMP, func=AF.Exp), eng="act")
        sop(lambda e: e.tensor_tensor(out=TH, in0=DTc, in1=aim, op=ALU.mult))

        def small_sin(x_ap, out_ap, shift):
            sop(lambda e: e.tensor_scalar(out=TMP, in0=x_ap, scalar1=shift, scalar2=1.0 / TWO_PI, op0=ALU.add, op1=ALU.mult))
            fw.op("dve", [smT], [smi], lambda e: e.tensor_copy(out=smi.ap, in_=TMP))
            fw.op("dve", [smi], [smT], lambda e: e.tensor_copy(out=TMP2, in_=smi.ap))
            sop(lambda e: e.tensor_scalar(out=TMP, in0=x_ap, scalar1=shift, scalar2=None, op0=ALU.add))
            sop(lambda e: e.scalar_tensor_tensor(out=TMP, in0=TMP2, scalar=-TWO_PI, in1=TMP, op0=ALU.mult, op1=ALU.add))
            sop(lambda e: e.tensor_scalar(out=TMP2, in0=TMP, scalar1=PI, scalar2=-TWO_PI, op0=ALU.is_gt, op1=ALU.mult))
            sop(lambda e: e.tensor_tensor(out=TMP, in0=TMP, in1=TMP2, op=ALU.add))
            sop(lambda e: e.tensor_scalar(out=TMP2, in0=TMP, scalar1=-PI, scalar2=TWO_PI, op0=ALU.is_lt, op1=ALU.mult))
            sop(lambda e: e.tensor_tensor(out=TMP, in0=TMP, in1=TMP2, op=ALU.add))
            sop(lambda e: e.activation(out=out_ap, in_=TMP, func=AF.Sin), eng="act")

        small_sin(TH, ABI, 0.0)
        small_sin(TH, ABR, PI / 2)
        sop(lambda e: e.tensor_tensor(out=ABI, in0=ABI, in1=MAG, op=ALU.mult))
        sop(lambda e: e.tensor_tensor(out=ABR, in0=ABR, in1=MAG, op=ALU.mult))
        sop(lambda e: e.tensor_tensor(out=TMP, in0=are, in1=are, op=ALU.mult))
        sop(lambda e: e.tensor_tensor(out=TMP2, in0=aim, in1=aim, op=ALU.mult))
        sop(lambda e: e.tensor_tensor(out=TMP, in0=TMP, in1=TMP2, op=ALU.add))
        sop(lambda e: e.reciprocal(out=TMP, in_=TMP))
        sop(lambda e: e.tensor_scalar(out=TMP3, in0=ABR, scalar1=-1.0, scalar2=None, op0=ALU.add))
        sop(lambda e: e.tensor_tensor(out=ZR, in0=TMP3, in1=are, op=ALU.mult))
        sop(lambda e: e.tensor_tensor(out=TMP2, in0=ABI, in1=aim, op=ALU.mult))
        sop(lambda e: e.tensor_tensor(out=ZR, in0=ZR, in1=TMP2, op=ALU.add))
        sop(lambda e: e.tensor_tensor(out=ZR, in0=ZR, in1=TMP, op=ALU.mult))
        sop(lambda e: e.tensor_tensor(out=ZI, in0=ABI, in1=are, op=ALU.mult))
        sop(lambda e: e.tensor_tensor(out=TMP2, in0=TMP3, in1=aim, op=ALU.mult))
        sop(lambda e: e.tensor_tensor(out=ZI, in0=ZI, in1=TMP2, op=ALU.subtract))
        sop(lambda e: e.tensor_tensor(out=ZI, in0=ZI, in1=TMP, op=ALU.mult))
        sop(lambda e: e.tensor_scalar(out=TMP3, in0=TH, scalar1=float(T5), scalar2=None, op0=ALU.mult))
        small_sin(TMP3, RI, 0.0)
        sop(lambda e: e.tensor_scalar(out=TMP3, in0=TH, scalar1=float(T5), scalar2=None, op0=ALU.mult))
        small_sin(TMP3, RR, PI / 2)
        for sc in range(16):
            fw.op("dve", [smT, g.jrow], [xr], lambda e, sc=sc: e.tensor_scalar(
                out=xr.ap[:, sc, :], in0=g.jrow.ap, scalar1=sm.ap[:, sc, 2:3], scalar2=None, op0=ALU.mult))
            fw.op("dve", [smT, g.onef], [magT], lambda e, sc=sc: e.tensor_scalar(
                out=magT.ap[:, sc, :], in0=g.onef.ap[:, 0:T5], scalar1=sm.ap[:, sc, 1:2], scalar2=None, op0=ALU.mult))
        sin_reduced(fw, tki, xi, xr, Sj)
        fw.op("dve", [xr], [xr], lambda e: e.tensor_scalar(out=xr.ap, in0=xr.ap, scalar1=PI / 2, scalar2=None, op0=ALU.add))
        sin_reduced(fw, tki, xi, xr, Cj)
        for sc in range(16):
            zr, zi = sm.ap[:, sc, 5:6], sm.ap[:, sc, 6:7]
            fw.op("dve", [smT, Cj], [Er], lambda e, sc=sc, zr=zr: e.tensor_scalar(out=Er.ap[:, sc, :], in0=Cj.ap[:, sc, :], scalar1=zr, scalar2=None, op0=ALU.mult))
            fw.op("dve", [smT, Sj, Er], [Er], lambda e, sc=sc, zi=zi: e.scalar_tensor_tensor(
                out=Er.ap[:, sc, :], in0=Sj.ap[:, sc, :], scalar=zi, in1=Er.ap[:, sc, :], op0=ALU.mult, op1=ALU.add))
            fw.op("dve", [smT, Cj], [Ei], lambda e, sc=sc, zi=zi: e.tensor_scalar(out=Ei.ap[:, sc, :], in0=Cj.ap[:, sc, :], scalar1=zi, scalar2=None, op0=ALU.mult))
            fw.op("dve", [smT, Sj], [xr], lambda e, sc=sc, zr=zr: e.tensor_scalar(out=xr.ap[:, sc, :], in0=Sj.ap[:, sc, :], scalar1=zr, scalar2=None, op0=ALU.mult))
        fw.op("dve", [Ei, xr], [Ei], lambda e: e.tensor_tensor(out=Ei.ap, in0=Ei.ap, in1=xr.ap, op=ALU.subtract))
        fw.op("dve", [], [init], lambda e: e.memset(init.ap, 0.0))
        ntile = S // T5
        for tl in range(ntile):
            ts_ = slice(tl * T5, (tl + 1) * T5)
            u = uf[tl % 2]
            fw.dma("sp", f"s_u{tl % 2}", u, DT["su"], in_ap=D_["su"][:, :, ts_].rearrange("c p t -> p c t"))
            fw.op("act", [u], [ub], lambda e, u=u: e.activation(out=ub.ap, in_=u.ap, func=AF.Copy))
            for g4 in range(4):
                pr_, pi_ = ps[(2 * g4) % 4], ps[(2 * g4 + 1) % 4]
                for s4 in range(4):
                    sc = 4 * g4 + s4
                    fw.op("pe", [BTr, ub], [pr_], lambda e, pr_=pr_, sc=sc, s4=s4: e.matmul(
                        pr_.ap[:, s4 * T5:(s4 + 1) * T5], BTr.ap[:, sc, :], ub.ap[:, sc // 4, :], start=True, stop=True))
                    fw.op("pe", [BTi, ub], [pi_], lambda e, pi_=pi_, sc=sc, s4=s4: e.matmul(
                        pi_.ap[:, s4 * T5:(s4 + 1) * T5], BTi.ap[:, sc, :], ub.ap[:, sc // 4, :], start=True, stop=True))
                scs = slice(4 * g4, 4 * g4 + 4)
                v3 = lambda t: t.ap.rearrange("p (a b) -> p a b", b=T5)
                fw.op("dve", [Er, pr_], [ta], lambda e, pr_=pr_, scs=scs: e.tensor_tensor(out=ta.ap, in0=Er.ap[:, scs, :], in1=v3(pr_), op=ALU.mult))
                fw.op("dve", [Ei, pi_], [tb], lambda e, pi_=pi_, scs=scs: e.tensor_tensor(out=tb.ap, in0=Ei.ap[:, scs, :], in1=v3(pi_), op=ALU.mult))
                fw.op("dve", [ta, tb], [xr], lambda e, scs=scs: e.tensor_tensor(out=xr.ap[:, scs, :], in0=ta.ap, in1=tb.ap, op=ALU.subtract))
                fw.op("dve", [Er, pi_], [ta], lambda e, pi_=pi_, scs=scs: e.tensor_tensor(out=ta.ap, in0=Er.ap[:, scs, :], in1=v3(pi_), op=ALU.mult))
                fw.op("dve", [Ei, pr_], [tb], lambda e, pr_=pr_, scs=scs: e.tensor_tensor(out=tb.ap, in0=Ei.ap[:, scs, :], in1=v3(pr_), op=ALU.mult))
                fw.op("dve", [ta, tb], [xi], lambda e, scs=scs: e.tensor_tensor(out=xi.ap[:, scs, :], in0=ta.ap, in1=tb.ap, op=ALU.add))
            for sc in range(16):
                fw.op("dve", [magT, xr, init], [gr_], lambda e, sc=sc: e.tensor_tensor_scan(
                    out=gr_.ap[:, sc, :], data0=magT.ap[:, sc, :], data1=xr.ap[:, sc, :], initial=init.ap[:, sc, 0:1], op0=ALU.mult, op1=ALU.add))
                fw.op("dve", [magT, xi, init], [gi_], lambda e, sc=sc: e.tensor_tensor_scan(
                    out=gi_.ap[:, sc, :], data0=magT.ap[:, sc, :], data1=xi.ap[:, sc, :], initial=init.ap[:, sc, 1:2], op0=ALU.mult, op1=ALU.add))
            glr_, gli_ = gr_.ap[:, :, T5 - 1], gi_.ap[:, :, T5 - 1]
            i0, i1, i2, i3 = (init.ap[:, :, j] for j in range(4))
            fw.op("dve", [gr_, smT], [init], lambda e: e.tensor_tensor(out=i2, in0=glr_, in1=RR, op=ALU.mult))
            fw.op("dve", [gi_, smT], [init], lambda e: e.tensor_tensor(out=i3, in0=gli_, in1=RI, op=ALU.mult))
            fw.op("dve", [init], [init], lambda e: e.tensor_tensor(out=i0, in0=i2, in1=i3, op=ALU.subtract))
            fw.op("dve", [gi_, smT], [init], lambda e: e.tensor_tensor(out=i2, in0=gli_, in1=RR, op=ALU.mult))
            fw.op("dve", [gr_, smT], [init], lambda e: e.tensor_tensor(out=i3, in0=glr_, in1=RI, op=ALU.mult))
            fw.op("dve", [init], [init], lambda e: e.tensor_tensor(out=i1, in0=i2, in1=i3, op=ALU.add))
            for g4 in range(4):
                scs = slice(4 * g4, 4 * g4 + 4)
                fw.op("dve", [Cj, gr_], [ta], lambda e, scs=scs: e.tensor_tensor(out=ta.ap, in0=Cj.ap[:, scs, :], in1=gr_.ap[:, scs, :], op=ALU.mult))
                fw.op("dve", [Sj, gi_], [tb], lambda e, scs=scs: e.tensor_tensor(out=tb.ap, in0=Sj.ap[:, scs, :], in1=gi_.ap[:, scs, :], op=ALU.mult))
                fw.op("dve", [ta, tb], [hr], lambda e, scs=scs: e.tensor_tensor(out=hr.ap[:, scs, :], in0=ta.ap, in1=tb.ap, op=ALU.subtract))
                fw.op("dve", [Cj, gi_], [ta], lambda e, scs=scs: e.tensor_tensor(out=ta.ap, in0=Cj.ap[:, scs, :], in1=gi_.ap[:, scs, :], op=ALU.mult))
                fw.op("dve", [Sj, gr_], [tb], lambda e, scs=scs: e.tensor_tensor(out=tb.ap, in0=Sj.ap[:, scs, :], in1=gr_.ap[:, scs, :], op=ALU.mult))
                fw.op("dve", [ta, tb], [hi], lambda e, scs=scs: e.tensor_tensor(out=hi.ap[:, scs, :], in0=ta.ap, in1=tb.ap, op=ALU.add))
            for oc in range(4):
                Y = ps[4 + oc % 2]
                for s4 in range(4):
                    sc = 4 * oc + s4
                    fw.op("pe", [CTr, hr], [Y], lambda e, Y=Y, sc=sc, s4=s4: e.matmul(
                        Y.ap[:, 0:T5], CTr.ap[:, sc, :], hr.ap[:, sc, :], start=(s4 == 0), stop=False))
                    fw.op("pe", [CTi, hi], [Y], lambda e, Y=Y, sc=sc, s4=s4: e.matmul(
                        Y.ap[:, 0:T5], CTi.ap[:, sc, :], hi.ap[:, sc, :], start=False, stop=(s4 == 3)))
                fw.op("dve", [u, pv, Y], [ygf], lambda e, Y=Y, oc=oc, u=u: e.scalar_tensor_tensor(
                    out=ygf.ap[:, oc, :], in0=u.ap[:, oc, :], scalar=pv.ap[:, 48 + oc:49 + oc], in1=Y.ap[:, 0:T5], op0=ALU.mult, op1=ALU.add))
            fw.op("act", [ygf], [ygf], lambda e: e.activation(out=ygf.ap, in_=ygf.ap, func=AF.Gelu))
            fw.op("act", [ygf], [ygb], lambda e: e.activation(out=ygb.ap, in_=ygf.ap, func=AF.Copy))
            for oc2 in range(4):
                Z = ps[6 + oc2 % 2]
                for kc in range(4):
                    fw.op("pe", [wgl, ygb], [Z], lambda e, Z=Z, oc2=oc2, kc=kc: e.matmul(
                        Z.ap[:, 0:T5], wgl.ap[:, oc2, kc, :], ygb.ap[:, kc, :], start=(kc == 0), stop=(kc == 3)))
                fw.op("act", [Z, pv], [sg], lambda e, Z=Z, oc2=oc2: e.activation(
                    out=sg.ap, in_=Z.ap[:, 0:T5], func=AF.Sigmoid, bias=pv.ap[:, 52 + oc2:53 + oc2], scale=1.0))
                fw.op("dve", [ygf, sg], [brc], lambda e, oc2=oc2, ts_=ts_: e.tensor_tensor(
                    out=brc.ap[:, oc2, ts_], in0=ygf.ap[:, oc2, :], in1=sg.ap, op=ALU.mult))
        for oc in range(4):
            fw.dma("sp", "s_o", DT["br"], brc, out_ap=D_["br"][2, oc], in_ap=brc.ap[:, oc, :])
        fw.barrier()


def merge_phase(g, l):
    nc, fw = g.nc, g.fw
    D_, DT = g.D, g.DT
    ps = g.ps
    AB = g.AB[l]
    Wg = [g.W[(f"wg{i}", l)][1] for i in range(4)]
    Wb = [g.W[(f"wb{i}", l)][1] for i in range(4)]
    Wo = g.W[("wo", l)][1]
    with contextlib.ExitStack() as st:
        sb = lambda n, s, d: g.sb(n, s, d, st)
        xt = sb("e_xt", [128, NCH, TT], F32)
        h = sb("e_h", [128, NCH, TT], BF16)
        br = sb("e_br", [128, 16, TT], BF16)
        mg = sb("e_mg", [128, NCH, TT], F32)
        mgb = sb("e_mgb", [128, NCH, TT], BF16)
        wgs = [sb(f"e_wg{i}", [128, NCH, 128], BF16) for i in range(3)]
        wbs = [sb(f"e_wb{i}", [128, 4, 128], BF16) for i in range(3)]
        sgt = [sb(f"e_sg{i}", [128, TT], F32) for i in range(2)]
        tmp = [sb(f"e_tmp{i}", [128, TT], F32) for i in range(2)]
        n = 0
        for tt in range(NTT):
            cs = slice(tt * TT, (tt + 1) * TT)
            fw.dma("sp", "e_xl", xt, g.yT[tt], in_ap=g.y[:, cs].rearrange("(c p) t -> p c t", p=128))
            fw.dma("sp", "e_hl", h, DT["hT"], in_ap=D_["hT"][:, cs].rearrange("(c p) t -> p c t", p=128))
            fw.dma("sp", "e_bl", br, DT["br"], in_ap=D_["br"][:, :, :, cs].rearrange("b c p t -> p (b c) t"))
            for i in range(4):
                for m in range(NCH):
                    wg_, wb_ = wgs[n % 3], wbs[n % 3]
                    fw.dma("sp", f"e_wg{n % 3}", wg_, Wg[i][m])
                    fw.dma("sp", f"e_wb{n % 3}", wb_, Wb[i][m])
                    n += 1
                    pg, pb = ps[(2 * m) % 6], ps[(2 * m + 1) % 6]
                    for k in range(NCH):
                        fw.op("pe", [wg_, h], [pg], lambda e, wg_=wg_, pg=pg, k=k: e.matmul(
                            pg.ap, wg_.ap[:, k, :], h.ap[:, k, :], start=(k == 0), stop=(k == NCH - 1)))
                    for k in range(4):
                        fw.op("pe", [wb_, br], [pb], lambda e, wb_=wb_, pb=pb, k=k, i=i: e.matmul(
                            pb.ap, wb_.ap[:, k, :], br.ap[:, 4 * i + k, :], start=(k == 0), stop=(k == 3)))
                    s_ = sgt[m % 2]
                    fw.op("act", [pg], [s_], lambda e, pg=pg, s_=s_: e.activation(out=s_.ap, in_=pg.ap, func=AF.Sigmoid))
                    if i == 0:
                        fw.op("dve", [s_, pb], [mg], lambda e, s_=s_, pb=pb, m=m: e.tensor_tensor(
                            out=mg.ap[:, m, :], in0=s_.ap, in1=pb.ap, op=ALU.mult))
                    else:
                        t_ = tmp[m % 2]
                        fw.op("dve", [s_, pb], [t_], lambda e, s_=s_, pb=pb, t_=t_: e.tensor_tensor(
                            out=t_.ap, in0=s_.ap, in1=pb.ap, op=ALU.mult))
                        fw.op("dve", [t_, mg], [mg], lambda e, t_=t_, m=m: e.tensor_tensor(
                            out=mg.ap[:, m, :], in0=mg.ap[:, m, :], in1=t_.ap, op=ALU.add))
            for c in range(NCH):
                fw.op("act", [mg], [mgb], lambda e, c=c: e.activation(out=mgb.ap[:, c, :], in_=mg.ap[:, c, :], func=AF.Copy))
            for m in range(NCH):
                wo_ = wgs[n % 3]
                fw.dma("sp", f"e_wg{n % 3}", wo_, Wo[m])
                n += 1
                po = ps[m % 6]
                for k in range(NCH):
                    fw.op("pe", [wo_, mgb], [po], lambda e, wo_=wo_, po=po, k=k: e.matmul(
                        po.ap, wo_.ap[:, k, :], mgb.ap[:, k, :], start=(k == 0), stop=(k == NCH - 1)))
                G = AB.ap[:, 5 * 16 + m:5 * 16 + m + 1]
                fw.op("dve", [po, xt, AB], [xt], lambda e, po=po, m=m, G=G: e.scalar_tensor_tensor(
                    out=xt.ap[:, m, :], in0=po.ap, scalar=G, in1=xt.ap[:, m, :], op0=ALU.mult, op1=ALU.add))
            fw.dma("sp", "e_xs", g.yT[tt], xt, out_ap=g.y[:, cs].rearrange("(c p) t -> p c t", p=128))
        fw.barrier()


def _swap(a, half):
    sh = a.shape
    b = a.reshape(sh[:-1] + (sh[-1] // (2 * half), 2, half))
    return np.ascontiguousarray(b[..., ::-1, :]).reshape(sh)


def pack_shared(inputs):
    f = lambda k: np.asarray(inputs[k], dtype=np.float32)
    out = {}
    for k in ("w_ada", "w_ff1_in", "w_ff1_out", "w_ff2_in", "w_ff2_out", "gla_w_gate", "s5_w_glu",
              "w_branch", "w_gate", "w_out"):
        out[k] = np.ascontiguousarray(f(k))
    w_in = f("w_in")
    seg = lambda o, n: w_in[:, :, o:o + n]
    blocks = []
    aq, ak = seg(O_AQ, 512), seg(O_AK, 512)
    blocks += [aq, _swap(aq, 64), ak, _swap(ak, 64)]
    iq = seg(O_IQ, 512)
    blocks += [iq, _swap(iq, 32)]
    ik = seg(O_IK, 64)
    blocks += [ik, ik, _swap(ik, 32), _swap(ik, 32)]
    blocks += [seg(O_GR, 512), seg(O_SU, 512), seg(O_MQ, 384), seg(O_MKV, 128)]
    kr = seg(O_MKR, 64)
    blocks += [kr, _swap(kr, 32)]
    blocks += [seg(O_GQ, 256), seg(O_GK, 256)]
    z112 = np.zeros((L, D, 112), np.float32)
    blocks += [seg(O_GLR, 16), z112]
    blocks += [seg(O_AV, 512), seg(O_GV, 512)]
    z120 = np.zeros((L, D, 120), np.float32)
    blocks += [seg(O_IW, 8), z120]
    out["w_inx"] = np.ascontiguousarray(np.concatenate(blocks, axis=-1))
    assert out["w_inx"].shape[-1] == NXB * 128
    uq = f("mla_w_uq").reshape(L, 384, 4, 192)
    nope = uq[..., :128].reshape(L, 384, 512)
    rope = uq[..., 128:]
    ropex = np.concatenate([rope, _swap(rope, 32)], axis=-1).reshape(L, 384, 512)
    out["w_uqx"] = np.ascontiguousarray(np.concatenate([nope, ropex], axis=-1))
    ukv = f("mla_w_ukv").reshape(L, 128, 4, 256)
    out["w_ukvx"] = np.ascontiguousarray(np.concatenate(
        [ukv[..., :128].reshape(L, 128, 512), ukv[..., 128:].reshape(L, 128, 512)], axis=-1))
    pv = np.zeros((L, 128, NP), np.float32)
    pc = lambda v: v.reshape(L, -1, 128).transpose(0, 2, 1)
    pv[:, :, 0:144] = pc(f("b_ada"))
    pv[:, :, 144:192] = pc(f("norm_g").reshape(L, 3 * D))
    dq = f("dsa_qk_norm")
    pv[:, :, 192] = dq[:, 0]; pv[:, :, 193] = _swap(dq[:, 0], 64)
    pv[:, :, 194] = dq[:, 1]; pv[:, :, 195] = _swap(dq[:, 1], 64)
    pv[:, :, 196:199] = pc(f("mla_q_norm"))
    pv[:, :, 199] = f("mla_kv_norm")
    mq = f("mla_qk_norm")
    dup = lambda v: np.concatenate([v, v], axis=-1)
    pv[:, :, 200] = mq[:, 0, :128]; pv[:, :, 201] = dup(mq[:, 0, 128:]); pv[:, :, 202] = dup(_swap(mq[:, 0, 128:], 32))
    pv[:, :, 203] = mq[:, 1, :128]; pv[:, :, 204] = dup(mq[:, 1, 128:]); pv[:, :, 205] = dup(_swap(mq[:, 1, 128:], 32))
    bg = f("gla_b_gate").reshape(L, 4, 64)
    for hd in range(4):
        pv[:, :, 206 + hd] = dup(bg[:, hd])
    pv[:, :, 210] = f("gla_out_norm")
    pv[:, :, 211:227] = pc(f("s5_a_re").reshape(L, 2048))
    pv[:, :, 227:243] = pc(f("s5_a_im").reshape(L, 2048))
    pv[:, :, 243:259] = pc(np.repeat(f("s5_log_dt"), 64, axis=-1))
    pv[:, :, 259:263] = pc(f("s5_d"))
    pv[:, :, 263:267] = pc(f("s5_b_glu"))
    out["pvec"] = pv
    BT = np.zeros((L, 2, 128, 16, 128), np.float32)
    CT = np.zeros((L, 2, 128, 16, 128), np.float32)
    for ri, (bn, cn) in enumerate((("s5_b_re", "s5_c_re"), ("s5_b_im", "s5_c_im"))):
        Bm = f(bn).reshape(L, 32, 64, 16)
        Cm = f(cn).reshape(L, 32, 16, 64)
        for sc in range(16):
            for gl in range(2):
                gg = 2 * sc + gl
                r0 = 32 * (sc % 4) + 16 * gl
                BT[:, ri, r0:r0 + 16, sc, 64 * gl:64 * gl + 64] = Bm[:, gg].transpose(0, 2, 1)
                CT[:, ri, 64 * gl:64 * gl + 64, sc, r0:r0 + 16] = Cm[:, gg].transpose(0, 2, 1)
    out["BT"], out["CT"] = BT, CT
    kc = np.zeros((128, 4), np.float32)
    p = np.arange(128)
    kc[:, 0] = np.power(np.float32(10000.0), -(p % 64).astype(np.float32) / 64).astype(np.float32)
    kc[:, 1] = np.where(p < 64, -1.0, 1.0)
    kc[:, 2] = np.power(np.float32(10000.0), -(p % 32).astype(np.float32) / 32).astype(np.float32)
    kc[:, 3] = np.where((p % 64) < 32, -1.0, 1.0)
    out["kconst"] = kc
    return out


def make_in_maps(inputs, cores, shared=None):
    if shared is None:
        shared = pack_shared(inputs)
    x = np.asarray(inputs["x"], dtype=np.float32)
    c = np.asarray(inputs["c"], dtype=np.float32)
    pos = np.asarray(inputs["positions"]).astype(np.int32)
    maps = []
    for b in cores:
        m = dict(shared)
        m["xT"] = np.ascontiguousarray(x[b].T)
        m["condc"] = np.ascontiguousarray(c[b].reshape(16, 128).T)
        m["pos"] = np.ascontiguousarray(pos[b])
        maps.append(m)
    return maps


def kernel(**inputs):
    nc = build()
    in_maps = make_in_maps(inputs, list(range(8)))
    res = run_bass_kernel_spmd(nc, in_maps, core_ids=list(range(8)))
    out = np.stack([np.ascontiguousarray(np.asarray(r["y"]).T) for r in res.results], axis=0)
    return out.astype(np.float32)
```

```python
import contextlib
import math
import numpy as np
import concourse.bass as bass
import concourse.mybir as mybir
from concourse.bass_utils import run_bass_kernel_spmd

F32 = mybir.dt.float32
BF16 = mybir.dt.bfloat16
I32 = mybir.dt.int32
AF = mybir.ActivationFunctionType
ALU = mybir.AluOpType
AX = mybir.AxisListType

D = 2048
S = 4096
L = 4
DFF = 5504
NCH = 16
NJ = 43
TT = 512
NTT = S // TT
EPS = 1e-6
INC = 4760
O_AQ, O_AK, O_AV, O_IQ, O_IK, O_IW = 0, 512, 1024, 1536, 2048, 2112
O_GQ, O_GK, O_GV, O_GLR, O_GR = 2120, 2376, 2632, 3144, 3160
O_SU = 3672
O_MQ, O_MKV, O_MKR = 4184, 4568, 4696
NXB = 53
NP = 272


class T:
    __slots__ = ("ap", "name", "w", "r")

    def __init__(self, ap, name=""):
        self.ap = ap
        self.name = name
        self.w = None
        self.r = {}


class FW:
    def __init__(self, nc):
        self.nc = nc
        self.engs = {}
        self.sems = {}
        self.tot = {}
        self.seen = {}
        self.isdma = {}
        self.ninst = 0
        self.nwait = 0
        for name, e in (("pe", nc.tensor), ("dve", nc.vector), ("act", nc.scalar),
                        ("pool", nc.gpsimd), ("sp", nc.sync)):
            self.engs[name] = e
            self.seen[name] = {}
            self._mksem(name, False)

    def _mksem(self, key, isdma):
        self.sems[key] = self.nc.alloc_semaphore("s_" + key)
        self.tot[key] = 0
        self.isdma[key] = isdma

    def _wait(self, en, key, val):
        if self.isdma[key]:
            val = self.tot[key]
        if self.seen[en].get(key, 0) >= val:
            return
        self.engs[en].wait_ge(self.sems[key], val)
        self.seen[en][key] = val
        self.nwait += 1

    def _deps(self, en, reads, writes, own):
        for t in reads:
            if t.w is not None:
                self._wait(en, *t.w)
        for t in writes:
            if t.w is not None and t.w[0] != own:
                self._wait(en, *t.w)
            for k, v in t.r.items():
                if k != own:
                    self._wait(en, k, v)

    def op(self, en, reads, writes, make):
        self._deps(en, reads, writes, en)
        ins = make(self.engs[en])
        self.tot[en] += 1
        ins.then_inc(self.sems[en], 1)
        v = self.tot[en]
        for t in reads:
            t.r[en] = v
        for t in writes:
            t.w = (en, v)
            t.r = {}
        self.ninst += 1

    def dma(self, en, key, out_t, in_t, out_ap=None, in_ap=None, **kw):
        if key not in self.sems:
            self._mksem(key, True)
        self._deps(en, [in_t], [out_t], None)
        ins = self.engs[en].dma_start(out=out_ap if out_ap is not None else out_t.ap,
                                      in_=in_ap if in_ap is not None else in_t.ap, **kw)
        self.tot[key] += 16
        ins.then_inc(self.sems[key], 16)
        v = self.tot[key]
        in_t.r[key] = v
        out_t.w = (key, v)
        out_t.r = {}
        self.ninst += 1
        return (key, v)

    def barrier(self, pool=False):
        for en in self.engs:
            if en == "pool" and not pool:
                continue
            for key in self.sems:
                if (key == "pool" and not pool) or key.startswith("cv"):
                    continue
                if key != en and self.tot[key] > 0:
                    self._wait(en, key, self.tot[key])


class Ctx:
    pass


def build(n_layers=L, stop=None, dump=None, stopsub=None):
    nc = bass.Bass("TRN2", target_bir_lowering=False)
    fw = FW(nc)
    g = Ctx()
    g.nc, g.fw = nc, fw
    g.stopsub = stopsub

    def din(name, shape, dt=F32):
        return nc.dram_tensor(name, list(shape), dt, kind="ExternalInput").ap()

    def dscr(name, shape, dt):
        return nc.dram_tensor(name, list(shape), dt, kind="Internal").ap()

    I = {}
    I["xT"] = din("xT", [D, S])
    I["condc"] = din("condc", [128, 16])
    I["pos"] = din("pos", [S], I32)
    I["kconst"] = din("kconst", [128, 4])
    I["pvec"] = din("pvec", [L, 128, NP])
    I["w_ada"] = din("w_ada", [L, D, 9 * D])
    I["w_ff1_in"] = din("w_ff1_in", [L, D, 2 * DFF])
    I["w_ff1_out"] = din("w_ff1_out", [L, DFF, D])
    I["w_ff2_in"] = din("w_ff2_in", [L, D, 2 * DFF])
    I["w_ff2_out"] = din("w_ff2_out", [L, DFF, D])
    I["w_inx"] = din("w_inx", [L, D, NXB * 128])
    I["w_uqx"] = din("w_uqx", [L, 384, 1024])
    I["w_ukvx"] = din("w_ukvx", [L, 128, 1024])
    I["gla_w_gate"] = din("gla_w_gate", [L, 16, 256])
    I["s5_w_glu"] = din("s5_w_glu", [L, 512, 512])
    I["BT"] = din("BT", [L, 2, 128, 16, 128])
    I["CT"] = din("CT", [L, 2, 128, 16, 128])
    I["w_branch"] = din("w_branch", [L, 4, 512, D])
    I["w_gate"] = din("w_gate", [L, 4, D, D])
    I["w_out"] = din("w_out", [L, D, D])
    g.I = I
    g.IT = {k: T(v, k) for k, v in I.items()}
    y = nc.dram_tensor("y", [D, S], F32, kind="ExternalOutput").ap()
    g.y = y
    g.yT = [T(y[:, t * TT:(t + 1) * TT], f"y{t}") for t in range(NTT)]
    g.dump = dump
    if dump is not None:
        g.dbg = nc.dram_tensor("dbg", list(dump[1]), dump[2], kind="ExternalOutput").ap()
        g.dbgT = T(g.dbg, "dbg")

    with contextlib.ExitStack() as st:
        uid = [0]

        def sb(name, shape, dt, stack=st):
            uid[0] += 1
            return T(stack.enter_context(nc.sbuf_tensor(f"{name}_{uid[0]}", list(shape), dt))[:], name)

        g.sb = sb
        g.ps = [T(st.enter_context(nc.psum_tensor(f"ps{i}", [128, 512], F32))[:], f"ps{i}") for i in range(8)]
        setup_consts(g, st)
        setup_masks(g)
        g.cv_f = [sb(f"cvf{i}", [128, 8, 256], F32) for i in range(2)]
        g.cv_b = [sb(f"cvb{i}", [128, 8, 256], BF16) for i in range(2)]
        g.cvn = 0
        g.W = {}
        for l in range(n_layers):
            declare_weights(g, l)
        declare_scratch(g)
        compute_mod(g, n_layers)
        rope_tables(g)
        convert_ffn(g, 0, 1)
        for l in range(n_layers):
            convert_mixer(g, l)
            ffn_phase(g, l, 1, first=(l == 0))
            if stop == ("ffn1", l):
                break
            convert_ffn(g, l, 2)
            mixer_phase(g, l)
            if stop == ("mix", l):
                break
            if l + 1 < n_layers:
                convert_ffn(g, l + 1, 1)
            ffn_phase(g, l, 2, first=False)
        if dump is not None:
            fw.barrier()
            fw.dma("sp", "dump", g.dbgT, g.DT[dump[0]], in_ap=dump[3](g.D[dump[0]]))
        fw.barrier()
    return nc


def setup_consts(g, st):
    nc, fw, sb = g.nc, g.fw, g.sb
    g.ones_bf = sb("ones_bf", [128, 128], BF16)
    fw.op("dve", [], [g.ones_bf], lambda e: e.memset(g.ones_bf.ap, 1.0))
    g.modT = [sb(f"modT{l}", [128, 144], F32) for l in range(L)]
    g.ngT = [sb(f"ngT{l}", [128, 48], F32) for l in range(L)]
    g.AB = [sb(f"AB{l}", [128, 9 * 16], F32) for l in range(L)]
    g.eps_t = sb("eps_t", [128, 1], F32)
    fw.op("dve", [], [g.eps_t], lambda e: e.memset(g.eps_t.ap, EPS))


def declare_weights(g, l):
    nc = g.nc

    def scr(name, nblk, nk, cw=128):
        ap = nc.dram_tensor(f"{name}_{l}", [nblk, 128, nk, cw], BF16, kind="Internal").ap()
        g.W[(name, l)] = (ap, [T(ap[b], f"{name}{l}_{b}") for b in range(nblk)])

    scr("ff1a", NJ, NCH)
    scr("ff1b", NJ, NCH)
    scr("ff1o", NCH, NJ)
    scr("ff2a", NJ, NCH)
    scr("ff2b", NJ, NCH)
    scr("ff2o", NCH, NJ)
    scr("winx", 44, NCH)
    scr("winv", 2, NCH, 512)
    scr("winw", 1, NCH)
    scr("uq", 8, 3)
    scr("ukvk", 4, 1)
    scr("ukvv", 1, 1, 512)
    scr("glu", 4, 4)
    for i in range(4):
        scr(f"wg{i}", NCH, NCH)
        scr(f"wb{i}", NCH, 4)
    scr("wo", NCH, NCH)


def convert(g, name, l, src, nk, ncols, col0=0, cw=128):
    fw = g.fw
    dap, dts = g.W[(name, l)]
    srcT = g.IT[src[0]]
    sap = src[1]
    for c0 in range(0, ncols, 256):
        w = min(256, ncols - c0)
        for k0 in range(0, nk, 8):
            kk = min(8, nk - k0)
            i = g.cvn % 2
            g.cvn += 1
            f, b = g.cv_f[i], g.cv_b[i]
            sview = sap[k0 * 128:(k0 + kk) * 128, col0 + c0:col0 + c0 + w].rearrange("(k p) c -> p k c", p=128)
            fw.dma("pool", f"cvl{i}", f, srcT, out_ap=f.ap[:, 0:kk, 0:w], in_ap=sview)
            fw.op("pool", [f], [b], lambda e, f=f, b=b, kk=kk, w=w: e.tensor_copy(out=b.ap[:, 0:kk, 0:w], in_=f.ap[:, 0:kk, 0:w]))
            for s0 in range(0, w, 128):
                col = c0 + s0
                blk, off = col // cw, col % cw
                fw.dma("pool", f"cvs{i}", dts[blk], b, out_ap=dap[blk, :, k0:k0 + kk, off:off + 128],
                       in_ap=b.ap[:, 0:kk, s0:s0 + 128])


def convert_mixer(g, l):
    I = g.I
    convert(g, "winx", l, ("w_inx", I["w_inx"][l]), NCH, 44 * 128, 0)
    convert(g, "winv", l, ("w_inx", I["w_inx"][l]), NCH, 1024, 44 * 128, cw=512)
    convert(g, "winw", l, ("w_inx", I["w_inx"][l]), NCH, 128, 52 * 128)
    convert(g, "uq", l, ("w_uqx", I["w_uqx"][l]), 3, 1024, 0)
    convert(g, "ukvk", l, ("w_ukvx", I["w_ukvx"][l]), 1, 512, 0)
    convert(g, "ukvv", l, ("w_ukvx", I["w_ukvx"][l]), 1, 512, 512, cw=512)
    convert(g, "glu", l, ("s5_w_glu", I["s5_w_glu"][l]), 4, 512, 0)
    for i in range(4):
        convert(g, f"wg{i}", l, ("w_gate", I["w_gate"][l, i]), NCH, D, 0)
        convert(g, f"wb{i}", l, ("w_branch", I["w_branch"][l, i]), 4, D, 0)
    convert(g, "wo", l, ("w_out", I["w_out"][l]), NCH, D, 0)


def convert_ffn(g, l, which):
    wi = g.I[f"w_ff{which}_in"][l]
    wo = g.I[f"w_ff{which}_out"][l]
    convert(g, f"ff{which}a", l, (f"w_ff{which}_in", wi), NCH, DFF, 0)
    convert(g, f"ff{which}b", l, (f"w_ff{which}_in", wi), NCH, DFF, DFF)
    convert(g, f"ff{which}o", l, (f"w_ff{which}_out", wo), NJ, D, 0)


def compute_mod(g, n_layers):
    nc, fw = g.nc, g.fw
    with contextlib.ExitStack() as st:
        sb = lambda n, s, d: g.sb(n, s, d, st)
        cT = sb("cT", [128, 16], F32)
        cond = sb("cond", [128, 16], F32)
        fw.dma("sp", "misc", cT, g.IT["condc"])
        fw.op("act", [cT], [cond], lambda e: e.activation(out=cond.ap, in_=cT.ap, func=AF.Silu))
        slabs = [sb(f"adas{i}", [128, 16, 128], F32) for i in range(4)]
        bT = sb("bT", [128, 144], F32)
        n = 0
        for l in range(n_layers):
            fw.dma("sp", "misc", bT, g.IT["pvec"], in_ap=g.I["pvec"][l][:, 0:144])
            fw.dma("sp", "misc2", g.ngT[l], g.IT["pvec"], in_ap=g.I["pvec"][l][:, 144:192])
            acc = g.ps[0]
            for ch in range(144):
                sl = slabs[n % 4]
                fw.dma("sp", f"ada{n % 4}", sl, g.IT["w_ada"],
                       in_ap=g.I["w_ada"][l][:, ch * 128:(ch + 1) * 128].rearrange("(k p) c -> p k c", p=128))
                n += 1
                for k in range(16):
                    fw.op("pe", [sl, cond], [acc], lambda e, sl=sl, k=k, ch=ch: e.matmul(
                        acc.ap[:, ch:ch + 1], sl.ap[:, k, :], cond.ap[:, k:k + 1], start=(k == 0), stop=(k == 15)))
            m = g.modT[l]
            fw.op("dve", [acc, bT], [m], lambda e, m=m: e.tensor_tensor(out=m.ap, in0=acc.ap[:, 0:144], in1=bT.ap, op=ALU.add))
            AB = g.AB[l]
            ng = g.ngT[l]
            for i in range(3):
                sh = m.ap[:, (3 * i) * 16:(3 * i + 1) * 16]
                sc = m.ap[:, (3 * i + 1) * 16:(3 * i + 2) * 16]
                gt = m.ap[:, (3 * i + 2) * 16:(3 * i + 3) * 16]
                A = AB.ap[:, (3 * i) * 16:(3 * i + 1) * 16]
                Bv = AB.ap[:, (3 * i + 1) * 16:(3 * i + 2) * 16]
                G = AB.ap[:, (3 * i + 2) * 16:(3 * i + 3) * 16]
                gi = ng.ap[:, i * 16:(i + 1) * 16]
                fw.op("dve", [m, ng], [AB], lambda e, A=A, sc=sc, gi=gi: e.scalar_tensor_tensor(
                    out=A, in0=sc, scalar=1.0, in1=gi, op0=ALU.add, op1=ALU.mult))
                fw.op("dve", [m], [AB], lambda e, Bv=Bv, sh=sh: e.tensor_copy(out=Bv, in_=sh))
                fw.op("dve", [m], [AB], lambda e, G=G, gt=gt, i=i: e.tensor_scalar(
                    out=G, in0=gt, scalar1=(1.0 if i == 1 else 0.5), scalar2=None, op0=ALU.mult))
        fw.barrier()


def ada_norm_tile(g, l, i, xt, h, sq, rstd, ps_ss):
    fw = g.fw
    AB = g.AB[l]
    for c in range(NCH):
        fw.op("act", [xt], [sq], lambda e, c=c: e.activation(out=sq.ap[:, c, :], in_=xt.ap[:, c, :], func=AF.Square))
    for c in range(NCH):
        fw.op("pe", [sq, g.ones_bf], [ps_ss], lambda e, c=c: e.matmul(
            ps_ss.ap, g.ones_bf.ap, sq.ap[:, c, :], start=(c == 0), stop=(c == NCH - 1)))
    fw.op("act", [ps_ss, g.eps_t], [rstd], lambda e: e.activation(
        out=rstd.ap, in_=ps_ss.ap, func=AF.Sqrt, bias=g.eps_t.ap[:, 0:1], scale=1.0 / D))
    fw.op("dve", [rstd], [rstd], lambda e: e.reciprocal(out=rstd.ap, in_=rstd.ap))
    for c in range(NCH):
        A = AB.ap[:, (3 * i) * 16 + c:(3 * i) * 16 + c + 1]
        Bv = AB.ap[:, (3 * i + 1) * 16 + c:(3 * i + 1) * 16 + c + 1]
        fw.op("dve", [xt, rstd, AB], [sq], lambda e, c=c, A=A: e.scalar_tensor_tensor(
            out=sq.ap[:, c, :], in0=xt.ap[:, c, :], scalar=A, in1=rstd.ap, op0=ALU.mult, op1=ALU.mult))
        fw.op("act", [sq, AB], [h], lambda e, c=c, Bv=Bv: e.activation(
            out=h.ap[:, c, :], in_=sq.ap[:, c, :], func=AF.Identity, bias=Bv, scale=1.0))


def ffn_phase(g, l, which, first):
    nc, fw = g.nc, g.fw
    i = 0 if which == 1 else 2
    _, wa = g.W[(f"ff{which}a", l)]
    _, wb = g.W[(f"ff{which}b", l)]
    _, wo = g.W[(f"ff{which}o", l)]
    AB = g.AB[l]
    with contextlib.ExitStack() as st:
        sb = lambda n, s, d: g.sb(n, s, d, st)
        xt = sb("f_xt", [128, NCH, TT], F32)
        sq = sb("f_sq", [128, NCH, TT], BF16)
        h = sb("f_h", [128, NCH, TT], BF16)
        u = sb("f_u", [128, NJ, TT], BF16)
        rstd = sb("f_rstd", [128, TT], F32)
        sil = [sb(f"f_sil{k}", [128, TT], F32) for k in range(2)]
        wab = [sb(f"f_wab{k}", [128, 2, NCH, 128], BF16) for k in range(3)]
        wos = [sb(f"f_wo{k}", [128, NJ, 128], BF16) for k in range(2)]
        nwa = 0
        nwo = 0
        for tt in range(NTT):
            src_t = g.IT["xT"] if first else g.yT[tt]
            src_ap = (g.I["xT"] if first else g.y)[:, tt * TT:(tt + 1) * TT].rearrange("(c p) t -> p c t", p=128)
            fw.dma("sp", "f_xl", xt, src_t, in_ap=src_ap)
            ada_norm_tile(g, l, i, xt, h, sq, rstd, g.ps[7])
            for j in range(NJ):
                wt = wab[nwa % 3]
                nwa += 1
                fw.dma("sp", f"f_wa{nwa % 3}", wt, wa[j], out_ap=wt.ap[:, 0])
                fw.dma("sp", f"f_wb{nwa % 3}", wt, wb[j], out_ap=wt.ap[:, 1])
                pa, pb = g.ps[(2 * j) % 6], g.ps[(2 * j + 1) % 6]
                for k in range(NCH):
                    fw.op("pe", [wt, h], [pa], lambda e, wt=wt, k=k, pa=pa: e.matmul(
                        pa.ap, wt.ap[:, 0, k, :], h.ap[:, k, :], start=(k == 0), stop=(k == NCH - 1)))
                for k in range(NCH):
                    fw.op("pe", [wt, h], [pb], lambda e, wt=wt, k=k, pb=pb: e.matmul(
                        pb.ap, wt.ap[:, 1, k, :], h.ap[:, k, :], start=(k == 0), stop=(k == NCH - 1)))
                sl = sil[j % 2]
                fw.op("act", [pa], [sl], lambda e, sl=sl, pa=pa: e.activation(out=sl.ap, in_=pa.ap, func=AF.Silu))
                fw.op("dve", [sl, pb], [u], lambda e, sl=sl, pb=pb, j=j: e.tensor_tensor(
                    out=u.ap[:, j, :], in0=sl.ap, in1=pb.ap, op=ALU.mult))
            for m in range(NCH):
                wt = wos[nwo % 2]
                nwo += 1
                fw.dma("sp", f"f_wo{nwo % 2}", wt, wo[m])
                po = g.ps[m % 6]
                for j in range(NJ):
                    fw.op("pe", [wt, u], [po], lambda e, wt=wt, j=j, po=po: e.matmul(
                        po.ap, wt.ap[:, j, :], u.ap[:, j, :], start=(j == 0), stop=(j == NJ - 1)))
                G = AB.ap[:, (3 * i + 2) * 16 + m:(3 * i + 2) * 16 + m + 1]
                fw.op("dve", [po, xt, AB], [xt], lambda e, po=po, m=m, G=G: e.scalar_tensor_tensor(
                    out=xt.ap[:, m, :], in0=po.ap, scalar=G, in1=xt.ap[:, m, :], op0=ALU.mult, op1=ALU.add))
            fw.dma("sp", "f_xs", g.yT[tt], xt,
                   out_ap=g.y[:, tt * TT:(tt + 1) * TT].rearrange("(c p) t -> p c t", p=128))
        fw.barrier()


def declare_scratch(g):
    nc = g.nc

    def scr(name, shape, dt):
        ap = nc.dram_tensor("sc_" + name, list(shape), dt, kind="Internal").ap()
        g.D[name] = ap
        g.DT[name] = T(ap, name)

    g.D, g.DT = {}, {}
    scr("tab", [4, 128, S], F32)
    scr("hT", [D, S], BF16)
    scr("qA", [4, 128, S], BF16)
    scr("kA", [4, 128, S], BF16)
    scr("vA", [S, 512], BF16)
    scr("iq", [4, 128, S], BF16)
    scr("ik", [128, S], BF16)
    scr("iw", [S, 8], F32)
    scr("gq", [4, 64, S], F32)
    scr("gk", [4, 64, S], F32)
    scr("gv", [S, 512], BF16)
    scr("glr", [16, S], BF16)
    scr("gr", [4, 128, S], BF16)
    scr("su", [4, 128, S], F32)
    scr("qn", [4, 128, S], BF16)
    scr("qr", [4, 64, S], BF16)
    scr("kn", [4, 128, S], BF16)
    scr("mv", [S, 512], BF16)
    scr("kr", [64, S], BF16)
    scr("br", [4, 4, 128, S], BF16)


TWO_PI = float(2 * math.pi)
PI = float(math.pi)


def sin_reduced(fw, ki, kf, x, out):
    fw.op("dve", [x], [kf], lambda e: e.tensor_scalar(out=kf.ap, in0=x.ap, scalar1=1.0 / TWO_PI, scalar2=None, op0=ALU.mult))
    fw.op("dve", [kf], [ki], lambda e: e.tensor_copy(out=ki.ap, in_=kf.ap))
    fw.op("dve", [ki], [kf], lambda e: e.tensor_copy(out=kf.ap, in_=ki.ap))
    fw.op("dve", [kf, x], [out], lambda e: e.scalar_tensor_tensor(out=out.ap, in0=kf.ap, scalar=-TWO_PI, in1=x.ap, op0=ALU.mult, op1=ALU.add))
    fw.op("dve", [out], [kf], lambda e: e.tensor_scalar(out=kf.ap, in0=out.ap, scalar1=PI, scalar2=-TWO_PI, op0=ALU.is_gt, op1=ALU.mult))
    fw.op("dve", [out, kf], [out], lambda e: e.tensor_tensor(out=out.ap, in0=out.ap, in1=kf.ap, op=ALU.add))
    fw.op("dve", [out], [kf], lambda e: e.tensor_scalar(out=kf.ap, in0=out.ap, scalar1=-PI, scalar2=TWO_PI, op0=ALU.is_lt, op1=ALU.mult))
    fw.op("dve", [out, kf], [out], lambda e: e.tensor_tensor(out=out.ap, in0=out.ap, in1=kf.ap, op=ALU.add))
    fw.op("act", [out], [out], lambda e: e.activation(out=out.ap, in_=out.ap, func=AF.Sin))


def rope_tables(g):
    nc, fw = g.nc, g.fw
    with contextlib.ExitStack() as st:
        sb = lambda n, s, d: g.sb(n, s, d, st)
        posi = sb("r_posi", [128, S], I32)
        posf = sb("r_posf", [128, S], F32)
        ang = sb("r_ang", [128, S], F32)
        ki = sb("r_ki", [128, S], I32)
        kf = sb("r_kf", [128, S], F32)
        out = sb("r_out", [128, S], F32)
        kc = sb("r_kc", [128, 4], F32)
        fw.dma("sp", "misc", kc, g.IT["kconst"])
        fw.dma("sp", "misc2", posi, g.IT["pos"], in_ap=g.I["pos"].partition_broadcast(128))
        fw.op("dve", [posi], [posf], lambda e: e.tensor_copy(out=posf.ap, in_=posi.ap))
        for t in range(2):
            invf = kc.ap[:, 2 * t:2 * t + 1]
            sgn = kc.ap[:, 2 * t + 1:2 * t + 2]
            fw.op("dve", [posf, kc], [ang], lambda e, invf=invf: e.tensor_scalar(out=ang.ap, in0=posf.ap, scalar1=invf, scalar2=None, op0=ALU.mult))
            sin_reduced(fw, ki, kf, ang, out)
            fw.op("dve", [out, kc], [out], lambda e, sgn=sgn: e.tensor_scalar(out=out.ap, in0=out.ap, scalar1=sgn, scalar2=None, op0=ALU.mult))
            fw.dma("sp", "r_st", g.DT["tab"], out, out_ap=g.D["tab"][2 * t + 1])
            fw.op("dve", [ang], [ang], lambda e: e.tensor_scalar(out=ang.ap, in0=ang.ap, scalar1=PI / 2, scalar2=None, op0=ALU.add))
            sin_reduced(fw, ki, kf, ang, out)
            fw.dma("sp", "r_st", g.DT["tab"], out, out_ap=g.D["tab"][2 * t])
        fw.barrier()


def setup_masks(g):
    fw = g.fw
    sb = g.sb
    onef = sb("c_onef", [128, 512], F32)
    fw.op("pool", [], [onef], lambda e: e.memset(onef.ap, 1.0))
    g.onef = onef
    zf = sb("c_zf", [128, 128], F32)
    fw.op("pool", [], [zf], lambda e: e.memset(zf.ap, 0.0))
    g.ident = sb("c_ident", [128, 128], BF16)
    fw.op("pool", [onef], [g.ident], lambda e: e.affine_select(
        out=g.ident.ap, in_=onef.ap[:, 0:128], pattern=[[1, 128]], compare_op=ALU.is_equal, fill=0.0, base=0, channel_multiplier=-1))
    g.cmask = []
    for r in range(4):
        m = sb(f"c_cm{r}", [128, 512], BF16)
        fw.op("pool", [onef], [m], lambda e, m=m, r=r: e.affine_select(
            out=m.ap, in_=onef.ap, pattern=[[1, 512]], compare_op=ALU.is_ge, fill=0.0, base=-128 * r, channel_multiplier=-1))
        g.cmask.append(m)
    g.gmask = sb("c_gm", [128, 128], BF16)
    fw.op("pool", [onef], [g.gmask], lambda e: e.affine_select(
        out=g.gmask.ap, in_=onef.ap[:, 0:128], pattern=[[1, 128]], compare_op=ALU.is_ge, fill=0.0, base=0, channel_multiplier=-1))
    fw.op("pool", [], [g.gmask], lambda e: e.memset(g.gmask.ap[0:64, 64:128], 0.0))
    g.dbias = sb("c_db", [128, 128], F32)
    fw.op("pool", [zf], [g.dbias], lambda e: e.affine_select(
        out=g.dbias.ap, in_=zf.ap, pattern=[[-1, 128]], compare_op=ALU.is_ge, fill=-1e30, base=0, channel_multiplier=1))
    g.rmask = sb("c_rm", [64, 512], F32)
    fw.op("pool", [], [g.rmask], lambda e: e.memset(g.rmask.ap, 1.0))
    fw.op("pool", [], [g.rmask], lambda e: e.memset(g.rmask.ap.rearrange("p (c j) -> p c j", j=64)[:, :, 0:1], 0.0))
    g.pw2 = sb("c_pw2", [128, NITER + 2], F32)
    for k in range(NITER + 2):
        fw.op("pool", [], [g.pw2], lambda e, k=k: e.memset(g.pw2.ap[:, k:k + 1], float(2.0 ** (1 - k))))
    g.jrow = sb("c_jrow", [128, 128], F32)
    fw.op("pool", [], [g.jrow], lambda e: e.iota(g.jrow.ap, pattern=[[1, 128]], base=0, channel_multiplier=0,
                                                  allow_small_or_imprecise_dtypes=True))


def mixer_phase(g, l):
    proj_phase(g, l)
    if g.stopsub == "proj":
        return
    mla_phase(g, l)
    if g.stopsub == "mla":
        return
    dsa_phase(g, l)
    if g.stopsub == "dsa":
        return
    gla_phase(g, l)
    if g.stopsub == "gla":
        return
    s5_phase(g, l)
    if g.stopsub == "s5":
        return
    merge_phase(g, l)


def proj_phase(g, l):
    nc, fw = g.nc, g.fw
    D_, DT = g.D, g.DT
    Wx = g.W[("winx", l)][1]
    Wv = g.W[("winv", l)][1]
    Ww = g.W[("winw", l)][1]
    Wuq = g.W[("uq", l)][1]
    Wkk = g.W[("ukvk", l)][1]
    Wkv = g.W[("ukvv", l)][1]
    ps = g.ps
    with contextlib.ExitStack() as st:
        sb = lambda n, s, d: g.sb(n, s, d, st)
        xt = sb("p_xt", [128, NCH, TT], F32)
        sq = sb("p_sq", [128, NCH, TT], BF16)
        h = sb("p_h", [128, NCH, TT], BF16)
        rstd = sb("p_rstd", [128, TT], F32)
        wsl = [sb(f"p_wsl{i}", [128, NCH, 128], BF16) for i in range(3)]
        wvs = [sb(f"p_wv{i}", [128, NCH, 512], BF16) for i in range(2)]
        wws = sb("p_ww", [128, NCH, 128], BF16)
        wuk = sb("p_wuk", [128, 1, 512], BF16)
        tabs = sb("p_tabs", [128, 4, TT], F32)
        t1 = sb("p_t1", [128, TT], F32)
        t2 = sb("p_t2", [128, TT], F32)
        sqb = sb("p_sqb", [128, TT], BF16)
        rs = sb("p_rs", [128, TT], F32)
        obs = [sb(f"p_ob{i}", [128, TT], BF16) for i in range(6)]
        ofs = [sb(f"p_of{i}", [128, TT], F32) for i in range(4)]
        ql = sb("p_ql", [128, 3, TT], BF16)
        kvl = sb("p_kvl", [128, TT], BF16)
        gn = sb("p_gn", [128, 16], F32)
        fw.dma("sp", "misc", gn, g.IT["pvec"], in_ap=g.I["pvec"][l][:, 192:208])
        fw.dma("sp", "misc2", wws, Ww[0])
        fw.dma("sp", "misc3", wuk, Wkv[0])
        cnt = {"w": 0, "ob": 0, "of": 0, "wv": 0}

        pend = []
        sets = [(ps[0], ps[1], ps[6]), (ps[2], ps[3], ps[5])]
        cur = {"s": 0}

        def nxt():
            cur["s"] ^= 1
            return sets[cur["s"]]

        def flush(min_age):
            while pend and pend[0][0] >= min_age:
                pend.pop(0)[1]()

        def defer_dma(key, dstT, o, out_ap=None, in_ap=None):
            pend.append([0, lambda: fw.dma("sp", key, dstT, o, out_ap=out_ap, in_ap=in_ap)])
            while len(pend) > 3:
                pend.pop(0)[1]()

        def slab(blkT, nk=NCH):
            w = wsl[cnt["w"] % 3]
            fw.dma("sp", f"p_w{cnt['w'] % 3}", w, blkT, out_ap=w.ap[:, 0:nk, :])
            cnt["w"] += 1
            for e_ in pend:
                e_[0] += 1
            flush(2)
            return w

        def mm(w, pst, M, rhs_fn, nk=NCH, c0=0):
            for k in range(nk):
                fw.op("pe", [w, h, ql, kvl], [pst], lambda e, k=k: e.matmul(
                    pst.ap[0:M, :], w.ap[:, k, c0:c0 + M], rhs_fn(k), start=(k == 0), stop=(k == nk - 1)))

        rh = lambda k: h.ap[:, k, :]

        def nob():
            o = obs[cnt["ob"] % 6]
            cnt["ob"] += 1
            return o, f"p_ob{cnt['ob'] % 6}"

        def nof():
            o = ofs[cnt["of"] % 4]
            cnt["of"] += 1
            return o, f"p_of{cnt['of'] % 4}"

        def norm_rs(px, M, d, pss=None):
            pss = pss if pss is not None else ps[6]
            fw.op("act", [px], [sqb], lambda e: e.activation(out=sqb.ap[0:M, :], in_=px.ap[0:M, :], func=AF.Square))
            fw.op("pe", [sqb, g.ones_bf], [pss], lambda e: e.matmul(
                pss.ap[0:M, :], g.ones_bf.ap[0:M, 0:M], sqb.ap[0:M, :], start=True, stop=True))
            fw.op("act", [pss, g.eps_t], [rs], lambda e: e.activation(
                out=rs.ap[0:M, :], in_=pss.ap[0:M, :], func=AF.Sqrt, bias=g.eps_t.ap[0:M, 0:1], scale=1.0 / d))
            fw.op("dve", [rs], [rs], lambda e: e.reciprocal(out=rs.ap[0:M, :], in_=rs.ap[0:M, :]))

        def rope_out(px, pxs, M, gcol, ti, norm_d, dst_ap, dstT, pss=None):
            if norm_d:
                norm_rs(px, M, norm_d, pss)
            ga = gn.ap[0:M, gcol:gcol + 1] if gcol is not None else 1.0
            gs = gn.ap[0:M, gcol + 1:gcol + 2] if gcol is not None else 1.0
            fw.op("dve", [px, gn, tabs], [t1], lambda e: e.scalar_tensor_tensor(
                out=t1.ap[0:M, :], in0=px.ap[0:M, :], scalar=ga, in1=tabs.ap[0:M, ti, :], op0=ALU.mult, op1=ALU.mult))
            fw.op("dve", [pxs, gn, tabs], [t2], lambda e: e.scalar_tensor_tensor(
                out=t2.ap[0:M, :], in0=pxs.ap[0:M, :], scalar=gs, in1=tabs.ap[0:M, ti + 1, :], op0=ALU.mult, op1=ALU.mult))
            o, key = nob()
            if norm_d:
                fw.op("dve", [t1, t2], [t1], lambda e: e.tensor_tensor(out=t1.ap[0:M, :], in0=t1.ap[0:M, :], in1=t2.ap[0:M, :], op=ALU.add))
                fw.op("dve", [t1, rs], [o], lambda e: e.tensor_tensor(out=o.ap[0:M, :], in0=t1.ap[0:M, :], in1=rs.ap[0:M, :], op=ALU.mult))
            else:
                fw.op("dve", [t1, t2], [o], lambda e: e.tensor_tensor(out=o.ap[0:M, :], in0=t1.ap[0:M, :], in1=t2.ap[0:M, :], op=ALU.add))
            defer_dma(key, dstT, o, out_ap=dst_ap, in_ap=o.ap[0:M, :])

        def norm_out(px, M, gcol, d, dst_ap, dstT, keep=None):
            norm_rs(px, M, d)
            if keep is not None:
                o, key = keep, None
            else:
                o, key = nob()
            fw.op("dve", [px, gn, rs], [o], lambda e: e.scalar_tensor_tensor(
                out=(o.ap[0:M, :] if keep is None else keep.ap), in0=px.ap[0:M, :], scalar=gn.ap[0:M, gcol:gcol + 1],
                in1=rs.ap[0:M, :], op0=ALU.mult, op1=ALU.mult))
            if keep is None:
                defer_dma(key, dstT, o, out_ap=dst_ap, in_ap=o.ap[0:M, :])

        for tt in range(NTT):
            flush(0)
            cs = slice(tt * TT, (tt + 1) * TT)
            fw.dma("sp", "p_xl", xt, g.yT[tt], in_ap=g.y[:, cs].rearrange("(c p) t -> p c t", p=128))
            fw.dma("sp", "p_tab", tabs, DT["tab"], in_ap=D_["tab"][:, :, cs].rearrange("f p t -> p f t"))
            ada_norm_tile(g, l, 1, xt, h, sq, rstd, ps[7])
            fw.dma("sp", "p_hs", DT["hT"], h, out_ap=D_["hT"][:, cs].rearrange("(c p) t -> p c t", p=128))
            for nm, b0, gcol in (("qA", 0, 0), ("kA", 8, 2)):
                for hd in range(4):
                    pa_, pb_, pc_ = nxt()
                    w = slab(Wx[b0 + hd]); mm(w, pa_, 128, rh)
                    w = slab(Wx[b0 + 4 + hd]); mm(w, pb_, 128, rh)
                    rope_out(pa_, pb_, 128, gcol, 0, 128, D_[nm][hd][:, cs], DT[nm], pc_)
            for c in range(4):
                pa_, pb_, pc_ = nxt()
                w = slab(Wx[16 + c]); mm(w, pa_, 128, rh)
                w = slab(Wx[20 + c]); mm(w, pb_, 128, rh)
                rope_out(pa_, pb_, 128, None, 2, 0, D_["iq"][c][:, cs], DT["iq"], pc_)
            pa_, pb_, pc_ = nxt()
            w = slab(Wx[24]); mm(w, pa_, 128, rh)
            w = slab(Wx[25]); mm(w, pb_, 128, rh)
            rope_out(pa_, pb_, 128, None, 2, 0, D_["ik"][:, cs], DT["ik"], pc_)
            for c in range(4):
                w = slab(Wx[26 + c]); mm(w, ps[2 + c % 2], 128, rh)
                o, key = nob()
                fw.op("act", [ps[2 + c % 2]], [o], lambda e, o=o, c=c: e.activation(out=o.ap, in_=ps[2 + c % 2].ap, func=AF.Silu))
                defer_dma(key, DT["gr"], o, out_ap=D_["gr"][c][:, cs])
            for c in range(4):
                w = slab(Wx[30 + c]); mm(w, ps[2 + c % 2], 128, rh)
                o, key = nof()
                fw.op("act", [ps[2 + c % 2]], [o], lambda e, o=o, c=c: e.activation(out=o.ap, in_=ps[2 + c % 2].ap, func=AF.Copy))
                defer_dma(key, DT["su"], o, out_ap=D_["su"][c][:, cs])
            for nm, b0, scl in (("gq", 39, 0.125), ("gk", 41, 1.0)):
                for hd in range(4):
                    if hd % 2 == 0:
                        w = slab(Wx[b0 + hd // 2])
                    pp = ps[2 + hd % 2]
                    mm(w, pp, 64, rh, c0=(hd % 2) * 64)
                    o, key = nof()
                    fw.op("act", [pp], [o], lambda e, o=o, pp=pp, scl=scl: e.activation(
                        out=o.ap[0:64, :], in_=pp.ap[0:64, :], func=AF.Copy, scale=scl))
                    defer_dma(key, DT[nm], o, out_ap=D_[nm][hd][:, cs], in_ap=o.ap[0:64, :])
            w = slab(Wx[43]); mm(w, ps[2], 16, rh)
            o, key = nob()
            fw.op("act", [ps[2]], [o], lambda e, o=o: e.activation(out=o.ap[0:16, :], in_=ps[2].ap[0:16, :], func=AF.Copy))
            defer_dma(key, DT["glr"], o, out_ap=D_["glr"][:, cs], in_ap=o.ap[0:16, :])
            for vi, nm in ((0, "vA"), (1, "gv")):
                wv = wvs[cnt["wv"] % 2]
                fw.dma("sp", f"p_wv{cnt['wv'] % 2}", wv, Wv[vi])
                cnt["wv"] += 1
                for sub in range(4):
                    pp = ps[2 + sub % 2]
                    for k in range(NCH):
                        fw.op("pe", [wv, h], [pp], lambda e, k=k, pp=pp, sub=sub, wv=wv: e.matmul(
                            pp.ap, h.ap[:, k, sub * 128:(sub + 1) * 128], wv.ap[:, k, :], start=(k == 0), stop=(k == NCH - 1)))
                    o, key = nob()
                    fw.op("act", [pp], [o], lambda e, o=o, pp=pp: e.activation(out=o.ap, in_=pp.ap, func=AF.Copy))
                    defer_dma(key, DT[nm], o, out_ap=D_[nm][tt * TT + sub * 128:tt * TT + (sub + 1) * 128, :])
            for sub in range(4):
                pp = ps[2 + sub % 2]
                for k in range(NCH):
                    fw.op("pe", [wws, h], [pp], lambda e, k=k, pp=pp, sub=sub: e.matmul(
                        pp.ap[:, 0:8], h.ap[:, k, sub * 128:(sub + 1) * 128], wws.ap[:, k, 0:8], start=(k == 0), stop=(k == NCH - 1)))
                o, key = nof()
                fw.op("act", [pp], [o], lambda e, o=o, pp=pp: e.activation(
                    out=o.ap[:, 0:8], in_=pp.ap[:, 0:8], func=AF.Copy, scale=float(8 ** -0.5 * 64 ** -0.5)))
                defer_dma(key, DT["iw"], o, out_ap=D_["iw"][tt * TT + sub * 128:tt * TT + (sub + 1) * 128, :], in_ap=o.ap[:, 0:8])
            for c in range(3):
                w = slab(Wx[34 + c]); mm(w, ps[c], 128, rh)
            for c in range(3):
                fw.op("act", [ps[c]], [sqb], lambda e, c=c: e.activation(out=sqb.ap, in_=ps[c].ap, func=AF.Square))
                fw.op("pe", [sqb, g.ones_bf], [ps[6]], lambda e, c=c: e.matmul(
                    ps[6].ap, g.ones_bf.ap, sqb.ap, start=(c == 0), stop=(c == 2)))
            fw.op("act", [ps[6], g.eps_t], [rs], lambda e: e.activation(
                out=rs.ap, in_=ps[6].ap, func=AF.Sqrt, bias=g.eps_t.ap[:, 0:1], scale=1.0 / 384))
            fw.op("dve", [rs], [rs], lambda e: e.reciprocal(out=rs.ap, in_=rs.ap))
            for c in range(3):
                fw.op("dve", [ps[c], gn, rs], [ql], lambda e, c=c: e.scalar_tensor_tensor(
                    out=ql.ap[:, c, :], in0=ps[c].ap, scalar=gn.ap[:, 4 + c:5 + c], in1=rs.ap, op0=ALU.mult, op1=ALU.mult))
            rq = lambda k: ql.ap[:, k, :]
            for hd in range(4):
                w = slab(Wuq[hd], 3); mm(w, ps[0], 128, rq, nk=3)
                norm_out(ps[0], 128, 8, 128, D_["qn"][hd][:, cs], DT["qn"])
                w = slab(Wuq[4 + hd], 3)
                mm(w, ps[0], 64, rq, nk=3, c0=0)
                mm(w, ps[1], 64, rq, nk=3, c0=64)
                rope_out(ps[0], ps[1], 64, 9, 2, 64, D_["qr"][hd][:, cs], DT["qr"])
            w = slab(Wx[37]); mm(w, ps[0], 128, rh)
            norm_out(ps[0], 128, 7, 128, None, None, keep=kvl)
            rk = lambda k: kvl.ap
            for hd in range(4):
                w = slab(Wkk[hd], 1); mm(w, ps[hd % 2], 128, rk, nk=1)
                norm_out(ps[hd % 2], 128, 11, 128, D_["kn"][hd][:, cs], DT["kn"])
            for sub in range(4):
                pp = ps[2 + sub % 2]
                fw.op("pe", [wuk, kvl], [pp], lambda e, pp=pp, sub=sub: e.matmul(
                    pp.ap, kvl.ap[:, sub * 128:(sub + 1) * 128], wuk.ap[:, 0, :], start=True, stop=True))
                o, key = nob()
                fw.op("act", [pp], [o], lambda e, o=o, pp=pp: e.activation(out=o.ap, in_=pp.ap, func=AF.Copy))
                defer_dma(key, DT["mv"], o, out_ap=D_["mv"][tt * TT + sub * 128:tt * TT + (sub + 1) * 128, :])
            w = slab(Wx[38])
            mm(w, ps[0], 64, rh, c0=0)
            mm(w, ps[1], 64, rh, c0=64)
            rope_out(ps[0], ps[1], 64, 12, 2, 64, D_["kr"][:, cs], DT["kr"])
        flush(0)
        fw.barrier()


def attn_core(g, st, name, qT_loader, kT, kr, v, nq_parts, scale, bri, qt_mask_fn):
    pass


def mla_phase(g, l):
    nc, fw = g.nc, g.fw
    D_, DT = g.D, g.DT
    ps = g.ps
    sc = float(192 ** -0.5)
    with contextlib.ExitStack() as st:
        sb = lambda n, s, d: g.sb(n, s, d, st)
        kn = sb("m_kn", [128, 4, S], BF16)
        kr = sb("m_kr", [64, S], BF16)
        v = sb("m_v", [128, 32, 512], BF16)
        qn = [sb(f"m_qn{i}", [128, 4, TT], BF16) for i in range(2)]
        qr = [sb(f"m_qr{i}", [64, 4, TT], BF16) for i in range(2)]
        pt = [sb(f"m_pt{i}", [128, TT], BF16) for i in range(3)]
        rd = sb("m_rd", [128, TT], F32)
        ob = [sb(f"m_ob{i}", [128, TT], BF16) for i in range(2)]
        fw.dma("sp", "m_l0", kn, DT["kn"], in_ap=D_["kn"].rearrange("h p t -> p h t"))
        fw.dma("sp", "m_l1", kr, DT["kr"])
        fw.dma("sp", "m_l2", v, DT["mv"], in_ap=D_["mv"].rearrange("(k p) c -> p k c", p=128))
        np_ = 0
        no = 0
        for qt in range(NTT):
            cs = slice(qt * TT, (qt + 1) * TT)
            q1, q2 = qn[qt % 2], qr[qt % 2]
            fw.dma("sp", f"m_q{qt % 2}", q1, DT["qn"], in_ap=D_["qn"][:, :, cs].rearrange("h p t -> p h t"))
            fw.dma("sp", f"m_r{qt % 2}", q2, DT["qr"], in_ap=D_["qr"][:, :, cs].rearrange("h p t -> p h t"))
            nkt = 4 * qt + 4
            for hd in range(4):
                O, Dn = ps[2 + hd % 2], ps[4 + hd % 2]

                def emitL(kt, hd=hd, q1=q1, q2=q2):
                    Lp = ps[kt % 2]
                    ks = slice(kt * 128, (kt + 1) * 128)
                    fw.op("pe", [kn, q1], [Lp], lambda e: e.matmul(
                        Lp.ap, kn.ap[:, hd, ks], q1.ap[:, hd, :], start=True, stop=False))
                    fw.op("pe", [kr, q2], [Lp], lambda e: e.matmul(
                        Lp.ap, kr.ap[:, ks], q2.ap[:, hd, :], start=False, stop=True))

                emitL(0)
                for kt in range(nkt):
                    Lp = ps[kt % 2]
                    P = pt[np_ % 3]
                    np_ += 1
                    fw.op("act", [Lp], [P], lambda e, Lp=Lp, P=P: e.activation(out=P.ap, in_=Lp.ap, func=AF.Exp, scale=sc))
                    if kt + 1 < nkt:
                        emitL(kt + 1)
                    if kt >= 4 * qt:
                        cm = g.cmask[kt - 4 * qt]
                        fw.op("dve", [P, cm], [P], lambda e, P=P, cm=cm: e.tensor_tensor(out=P.ap, in0=P.ap, in1=cm.ap, op=ALU.mult))
                    fw.op("pe", [v, P], [O], lambda e, O=O, P=P, kt=kt, hd=hd: e.matmul(
                        O.ap, v.ap[:, kt, hd * 128:(hd + 1) * 128], P.ap, start=(kt == 0), stop=(kt == nkt - 1)))
                    fw.op("pe", [g.ones_bf, P], [Dn], lambda e, Dn=Dn, P=P, kt=kt: e.matmul(
                        Dn.ap, g.ones_bf.ap, P.ap, start=(kt == 0), stop=(kt == nkt - 1)))
                fw.op("dve", [Dn], [rd], lambda e, Dn=Dn: e.reciprocal(out=rd.ap, in_=Dn.ap))
                o = ob[no % 2]
                fw.op("dve", [O, rd], [o], lambda e, O=O, o=o: e.tensor_tensor(out=o.ap, in0=O.ap, in1=rd.ap, op=ALU.mult))
                fw.dma("sp", f"m_o{no % 2}", DT["br"], o, out_ap=D_["br"][3, hd][:, cs])
                no += 1
        fw.barrier(pool=True)


NITER = 14


def dsa_phase(g, l):
    nc, fw = g.nc, g.fw
    D_, DT = g.D, g.DT
    ps = g.ps
    sc = float(128 ** -0.5)
    with contextlib.ExitStack() as st:
        sb = lambda n, s, d: g.sb(n, s, d, st)
        kA = sb("a_k", [128, 2, S], BF16)
        vA = sb("a_v", [128, 32, 256], BF16)
        ik = sb("a_ik", [128, S], BF16)
        iqs = [sb(f"a_iq{i}", [128, 4, 128], BF16) for i in range(2)]
        maskTs = [sb(f"a_mT{i}", [128, 32, TT], BF16) for i in range(2)]
        score = sb("a_sc", [128, S], F32)
        maskq = sb("a_mq", [128, S], BF16)
        junk = maskq
        rl = [sb(f"a_rl{i}", [128, 512], F32) for i in range(2)]
        iw = [sb(f"a_iw{i}", [128, 8], F32) for i in range(2)]
        sm = sb("a_sm", [128, 8], F32)
        dk = sb("a_dk", [128, NITER + 2], F32)
        qA = [sb(f"a_q{i}", [128, 4, TT], BF16) for i in range(2)]
        pt = [sb(f"a_pt{i}", [128, TT], BF16) for i in range(4)]
        lnd = sb("a_lnd", [128, TT], F32)
        rd = [sb(f"a_rd{i}", [128, TT], F32) for i in range(2)]
        osb = [sb(f"a_os{i}", [128, TT], F32) for i in range(2)]
        ob = [sb(f"a_ob{i}", [128, TT], BF16) for i in range(2)]
        fw.dma("sp", "a_l2", ik, DT["ik"])
        d0, lo, mid, cn, stp = (sm.ap[:, i:i + 1] for i in range(5))
        ptb = ps[6].ap.bitcast(BF16)
        cnt = {"p": 0, "o": 0, "r": 0}
        Lb = [ps[0], ps[1], ps[7]]

        def indexer(qt):
            maskT = maskTs[qt % 2]
            fw.op("dve", [], [maskT], lambda e: e.memset(maskT.ap[:, 4 * qt:4 * qt + 4, :], 0.0))
            for qi in range(4):
                i = 4 * qt + qi
                nk = (i + 1) * 128
                qs = slice(i * 128, (i + 1) * 128)
                iwt = iw[i % 2]
                fw.dma("sp", f"a_iw{i % 2}", iwt, DT["iw"], in_ap=D_["iw"][qs, :])
                iq = iqs[i % 2]
                fw.dma("sp", f"a_iq{i % 2}", iq, DT["iq"], in_ap=D_["iq"][:, :, qs].rearrange("h p t -> p h t"))
                for kk in range((nk + 511) // 512):
                    w = min(512, nk - kk * 512)
                    ks = slice(kk * 512, kk * 512 + w)
                    for hh in range(8):
                        c, half = hh // 2, hh % 2
                        pr = slice(half * 64, half * 64 + 64)
                        Sp = SpB[hh % 2]
                        fw.op("pe", [iq, ik], [Sp], lambda e, Sp=Sp, pr=pr, c=c, iq=iq, ks=ks, w=w: e.matmul(
                            Sp.ap[:, 0:w], iq.ap[pr, c, :], ik.ap[pr, ks], start=True, stop=True))
                        r = rl[cnt["r"] % 2]
                        cnt["r"] += 1
                        fw.op("act", [Sp], [r], lambda e, Sp=Sp, r=r, w=w: e.activation(out=r.ap[:, 0:w], in_=Sp.ap[:, 0:w], func=AF.Relu))
                        if hh == 0:
                            fw.op("dve", [r, iwt], [score], lambda e, r=r, w=w, ks=ks, iwt=iwt: e.tensor_scalar(
                                out=score.ap[:, ks], in0=r.ap[:, 0:w], scalar1=iwt.ap[:, 0:1], scalar2=None, op0=ALU.mult))
                        else:
                            fw.op("dve", [r, iwt, score], [score], lambda e, r=r, w=w, ks=ks, iwt=iwt, hh=hh: e.scalar_tensor_tensor(
                                out=score.ap[:, ks], in0=r.ap[:, 0:w], scalar=iwt.ap[:, hh:hh + 1], in1=score.ap[:, ks],
                                op0=ALU.mult, op1=ALU.add))
                fw.op("dve", [score], [sm], lambda e, nk=nk: e.reduce_max(
                    out=d0, in_=score.ap[:, 0:nk], axis=AX.X, apply_absolute_value=True))
                fw.op("dve", [score, g.dbias], [score], lambda e, qs=qs: e.tensor_tensor(
                    out=score.ap[:, qs], in0=score.ap[:, qs], in1=g.dbias.ap, op=ALU.add))
                fw.op("dve", [sm], [sm], lambda e: e.tensor_scalar(out=d0, in0=d0, scalar1=1.0, scalar2=None, op0=ALU.add))
                fw.op("dve", [sm, g.pw2], [dk], lambda e: e.tensor_scalar(out=dk.ap, in0=g.pw2.ap, scalar1=d0, scalar2=None, op0=ALU.mult))
                fw.op("dve", [], [sm], lambda e: e.memset(mid, 0.0))
                for it in range(1, NITER + 1):
                    fw.op("dve", [score, sm], [maskq, sm], lambda e, nk=nk: e.tensor_scalar(
                        out=junk.ap[:, 0:nk], in0=score.ap[:, 0:nk], scalar1=mid, scalar2=0.0, op0=ALU.is_ge, op1=ALU.add, accum_out=cn))
                    fw.op("dve", [sm], [sm], lambda e: e.tensor_scalar(
                        out=stp, in0=cn, scalar1=255.5, scalar2=0.5, op0=ALU.is_ge, op1=ALU.subtract))
                    fw.op("dve", [sm, dk], [sm], lambda e, it=it: e.scalar_tensor_tensor(
                        out=mid, in0=stp, scalar=dk.ap[:, it:it + 1], in1=mid, op0=ALU.mult, op1=ALU.add))
                fw.op("dve", [sm, dk], [sm], lambda e: e.scalar_tensor_tensor(
                    out=lo, in0=dk.ap[:, NITER:NITER + 1], scalar=-0.5, in1=mid, op0=ALU.mult, op1=ALU.add))
                fw.op("dve", [score, sm], [maskq], lambda e, nk=nk: e.tensor_scalar(
                    out=maskq.ap[:, 0:nk], in0=score.ap[:, 0:nk], scalar1=lo, scalar2=None, op0=ALU.is_ge))
                for kb0 in range(0, i + 1, 4):
                    nb = min(4, i + 1 - kb0)
                    for b in range(nb):
                        kb = kb0 + b
                        fw.op("pe", [maskq, g.ident], [ps[6]], lambda e, kb=kb, b=b: e.transpose(
                            ptb[:, b * 128:(b + 1) * 128], maskq.ap[:, kb * 128:(kb + 1) * 128], g.ident.ap))
                    fw.op("act", [ps[6]], [maskT], lambda e, kb0=kb0, nb=nb, qi=qi: e.activation(
                        out=maskT.ap[:, kb0:kb0 + nb, qi * 128:(qi + 1) * 128],
                        in_=ptb[:, 0:nb * 128].rearrange("p (b c) -> p b c", c=128), func=AF.Copy))

        def attention(qt):
            maskT = maskTs[qt % 2]
            cs = slice(qt * TT, (qt + 1) * TT)
            q1 = qA[qt % 2]
            fw.dma("sp", f"a_q{qt % 2}", q1, DT["qA"], in_ap=D_["qA"][:, :, cs].rearrange("h p t -> p h t"))
            nkt = 4 * qt + 4
            for hd in range(4):
                O, Dn = ps[2], ps[3]
                hl = hd % 2
                if hl == 0:
                    fw.dma("sp", "a_l0", kA, DT["kA"], out_ap=kA.ap[:, :, 0:nkt * 128],
                           in_ap=D_["kA"][hd:hd + 2, :, 0:nkt * 128].rearrange("h p t -> p h t"))
                    fw.dma("sp", "a_l1", vA, DT["vA"], out_ap=vA.ap[:, 0:nkt, :],
                           in_ap=D_["vA"][0:nkt * 128, hd * 128:hd * 128 + 256].rearrange("(k p) c -> p k c", p=128))

                def emitL(kt, hd=hd, hl=hl):
                    Lp = Lb[kt % 3]
                    ks = slice(kt * 128, (kt + 1) * 128)
                    fw.op("pe", [kA, q1], [Lp], lambda e: e.matmul(
                        Lp.ap, kA.ap[:, hl, ks], q1.ap[:, hd, :], start=True, stop=True))

                emitL(0)
                if nkt > 1:
                    emitL(1)
                for kt in range(nkt):
                    Lp = Lb[kt % 3]
                    P = pt[cnt["p"] % 4]
                    cnt["p"] += 1
                    fw.op("act", [Lp], [P], lambda e, Lp=Lp, P=P: e.activation(out=P.ap, in_=Lp.ap, func=AF.Exp, scale=sc))
                    fw.op("pool", [P, maskT], [P], lambda e, P=P, kt=kt: e.tensor_tensor(
                        out=P.ap, in0=P.ap, in1=maskT.ap[:, kt, :], op=ALU.mult))
                    if kt + 2 < nkt:
                        emitL(kt + 2)
                    fw.op("pe", [vA, P], [O], lambda e, P=P, kt=kt, hl=hl: e.matmul(
                        O.ap, vA.ap[:, kt, hl * 128:(hl + 1) * 128], P.ap, start=(kt == 0), stop=(kt == nkt - 1)))
                    fw.op("pe", [g.ones_bf, P], [Dn], lambda e, P=P, kt=kt: e.matmul(
                        Dn.ap, g.ones_bf.ap, P.ap, start=(kt == 0), stop=(kt == nkt - 1)))
                r_, o_, ob_ = rd[cnt["o"] % 2], osb[cnt["o"] % 2], ob[cnt["o"] % 2]
                fw.op("act", [Dn], [lnd], lambda e: e.activation(out=lnd.ap, in_=Dn.ap, func=AF.Ln))
                fw.op("act", [lnd], [r_], lambda e, r_=r_: e.activation(out=r_.ap, in_=lnd.ap, func=AF.Exp, scale=-1.0))
                fw.op("act", [O], [o_], lambda e, o_=o_: e.activation(out=o_.ap, in_=O.ap, func=AF.Copy))
                fw.op("pool", [o_, r_], [ob_], lambda e, o_=o_, r_=r_, ob_=ob_: e.tensor_tensor(out=ob_.ap, in0=o_.ap, in1=r_.ap, op=ALU.mult))
                fw.dma("sp", f"a_o{cnt['o'] % 2}", DT["br"], ob_, out_ap=D_["br"][0, hd][:, cs])
                cnt["o"] += 1

        SpB = [ps[4], ps[5]]
        indexer(0)
        for qt in range(NTT):
            if qt + 1 < NTT:
                indexer(qt + 1)
            attention(qt)
        fw.barrier(pool=True)


def gla_phase(g, l):
    nc, fw = g.nc, g.fw
    D_, DT = g.D, g.DT
    ps = g.ps
    with contextlib.ExitStack() as st:
        sb = lambda n, s, d: g.sb(n, s, d, st)
        gv = sb("g_v", [128, 32, 512], BF16)
        glr = sb("g_lr", [16, S], BF16)
        wgf = sb("g_wgf", [16, 256], F32)
        wgb = sb("g_wgb", [16, 256], BF16)
        pv = sb("g_pv", [128, 8], F32)
        nbg = sb("g_nbg", [64, 4], F32)
        q = sb("g_q", [64, S], F32)
        k = sb("g_k", [64, S], F32)
        cs_ = sb("g_cs", [64, S], F32)
        eb = sb("g_eb", [64, S], F32)
        qt_ = sb("g_qt", [64, S], BF16)
        ktb = sb("g_ktb", [64, S], BF16)
        khat = sb("g_kh", [64, S], BF16)
        khT = sb("g_khT", [128, 32, 64], BF16)
        Ofm = sb("g_O", [128, S], F32)
        Ab = [sb(f"g_Ab{i}", [128, 128], BF16) for i in range(2)]
        Sst = sb("g_S", [64, 128], F32)
        Sb = sb("g_Sb", [64, 128], BF16)
        sqb = sb("g_sqb", [128, TT], BF16)
        rs = sb("g_rs", [128, TT], F32)
        grt = [sb(f"g_gr{i}", [128, TT], BF16) for i in range(2)]
        t1 = sb("g_t1", [128, TT], F32)
        ob = [sb(f"g_ob{i}", [128, TT], BF16) for i in range(2)]
        fw.dma("sp", "g_l0", gv, DT["gv"], in_ap=D_["gv"].rearrange("(k p) c -> p k c", p=128))
        fw.dma("sp", "g_l1", glr, DT["glr"])
        fw.dma("sp", "g_l2", wgf, g.IT["gla_w_gate"], in_ap=g.I["gla_w_gate"][l])
        fw.dma("sp", "g_l3", pv, g.IT["pvec"], in_ap=g.I["pvec"][l][:, 206:214])
        fw.op("dve", [wgf], [wgb], lambda e: e.tensor_copy(out=wgb.ap, in_=wgf.ap))
        fw.op("dve", [pv], [nbg], lambda e: e.tensor_scalar(out=nbg.ap, in0=pv.ap[0:64, 0:4], scalar1=-1.0, scalar2=None, op0=ALU.mult))
        ptb = ps[6].ap.bitcast(BF16)
        no = 0
        for hd in range(4):
            fw.dma("sp", "g_lq", q, DT["gq"], in_ap=D_["gq"][hd])
            fw.dma("sp", "g_lk", k, DT["gk"], in_ap=D_["gk"][hd])
            for tt in range(NTT):
                cs = slice(tt * TT, (tt + 1) * TT)
                pz = ps[tt % 2]
                fw.op("pe", [wgb, glr], [pz], lambda e, pz=pz, cs=cs, hd=hd: e.matmul(
                    pz.ap[0:64, :], wgb.ap[:, hd * 64:(hd + 1) * 64], glr.ap[:, cs], start=True, stop=True))
                fw.op("act", [pz, nbg], [eb], lambda e, pz=pz, cs=cs, hd=hd: e.activation(
                    out=eb.ap[:, cs], in_=pz.ap[0:64, :], func=AF.Exp, bias=nbg.ap[:, hd:hd + 1], scale=-1.0))
                fw.op("act", [eb], [eb], lambda e, cs=cs: e.activation(
                    out=eb.ap[:, cs], in_=eb.ap[:, cs], func=AF.Ln, bias=1.0, scale=1.0))
                fw.op("dve", [eb, g.rmask], [cs_], lambda e, cs=cs: e.tensor_tensor_scan(
                    out=cs_.ap[:, cs], data0=g.rmask.ap, data1=eb.ap[:, cs], initial=0.0, op0=ALU.mult, op1=ALU.add))
            fw.op("act", [cs_], [eb], lambda e: e.activation(out=eb.ap, in_=cs_.ap, func=AF.Exp, scale=-1.0 / 16))
            fw.op("dve", [q, eb], [qt_], lambda e: e.tensor_tensor(out=qt_.ap, in0=q.ap, in1=eb.ap, op=ALU.mult))
            fw.op("act", [cs_], [cs_], lambda e: e.activation(out=cs_.ap, in_=cs_.ap, func=AF.Exp, scale=1.0 / 16))
            fw.op("dve", [k, cs_], [k], lambda e: e.tensor_tensor(out=k.ap, in0=k.ap, in1=cs_.ap, op=ALU.mult))
            fw.op("act", [k], [ktb], lambda e: e.activation(out=ktb.ap, in_=k.ap, func=AF.Copy))
            ebl = eb.ap.rearrange("p (c j) -> p c j", j=64)[:, :, 63:64]
            fw.op("dve", [k, eb], [khat], lambda e: e.tensor_tensor(
                out=khat.ap.rearrange("p (c j) -> p c j", j=64), in0=k.ap.rearrange("p (c j) -> p c j", j=64),
                in1=ebl.to_broadcast([64, 64, 64]), op=ALU.mult))
            for t0 in range(0, 32, 8):
                for b in range(8):
                    fw.op("pe", [khat, g.ident], [ps[6]], lambda e, t0=t0, b=b: e.transpose(
                        ptb[:, b * 64:(b + 1) * 64], khat.ap[:, (t0 + b) * 128:(t0 + b + 1) * 128], g.ident.ap[0:64, 0:64]))
                fw.op("act", [ps[6]], [khT], lambda e, t0=t0: e.activation(
                    out=khT.ap[:, t0:t0 + 8, :], in_=ptb[:, 0:512].rearrange("p (b c) -> p b c", c=64), func=AF.Copy))
            fw.op("dve", [], [Sst], lambda e: e.memset(Sst.ap, 0.0))
            for tl in range(32):
                ts_ = slice(tl * 128, (tl + 1) * 128)
                pa = ps[tl % 2]
                fw.op("pe", [ktb, qt_], [pa], lambda e, pa=pa, ts_=ts_: e.matmul(
                    pa.ap[:, 0:128], ktb.ap[:, ts_], qt_.ap[:, ts_], start=True, stop=True))
                A = Ab[tl % 2]
                fw.op("dve", [pa, g.gmask], [A], lambda e, pa=pa, A=A: e.tensor_tensor(
                    out=A.ap, in0=pa.ap[:, 0:128], in1=g.gmask.ap, op=ALU.mult))
                O = ps[2 + tl % 2]
                last_is_inter = True
                fw.op("pe", [gv, A], [O], lambda e, O=O, A=A, tl=tl, hd=hd: e.matmul(
                    O.ap[:, 0:128], gv.ap[:, tl, hd * 128:(hd + 1) * 128], A.ap, start=True, stop=False))
                for ci in range(2):
                    c = 2 * tl + ci
                    cc = slice(c * 64, (c + 1) * 64)
                    if c > 0:
                        fw.op("pe", [Sb, qt_], [O], lambda e, O=O, ci=ci, cc=cc: e.matmul(
                            O.ap[:, ci * 64:(ci + 1) * 64], Sb.ap, qt_.ap[:, cc], start=False, stop=(ci == 1)))
                    U = ps[4 + c % 2]
                    pr = slice(ci * 64, (ci + 1) * 64)
                    fw.op("pe", [khT, gv], [U], lambda e, U=U, pr=pr, tl=tl, hd=hd: e.matmul(
                        U.ap[0:64, 0:128], khT.ap[pr, tl, :], gv.ap[pr, tl, hd * 128:(hd + 1) * 128], start=True, stop=True))
                    fw.op("dve", [U, Sst, eb], [Sst], lambda e, U=U, c=c: e.scalar_tensor_tensor(
                        out=Sst.ap, in0=Sst.ap, scalar=eb.ap[:, c * 64 + 63:c * 64 + 64], in1=U.ap[0:64, 0:128],
                        op0=ALU.mult, op1=ALU.add))
                    fw.op("act", [Sst], [Sb], lambda e: e.activation(out=Sb.ap, in_=Sst.ap, func=AF.Copy))
                fw.op("act", [O], [Ofm], lambda e, O=O, ts_=ts_: e.activation(out=Ofm.ap[:, ts_], in_=O.ap[:, 0:128], func=AF.Copy))
            for tt in range(NTT):
                cs = slice(tt * TT, (tt + 1) * TT)
                gr = grt[tt % 2]
                fw.dma("sp", f"g_gr{tt % 2}", gr, DT["gr"], in_ap=D_["gr"][hd][:, cs])
                fw.op("act", [Ofm], [sqb], lambda e, cs=cs: e.activation(out=sqb.ap, in_=Ofm.ap[:, cs], func=AF.Square))
                fw.op("pe", [sqb, g.ones_bf], [ps[7]], lambda e: e.matmul(ps[7].ap, g.ones_bf.ap, sqb.ap, start=True, stop=True))
                fw.op("act", [ps[7], g.eps_t], [rs], lambda e: e.activation(
                    out=rs.ap, in_=ps[7].ap, func=AF.Sqrt, bias=g.eps_t.ap[:, 0:1], scale=1.0 / 128))
                fw.op("dve", [rs], [rs], lambda e: e.reciprocal(out=rs.ap, in_=rs.ap))
                fw.op("dve", [Ofm, pv, rs], [t1], lambda e, cs=cs: e.scalar_tensor_tensor(
                    out=t1.ap, in0=Ofm.ap[:, cs], scalar=pv.ap[:, 4:5], in1=rs.ap, op0=ALU.mult, op1=ALU.mult))
                o = ob[no % 2]
                fw.op("dve", [t1, gr], [o], lambda e, o=o, gr=gr: e.tensor_tensor(out=o.ap, in0=t1.ap, in1=gr.ap, op=ALU.mult))
                fw.dma("sp", f"g_o{no % 2}", DT["br"], o, out_ap=D_["br"][1, hd][:, cs])
                no += 1
        fw.barrier()


T5 = 128


def s5_phase(g, l):
    nc, fw = g.nc, g.fw
    D_, DT = g.D, g.DT
    ps = g.ps
    Wglu = g.W[("glu", l)][1]
    with contextlib.ExitStack() as st:
        sb = lambda n, s, d: g.sb(n, s, d, st)
        pv = sb("s_pv", [128, 56], F32)
        sm = sb("s_sm", [128, 16, 12], F32)
        smi = sb("s_smi", [128, 16], I32)
        BTf = sb("s_BTf", [128, 16, 128], F32)
        BTr = sb("s_BTr", [128, 16, 128], BF16)
        BTi = sb("s_BTi", [128, 16, 128], BF16)
        CTr = sb("s_CTr", [128, 16, 128], BF16)
        CTi = sb("s_CTi", [128, 16, 128], BF16)
        wgl = sb("s_wgl", [128, 4, 4, 128], BF16)
        Cj = sb("s_Cj", [128, 16, T5], F32)
        Sj = sb("s_Sj", [128, 16, T5], F32)
        Er = sb("s_Er", [128, 16, T5], F32)
        Ei = sb("s_Ei", [128, 16, T5], F32)
        magT = sb("s_mag", [128, 16, T5], F32)
        tki = sb("s_tki", [128, 16, T5], I32)
        uf = [sb(f"s_uf{i}", [128, 4, T5], F32) for i in range(2)]
        ub = sb("s_ub", [128, 4, T5], BF16)
        xr = sb("s_xr", [128, 16, T5], F32)
        xi = sb("s_xi", [128, 16, T5], F32)
        gr_ = sb("s_gr", [128, 16, T5], F32)
        gi_ = sb("s_gi", [128, 16, T5], F32)
        ta = sb("s_ta", [128, 4, T5], F32)
        tb = sb("s_tb", [128, 4, T5], F32)
        hr = sb("s_hr", [128, 16, T5], BF16)
        hi = sb("s_hi", [128, 16, T5], BF16)
        init = sb("s_init", [128, 16, 4], F32)
        ygf = sb("s_ygf", [128, 4, T5], F32)
        ygb = sb("s_ygb", [128, 4, T5], BF16)
        sg = sb("s_sg", [128, T5], F32)
        brc = sb("s_brc", [128, 4, S], BF16)
        fw.dma("sp", "s_l0", pv, g.IT["pvec"], in_ap=g.I["pvec"][l][:, 211:267])
        for i, Wt in enumerate(Wglu):
            fw.dma("sp", "s_l1", wgl, Wt, out_ap=wgl.ap[:, i])
        for (src, idx, dst, scl) in (("BT", 0, BTr, 1.0), ("BT", 1, BTi, 1.0), ("CT", 0, CTr, 1.0), ("CT", 1, CTi, -1.0)):
            fw.dma("sp", "s_l2", BTf, g.IT[src], in_ap=g.I[src][l, idx])
            fw.op("act", [BTf], [dst], lambda e, dst=dst, scl=scl: e.activation(out=dst.ap, in_=BTf.ap, func=AF.Copy, scale=scl))
        col = lambda i: sm.ap[:, :, i]
        are, aim, ldt = pv.ap[:, 0:16], pv.ap[:, 16:32], pv.ap[:, 32:48]
        DTc, MAG, TH, ABR, ABI, ZR, ZI, RR, RI, TMP, TMP2, TMP3 = (col(i) for i in range(12))
        smT = sm

        def sop(fn, reads=(), eng="dve"):
            fw.op(eng, [smT, pv] + list(reads), [smT], fn)

        sop(lambda e: e.activation(out=DTc, in_=ldt, func=AF.Exp), eng="act")
        sop(lambda e: e.tensor_tensor(out=TMP, in0=DTc, in1=are, op=ALU.mult))
        sop(lambda e: e.activation(out=MAG, in_=TMP, func=AF.Exp), eng="act")
        sop(lambda e: e.tensor_tensor(out=TH, in0=DTc, in1=aim, op=ALU.mult))

        def small_sin(x_ap, out_ap, shift):
            sop(lambda e: e.tensor_scalar(out=TMP, in0=x_ap, scalar1=shift, scalar2=1.0 / TWO_PI, op0=ALU.add, op1=ALU.mult))
            fw.op("dve", [smT], [smi], lambda e: e.tensor_copy(out=smi.ap, in_=TMP))
            fw.op("dve", [smi], [smT], lambda e: e.tensor_copy(out=TMP2, in_=smi.ap))
            sop(lambda e: e.tensor_scalar(out=TMP, in0=x_ap, scalar1=shift, scalar2=None, op0=ALU.add))
            sop(lambda e: e.scalar_tensor_tensor(out=TMP, in0=TMP2, scalar=-TWO_PI, in1=TMP, op0=ALU.mult, op1=ALU.add))
            sop(lambda e: e.tensor_scalar(out=TMP2, in0=TMP, scalar1=PI, scalar2=-TWO_PI, op0=ALU.is_gt, op1=ALU.mult))
            sop(lambda e: e.tensor_tensor(out=TMP, in0=TMP, in1=TMP2, op=ALU.add))
            sop(lambda e: e.tensor_scalar(out=TMP2, in0=TMP, scalar1=-PI, scalar2=TWO_PI, op0=ALU.is_lt, op1=ALU.mult))
            sop(lambda e: e.tensor_tensor(out=TMP, in0=TMP, in1=TMP2, op=ALU.add))
            sop(lambda e: e.activation(out=out_ap, in_=TMP, func=AF.Sin), eng="act")

        small_sin(TH, ABI, 0.0)
        small_sin(TH, ABR, PI / 2)
        sop(lambda e: e.tensor_tensor(out=ABI, in0=ABI, in1=MAG, op=ALU.mult))
        sop(lambda e: e.tensor_tensor(out=ABR, in0=ABR, in1=MAG, op=ALU.mult))
        sop(lambda e: e.tensor_tensor(out=TMP, in0=are, in1=are, op=ALU.mult))
        sop(lambda e: e.tensor_tensor(out=TMP2, in0=aim, in1=aim, op=ALU.mult))
        sop(lambda e: e.tensor_tensor(out=TMP, in0=TMP, in1=TMP2, op=ALU.add))
        sop(lambda e: e.reciprocal(out=TMP, in_=TMP))
        sop(lambda e: e.tensor_scalar(out=TMP3, in0=ABR, scalar1=-1.0, scalar2=None, op0=ALU.add))
        sop(lambda e: e.tensor_tensor(out=ZR, in0=TMP3, in1=are, op=ALU.mult))
        sop(lambda e: e.tensor_tensor(out=TMP2, in0=ABI, in1=aim, op=ALU.mult))
        sop(lambda e: e.tensor_tensor(out=ZR, in0=ZR, in1=TMP2, op=ALU.add))
        sop(lambda e: e.tensor_tensor(out=ZR, in0=ZR, in1=TMP, op=ALU.mult))
        sop(lambda e: e.tensor_tensor(out=ZI, in0=ABI, in1=are, op=ALU.mult))
        sop(lambda e: e.tensor_tensor(out=TMP2, in0=TMP3, in1=aim, op=ALU.mult))
        sop(lambda e: e.tensor_tensor(out=ZI, in0=ZI, in1=TMP2, op=ALU.subtract))
        sop(lambda e: e.tensor_tensor(out=ZI, in0=ZI, in1=TMP, op=ALU.mult))
        sop(lambda e: e.tensor_scalar(out=TMP3, in0=TH, scalar1=float(T5), scalar2=None, op0=ALU.mult))
        small_sin(TMP3, RI, 0.0)
        sop(lambda e: e.tensor_scalar(out=TMP3, in0=TH, scalar1=float(T5), scalar2=None, op0=ALU.mult))
        small_sin(TMP3, RR, PI / 2)
        for sc in range(16):
            fw.op("dve", [smT, g.jrow], [xr], lambda e, sc=sc: e.tensor_scalar(
                out=xr.ap[:, sc, :], in0=g.jrow.ap, scalar1=sm.ap[:, sc, 2:3], scalar2=None, op0=ALU.mult))
            fw.op("dve", [smT, g.onef], [magT], lambda e, sc=sc: e.tensor_scalar(
                out=magT.ap[:, sc, :], in0=g.onef.ap[:, 0:T5], scalar1=sm.ap[:, sc, 1:2], scalar2=None, op0=ALU.mult))
        sin_reduced(fw, tki, xi, xr, Sj)
        fw.op("dve", [xr], [xr], lambda e: e.tensor_scalar(out=xr.ap, in0=xr.ap, scalar1=PI / 2, scalar2=None, op0=ALU.add))
        sin_reduced(fw, tki, xi, xr, Cj)
        for sc in range(16):
            zr, zi = sm.ap[:, sc, 5:6], sm.ap[:, sc, 6:7]
            fw.op("dve", [smT, Cj], [Er], lambda e, sc=sc, zr=zr: e.tensor_scalar(out=Er.ap[:, sc, :], in0=Cj.ap[:, sc, :], scalar1=zr, scalar2=None, op0=ALU.mult))
            fw.op("dve", [smT, Sj, Er], [Er], lambda e, sc=sc, zi=zi: e.scalar_tensor_tensor(
                out=Er.ap[:, sc, :], in0=Sj.ap[:, sc, :], scalar=zi, in1=Er.ap[:, sc, :], op0=ALU.mult, op1=ALU.add))
            fw.op("dve", [smT, Cj], [Ei], lambda e, sc=sc, zi=zi: e.tensor_scalar(out=Ei.ap[:, sc, :], in0=Cj.ap[:, sc, :], scalar1=zi, scalar2=None, op0=ALU.mult))
            fw.op("dve", [smT, Sj], [xr], lambda e, sc=sc, zr=zr: e.tensor_scalar(out=xr.ap[:, sc, :], in0=Sj.ap[:, sc, :], scalar1=zr, scalar2=None, op0=ALU.mult))
        fw.op("dve", [Ei, xr], [Ei], lambda e: e.tensor_tensor(out=Ei.ap, in0=Ei.ap, in1=xr.ap, op=ALU.subtract))
        fw.op("dve", [], [init], lambda e: e.memset(init.ap, 0.0))
        ntile = S // T5
        for tl in range(ntile):
            ts_ = slice(tl * T5, (tl + 1) * T5)
            u = uf[tl % 2]
            fw.dma("sp", f"s_u{tl % 2}", u, DT["su"], in_ap=D_["su"][:, :, ts_].rearrange("c p t -> p c t"))
            fw.op("act", [u], [ub], lambda e, u=u: e.activation(out=ub.ap, in_=u.ap, func=AF.Copy))
            for g4 in range(4):
                pr_, pi_ = ps[(2 * g4) % 4], ps[(2 * g4 + 1) % 4]
                for s4 in range(4):
                    sc = 4 * g4 + s4
                    fw.op("pe", [BTr, ub], [pr_], lambda e, pr_=pr_, sc=sc, s4=s4: e.matmul(
                        pr_.ap[:, s4 * T5:(s4 + 1) * T5], BTr.ap[:, sc, :], ub.ap[:, sc // 4, :], start=True, stop=True))
                    fw.op("pe", [BTi, ub], [pi_], lambda e, pi_=pi_, sc=sc, s4=s4: e.matmul(
                        pi_.ap[:, s4 * T5:(s4 + 1) * T5], BTi.ap[:, sc, :], ub.ap[:, sc // 4, :], start=True, stop=True))
                scs = slice(4 * g4, 4 * g4 + 4)
                v3 = lambda t: t.ap.rearrange("p (a b) -> p a b", b=T5)
                fw.op("dve", [Er, pr_], [ta], lambda e, pr_=pr_, scs=scs: e.tensor_tensor(out=ta.ap, in0=Er.ap[:, scs, :], in1=v3(pr_), op=ALU.mult))
                fw.op("dve", [Ei, pi_], [tb], lambda e, pi_=pi_, scs=scs: e.tensor_tensor(out=tb.ap, in0=Ei.ap[:, scs, :], in1=v3(pi_), op=ALU.mult))
                fw.op("dve", [ta, tb], [xr], lambda e, scs=scs: e.tensor_tensor(out=xr.ap[:, scs, :], in0=ta.ap, in1=tb.ap, op=ALU.subtract))
                fw.op("dve", [Er, pi_], [ta], lambda e, pi_=pi_, scs=scs: e.tensor_tensor(out=ta.ap, in0=Er.ap[:, scs, :], in1=v3(pi_), op=ALU.mult))
                fw.op("dve", [Ei, pr_], [tb], lambda e, pr_=pr_, scs=scs: e.tensor_tensor(out=tb.ap, in0=Ei.ap[:, scs, :], in1=v3(pr_), op=ALU.mult))
                fw.op("dve", [ta, tb], [xi], lambda e, scs=scs: e.tensor_tensor(out=xi.ap[:, scs, :], in0=ta.ap, in1=tb.ap, op=ALU.add))
            for sc in range(16):
                fw.op("dve", [magT, xr, init], [gr_], lambda e, sc=sc: e.tensor_tensor_scan(
                    out=gr_.ap[:, sc, :], data0=magT.ap[:, sc, :], data1=xr.ap[:, sc, :], initial=init.ap[:, sc, 0:1], op0=ALU.mult, op1=ALU.add))
                fw.op("dve", [magT, xi, init], [gi_], lambda e, sc=sc: e.tensor_tensor_scan(
                    out=gi_.ap[:, sc, :], data0=magT.ap[:, sc, :], data1=xi.ap[:, sc, :], initial=init.ap[:, sc, 1:2], op0=ALU.mult, op1=ALU.add))
            glr_, gli_ = gr_.ap[:, :, T5 - 1], gi_.ap[:, :, T5 - 1]
            i0, i1, i2, i3 = (init.ap[:, :, j] for j in range(4))
            fw.op("dve", [gr_, smT], [init], lambda e: e.tensor_tensor(out=i2, in0=glr_, in1=RR, op=ALU.mult))
            fw.op("dve", [gi_, smT], [init], lambda e: e.tensor_tensor(out=i3, in0=gli_, in1=RI, op=ALU.mult))
            fw.op("dve", [init], [init], lambda e: e.tensor_tensor(out=i0, in0=i2, in1=i3, op=ALU.subtract))
            fw.op("dve", [gi_, smT], [init], lambda e: e.tensor_tensor(out=i2, in0=gli_, in1=RR, op=ALU.mult))
            fw.op("dve", [gr_, smT], [init], lambda e: e.tensor_tensor(out=i3, in0=glr_, in1=RI, op=ALU.mult))
            fw.op("dve", [init], [init], lambda e: e.tensor_tensor(out=i1, in0=i2, in1=i3, op=ALU.add))
            for g4 in range(4):
                scs = slice(4 * g4, 4 * g4 + 4)
                fw.op("dve", [Cj, gr_], [ta], lambda e, scs=scs: e.tensor_tensor(out=ta.ap, in0=Cj.ap[:, scs, :], in1=gr_.ap[:, scs, :], op=ALU.mult))
                fw.op("dve", [Sj, gi_], [tb], lambda e, scs=scs: e.tensor_tensor(out=tb.ap, in0=Sj.ap[:, scs, :], in1=gi_.ap[:, scs, :], op=ALU.mult))
                fw.op("dve", [ta, tb], [hr], lambda e, scs=scs: e.tensor_tensor(out=hr.ap[:, scs, :], in0=ta.ap, in1=tb.ap, op=ALU.subtract))
                fw.op("dve", [Cj, gi_], [ta], lambda e, scs=scs: e.tensor_tensor(out=ta.ap, in0=Cj.ap[:, scs, :], in1=gi_.ap[:, scs, :], op=ALU.mult))
                fw.op("dve", [Sj, gr_], [tb], lambda e, scs=scs: e.tensor_tensor(out=tb.ap, in0=Sj.ap[:, scs, :], in1=gr_.ap[:, scs, :], op=ALU.mult))
                fw.op("dve", [ta, tb], [hi], lambda e, scs=scs: e.tensor_tensor(out=hi.ap[:, scs, :], in0=ta.ap, in1=tb.ap, op=ALU.add))
            for oc in range(4):
                Y = ps[4 + oc % 2]
                for s4 in range(4):
                    sc = 4 * oc + s4
                    fw.op("pe", [CTr, hr], [Y], lambda e, Y=Y, sc=sc, s4=s4: e.matmul(
                        Y.ap[:, 0:T5], CTr.ap[:, sc, :], hr.ap[:, sc, :], start=(s4 == 0), stop=False))
                    fw.op("pe", [CTi, hi], [Y], lambda e, Y=Y, sc=sc, s4=s4: e.matmul(
                        Y.ap[:, 0:T5], CTi.ap[:, sc, :], hi.ap[:, sc, :], start=False, stop=(s4 == 3)))
                fw.op("dve", [u, pv, Y], [ygf], lambda e, Y=Y, oc=oc, u=u: e.scalar_tensor_tensor(
                    out=ygf.ap[:, oc, :], in0=u.ap[:, oc, :], scalar=pv.ap[:, 48 + oc:49 + oc], in1=Y.ap[:, 0:T5], op0=ALU.mult, op1=ALU.add))
            fw.op("act", [ygf], [ygf], lambda e: e.activation(out=ygf.ap, in_=ygf.ap, func=AF.Gelu))
            fw.op("act", [ygf], [ygb], lambda e: e.activation(out=ygb.ap, in_=ygf.ap, func=AF.Copy))
            for oc2 in range(4):
                Z = ps[6 + oc2 % 2]
                for kc in range(4):
                    fw.op("pe", [wgl, ygb], [Z], lambda e, Z=Z, oc2=oc2, kc=kc: e.matmul(
                        Z.ap[:, 0:T5], wgl.ap[:, oc2, kc, :], ygb.ap[:, kc, :], start=(kc == 0), stop=(kc == 3)))
                fw.op("act", [Z, pv], [sg], lambda e, Z=Z, oc2=oc2: e.activation(
                    out=sg.ap, in_=Z.ap[:, 0:T5], func=AF.Sigmoid, bias=pv.ap[:, 52 + oc2:53 + oc2], scale=1.0))
                fw.op("dve", [ygf, sg], [brc], lambda e, oc2=oc2, ts_=ts_: e.tensor_tensor(
                    out=brc.ap[:, oc2, ts_], in0=ygf.ap[:, oc2, :], in1=sg.ap, op=ALU.mult))
        for oc in range(4):
            fw.dma("sp", "s_o", DT["br"], brc, out_ap=D_["br"][2, oc], in_ap=brc.ap[:, oc, :])
        fw.barrier()


def merge_phase(g, l):
    nc, fw = g.nc, g.fw
    D_, DT = g.D, g.DT
    ps = g.ps
    AB = g.AB[l]
    Wg = [g.W[(f"wg{i}", l)][1] for i in range(4)]
    Wb = [g.W[(f"wb{i}", l)][1] for i in range(4)]
    Wo = g.W[("wo", l)][1]
    with contextlib.ExitStack() as st:
        sb = lambda n, s, d: g.sb(n, s, d, st)
        xt = sb("e_xt", [128, NCH, TT], F32)
        h = sb("e_h", [128, NCH, TT], BF16)
        br = sb("e_br", [128, 16, TT], BF16)
        mg = sb("e_mg", [128, NCH, TT], F32)
        mgb = sb("e_mgb", [128, NCH, TT], BF16)
        wgs = [sb(f"e_wg{i}", [128, NCH, 128], BF16) for i in range(3)]
        wbs = [sb(f"e_wb{i}", [128, 4, 128], BF16) for i in range(3)]
        sgt = [sb(f"e_sg{i}", [128, TT], F32) for i in range(2)]
        tmp = [sb(f"e_tmp{i}", [128, TT], F32) for i in range(2)]
        n = 0
        for tt in range(NTT):
            cs = slice(tt * TT, (tt + 1) * TT)
            fw.dma("sp", "e_xl", xt, g.yT[tt], in_ap=g.y[:, cs].rearrange("(c p) t -> p c t", p=128))
            fw.dma("sp", "e_hl", h, DT["hT"], in_ap=D_["hT"][:, cs].rearrange("(c p) t -> p c t", p=128))
            fw.dma("sp", "e_bl", br, DT["br"], in_ap=D_["br"][:, :, :, cs].rearrange("b c p t -> p (b c) t"))
            for i in range(4):
                for m in range(NCH):
                    wg_, wb_ = wgs[n % 3], wbs[n % 3]
                    fw.dma("sp", f"e_wg{n % 3}", wg_, Wg[i][m])
                    fw.dma("sp", f"e_wb{n % 3}", wb_, Wb[i][m])
                    n += 1
                    pg, pb = ps[(2 * m) % 6], ps[(2 * m + 1) % 6]
                    for k in range(NCH):
                        fw.op("pe", [wg_, h], [pg], lambda e, wg_=wg_, pg=pg, k=k: e.matmul(
                            pg.ap, wg_.ap[:, k, :], h.ap[:, k, :], start=(k == 0), stop=(k == NCH - 1)))
                    for k in range(4):
                        fw.op("pe", [wb_, br], [pb], lambda e, wb_=wb_, pb=pb, k=k, i=i: e.matmul(
                            pb.ap, wb_.ap[:, k, :], br.ap[:, 4 * i + k, :], start=(k == 0), stop=(k == 3)))
                    s_ = sgt[m % 2]
                    fw.op("act", [pg], [s_], lambda e, pg=pg, s_=s_: e.activation(out=s_.ap, in_=pg.ap, func=AF.Sigmoid))
                    if i == 0:
                        fw.op("dve", [s_, pb], [mg], lambda e, s_=s_, pb=pb, m=m: e.tensor_tensor(
                            out=mg.ap[:, m, :], in0=s_.ap, in1=pb.ap, op=ALU.mult))
                    else:
                        t_ = tmp[m % 2]
                        fw.op("dve", [s_, pb], [t_], lambda e, s_=s_, pb=pb, t_=t_: e.tensor_tensor(
                            out=t_.ap, in0=s_.ap, in1=pb.ap, op=ALU.mult))
                        fw.op("dve", [t_, mg], [mg], lambda e, t_=t_, m=m: e.tensor_tensor(
                            out=mg.ap[:, m, :], in0=mg.ap[:, m, :], in1=t_.ap, op=ALU.add))
            for c in range(NCH):
                fw.op("act", [mg], [mgb], lambda e, c=c: e.activation(out=mgb.ap[:, c, :], in_=mg.ap[:, c, :], func=AF.Copy))
            for m in range(NCH):
                wo_ = wgs[n % 3]
                fw.dma("sp", f"e_wg{n % 3}", wo_, Wo[m])
                n += 1
                po = ps[m % 6]
                for k in range(NCH):
                    fw.op("pe", [wo_, mgb], [po], lambda e, wo_=wo_, po=po, k=k: e.matmul(
                        po.ap, wo_.ap[:, k, :], mgb.ap[:, k, :], start=(k == 0), stop=(k == NCH - 1)))
                G = AB.ap[:, 5 * 16 + m:5 * 16 + m + 1]
                fw.op("dve", [po, xt, AB], [xt], lambda e, po=po, m=m, G=G: e.scalar_tensor_tensor(
                    out=xt.ap[:, m, :], in0=po.ap, scalar=G, in1=xt.ap[:, m, :], op0=ALU.mult, op1=ALU.add))
            fw.dma("sp", "e_xs", g.yT[tt], xt, out_ap=g.y[:, cs].rearrange("(c p) t -> p c t", p=128))
        fw.barrier()


def _swap(a, half):
    sh = a.shape
    b = a.reshape(sh[:-1] + (sh[-1] // (2 * half), 2, half))
    return np.ascontiguousarray(b[..., ::-1, :]).reshape(sh)


def pack_shared(inputs):
    f = lambda k: np.asarray(inputs[k], dtype=np.float32)
    out = {}
    for k in ("w_ada", "w_ff1_in", "w_ff1_out", "w_ff2_in", "w_ff2_out", "gla_w_gate", "s5_w_glu",
              "w_branch", "w_gate", "w_out"):
        out[k] = np.ascontiguousarray(f(k))
    w_in = f("w_in")
    seg = lambda o, n: w_in[:, :, o:o + n]
    blocks = []
    aq, ak = seg(O_AQ, 512), seg(O_AK, 512)
    blocks += [aq, _swap(aq, 64), ak, _swap(ak, 64)]
    iq = seg(O_IQ, 512)
    blocks += [iq, _swap(iq, 32)]
    ik = seg(O_IK, 64)
    blocks += [ik, ik, _swap(ik, 32), _swap(ik, 32)]
    blocks += [seg(O_GR, 512), seg(O_SU, 512), seg(O_MQ, 384), seg(O_MKV, 128)]
    kr = seg(O_MKR, 64)
    blocks += [kr, _swap(kr, 32)]
    blocks += [seg(O_GQ, 256), seg(O_GK, 256)]
    z112 = np.zeros((L, D, 112), np.float32)
    blocks += [seg(O_GLR, 16), z112]
    blocks += [seg(O_AV, 512), seg(O_GV, 512)]
    z120 = np.zeros((L, D, 120), np.float32)
    blocks += [seg(O_IW, 8), z120]
    out["w_inx"] = np.ascontiguousarray(np.concatenate(blocks, axis=-1))
    assert out["w_inx"].shape[-1] == NXB * 128
    uq = f("mla_w_uq").reshape(L, 384, 4, 192)
    nope = uq[..., :128].reshape(L, 384, 512)
    rope = uq[..., 128:]
    ropex = np.concatenate([rope, _swap(rope, 32)], axis=-1).reshape(L, 384, 512)
    out["w_uqx"] = np.ascontiguousarray(np.concatenate([nope, ropex], axis=-1))
    ukv = f("mla_w_ukv").reshape(L, 128, 4, 256)
    out["w_ukvx"] = np.ascontiguousarray(np.concatenate(
        [ukv[..., :128].reshape(L, 128, 512), ukv[..., 128:].reshape(L, 128, 512)], axis=-1))
    pv = np.zeros((L, 128, NP), np.float32)
    pc = lambda v: v.reshape(L, -1, 128).transpose(0, 2, 1)
    pv[:, :, 0:144] = pc(f("b_ada"))
    pv[:, :, 144:192] = pc(f("norm_g").reshape(L, 3 * D))
    dq = f("dsa_qk_norm")
    pv[:, :, 192] = dq[:, 0]; pv[:, :, 193] = _swap(dq[:, 0], 64)
    pv[:, :, 194] = dq[:, 1]; pv[:, :, 195] = _swap(dq[:, 1], 64)
    pv[:, :, 196:199] = pc(f("mla_q_norm"))
    pv[:, :, 199] = f("mla_kv_norm")
    mq = f("mla_qk_norm")
    dup = lambda v: np.concatenate([v, v], axis=-1)
    pv[:, :, 200] = mq[:, 0, :128]; pv[:, :, 201] = dup(mq[:, 0, 128:]); pv[:, :, 202] = dup(_swap(mq[:, 0, 128:], 32))
    pv[:, :, 203] = mq[:, 1, :128]; pv[:, :, 204] = dup(mq[:, 1, 128:]); pv[:, :, 205] = dup(_swap(mq[:, 1, 128:], 32))
    bg = f("gla_b_gate").reshape(L, 4, 64)
    for hd in range(4):
        pv[:, :, 206 + hd] = dup(bg[:, hd])
    pv[:, :, 210] = f("gla_out_norm")
    pv[:, :, 211:227] = pc(f("s5_a_re").reshape(L, 2048))
    pv[:, :, 227:243] = pc(f("s5_a_im").reshape(L, 2048))
    pv[:, :, 243:259] = pc(np.repeat(f("s5_log_dt"), 64, axis=-1))
    pv[:, :, 259:263] = pc(f("s5_d"))
    pv[:, :, 263:267] = pc(f("s5_b_glu"))
    out["pvec"] = pv
    BT = np.zeros((L, 2, 128, 16, 128), np.float32)
    CT = np.zeros((L, 2, 128, 16, 128), np.float32)
    for ri, (bn, cn) in enumerate((("s5_b_re", "s5_c_re"), ("s5_b_im", "s5_c_im"))):
        Bm = f(bn).reshape(L, 32, 64, 16)
        Cm = f(cn).reshape(L, 32, 16, 64)
        for sc in range(16):
            for gl in range(2):
                gg = 2 * sc + gl
                r0 = 32 * (sc % 4) + 16 * gl
                BT[:, ri, r0:r0 + 16, sc, 64 * gl:64 * gl + 64] = Bm[:, gg].transpose(0, 2, 1)
                CT[:, ri, 64 * gl:64 * gl + 64, sc, r0:r0 + 16] = Cm[:, gg].transpose(0, 2, 1)
    out["BT"], out["CT"] = BT, CT
    kc = np.zeros((128, 4), np.float32)
    p = np.arange(128)
    kc[:, 0] = np.power(np.float32(10000.0), -(p % 64).astype(np.float32) / 64).astype(np.float32)
    kc[:, 1] = np.where(p < 64, -1.0, 1.0)
    kc[:, 2] = np.power(np.float32(10000.0), -(p % 32).astype(np.float32) / 32).astype(np.float32)
    kc[:, 3] = np.where((p % 64) < 32, -1.0, 1.0)
    out["kconst"] = kc
    return out


def make_in_maps(inputs, cores, shared=None):
    if shared is None:
        shared = pack_shared(inputs)
    x = np.asarray(inputs["x"], dtype=np.float32)
    c = np.asarray(inputs["c"], dtype=np.float32)
    pos = np.asarray(inputs["positions"]).astype(np.int32)
    maps = []
    for b in cores:
        m = dict(shared)
        m["xT"] = np.ascontiguousarray(x[b].T)
        m["condc"] = np.ascontiguousarray(c[b].reshape(16, 128).T)
        m["pos"] = np.ascontiguousarray(pos[b])
        maps.append(m)
    return maps


def kernel(**inputs):
    nc = build()
    in_maps = make_in_maps(inputs, list(range(8)))
    res = run_bass_kernel_spmd(nc, in_maps, core_ids=list(range(8)))
    out = np.stack([np.ascontiguousarray(np.asarray(r["y"]).T) for r in res.results], axis=0)
    return out.astype(np.float32)
```
